# Optimizing a Trainium2 kernel written in Bass

```python
import math
import jax, jax.numpy as jnp
from jax import lax
import numpy as np

D_MODEL = 2048
BATCH = 4
SEQ = 2048
DEPTH = 1

MEM_TOKENS = 256
ROPE_THETA = 10000.0
Q_BLOCK = 128
EPS = 1e-6
MLA_HEADS = 8
MLA_Q_RANK = 512
MLA_KV_RANK = 512
MLA_NOPE_DIM = 128
MLA_ROPE_DIM = 64
MLA_V_DIM = 128
DIFF_HEADS = 8
DIFF_HEAD_DIM = 64
DIFF_V_DIM = 2 * DIFF_HEAD_DIM
N_BRANCHES = 2
XATTN_HEADS = 4
XATTN_HEAD_DIM = 128
FFN_DIM = 5632
CONV_WIDTH = 3
IN_DIM = (MLA_Q_RANK + MLA_KV_RANK + MLA_ROPE_DIM + 2 * DIFF_HEADS * 2 * DIFF_HEAD_DIM
          + DIFF_HEADS * DIFF_V_DIM + N_BRANCHES * D_MODEL)

kernel_name = 'hybrid_mla_diffattn_convffn_encoder'


def _in_split_points():
    sizes = (MLA_Q_RANK, MLA_KV_RANK, MLA_ROPE_DIM, DIFF_HEADS * 2 * DIFF_HEAD_DIM,
             DIFF_HEADS * 2 * DIFF_HEAD_DIM, DIFF_HEADS * DIFF_V_DIM, N_BRANCHES * D_MODEL)
    points, acc = [], 0
    for s in sizes[:-1]:
        acc += s
        points.append(acc)
    return points


def rms_norm(x, g):
    xf = x.astype(jnp.float32)
    y = xf * lax.rsqrt(jnp.mean(xf * xf, axis=-1, keepdims=True) + EPS)
    return (y * g.astype(jnp.float32)).astype(x.dtype)


def rope_tables(positions, dim, dtype):
    inv = ROPE_THETA ** (-jnp.arange(0, dim, 2, dtype=jnp.float32) / dim)
    ang = positions.astype(jnp.float32)[..., None] * inv
    return jnp.cos(ang)[:, :, None, :].astype(dtype), jnp.sin(ang)[:, :, None, :].astype(dtype)


def apply_rope(x, cos, sin):
    x1, x2 = jnp.split(x, 2, axis=-1)
    return jnp.concatenate([x1 * cos - x2 * sin, x2 * cos + x1 * sin], axis=-1)


def query_blocks(q):
    b, s = q.shape[:2]
    q = q.reshape((b, s // Q_BLOCK, Q_BLOCK) + q.shape[2:])
    return jnp.moveaxis(q, 1, 0)


def merge_blocks(o):
    o = jnp.moveaxis(o, 0, 1)
    return o.reshape((o.shape[0], o.shape[1] * o.shape[2]) + o.shape[3:])


def mla_attention(q_nope, q_pe, k_nope, k_pe, v):
    scale = (MLA_NOPE_DIM + MLA_ROPE_DIM) ** -0.5

    def block(qs):
        qn, qp = qs
        s = (jnp.einsum('bqhd,bkhd->bhqk', qn, k_nope)
             + jnp.einsum('bqhd,bkd->bhqk', qp, k_pe))
        p = jax.nn.softmax(s.astype(jnp.float32) * scale, axis=-1).astype(v.dtype)
        return jnp.einsum('bhqk,bkhd->bqhd', p, v)

    return merge_blocks(lax.map(block, (query_blocks(q_nope), query_blocks(q_pe))))


def diff_attention(q, k, v, lam):
    scale = DIFF_HEAD_DIM ** -0.5

    def block(qb):
        s = jnp.einsum('bqhcd,bkhcd->bhcqk', qb, k)
        p = jax.nn.softmax(s.astype(jnp.float32) * scale, axis=-1)
        a = (p[:, :, 0] - lam * p[:, :, 1]).astype(v.dtype)
        return jnp.einsum('bhqk,bkhd->bqhd', a, v)

    return merge_blocks(lax.map(block, query_blocks(q)))


def hybrid_mixer(h, cos_m, sin_m, cos_d, sin_d, lam_init, w_in, g_q_norm, w_uq, g_kv_norm, w_ukv,
                 w_o_mla, lambda_q1, lambda_k1, lambda_q2, lambda_k2, g_diff_sub, w_o_diff, w_out):
    b, s = h.shape[:2]
    z = h @ w_in
    c_q, c_kv, k_pe, dq, dk, dv, gate_logits = jnp.split(z, _in_split_points(), axis=-1)

    q = (rms_norm(c_q, g_q_norm) @ w_uq).reshape(b, s, MLA_HEADS, MLA_NOPE_DIM + MLA_ROPE_DIM)
    q_nope = q[..., :MLA_NOPE_DIM]
    q_pe = apply_rope(q[..., MLA_NOPE_DIM:], cos_m, sin_m)
    kv = (rms_norm(c_kv, g_kv_norm) @ w_ukv).reshape(b, s, MLA_HEADS, MLA_NOPE_DIM + MLA_V_DIM)
    k_nope, v_m = kv[..., :MLA_NOPE_DIM], kv[..., MLA_NOPE_DIM:]
    k_pe = apply_rope(k_pe[:, :, None, :], cos_m, sin_m)[:, :, 0]
    y_a = mla_attention(q_nope, q_pe, k_nope, k_pe, v_m).reshape(b, s, MLA_HEADS * MLA_V_DIM) @ w_o_mla

    dq = apply_rope(dq.reshape(b, s, DIFF_HEADS * 2, DIFF_HEAD_DIM), cos_d, sin_d)
    dk = apply_rope(dk.reshape(b, s, DIFF_HEADS * 2, DIFF_HEAD_DIM), cos_d, sin_d)
    dq = dq.reshape(b, s, DIFF_HEADS, 2, DIFF_HEAD_DIM)
    dk = dk.reshape(b, s, DIFF_HEADS, 2, DIFF_HEAD_DIM)
    dv = dv.reshape(b, s, DIFF_HEADS, DIFF_V_DIM)
    lam = (jnp.exp(jnp.sum(lambda_q1.astype(jnp.float32) * lambda_k1.astype(jnp.float32)))
           - jnp.exp(jnp.sum(lambda_q2.astype(jnp.float32) * lambda_k2.astype(jnp.float32)))
           + lam_init)
    o = diff_attention(dq, dk, dv, lam)
    o = rms_norm(o, g_diff_sub) * (1.0 - lam_init)
    y_b = o.reshape(b, s, DIFF_HEADS * DIFF_V_DIM) @ w_o_diff

    g_a, g_b = jnp.split(jax.nn.sigmoid(gate_logits), N_BRANCHES, axis=-1)
    return (g_a * y_a + g_b * y_b) @ w_out


def memory_cross_attention(h, mem_n, w_q, w_kv, w_o):
    b, s = h.shape[:2]
    m = mem_n.shape[1]
    q = (h @ w_q).reshape(b, s, XATTN_HEADS, XATTN_HEAD_DIM)
    kv = (mem_n @ w_kv).reshape(b, m, XATTN_HEADS, 2 * XATTN_HEAD_DIM)
    k, v = kv[..., :XATTN_HEAD_DIM], kv[..., XATTN_HEAD_DIM:]
    sc = jnp.einsum('bqhd,bkhd->bhqk', q, k).astype(jnp.float32) * (XATTN_HEAD_DIM ** -0.5)
    p = jax.nn.softmax(sc, axis=-1).astype(v.dtype)
    o = jnp.einsum('bhqk,bkhd->bqhd', p, v).reshape(b, s, XATTN_HEADS * XATTN_HEAD_DIM)
    return o @ w_o


def depthwise_conv(u, w, bias):
    y = lax.conv_general_dilated(
        u, w[:, None, :].astype(u.dtype), window_strides=(1,),
        padding=((CONV_WIDTH // 2, CONV_WIDTH // 2),),
        dimension_numbers=('NWC', 'WIO', 'NWC'), feature_group_count=u.shape[-1])
    return y + bias.astype(u.dtype)


def conv_ffn(h, w_up, conv_w, conv_b, w_down):
    u = depthwise_conv(h @ w_up, conv_w, conv_b)
    gate, val = jnp.split(u, 2, axis=-1)
    return (jax.nn.silu(gate) * val) @ w_down


def setup_inputs(seed: int = 0) -> dict:
    key = jax.random.key(seed)
    ks = jax.random.split(key, 32)
    f32 = jnp.float32

    def dense(k, fan_in, fan_out):
        return jax.random.normal(k, (DEPTH, fan_in, fan_out), f32) * fan_in ** -0.5

    def gain(k, dim):
        return 1.0 + 0.02 * jax.random.normal(k, (DEPTH, dim), f32)

    x = jax.random.normal(ks[0], (BATCH, SEQ, D_MODEL), f32)
    mem = jax.random.normal(ks[1], (BATCH, MEM_TOKENS, D_MODEL), f32)
    positions = (jnp.arange(SEQ, dtype=jnp.int32)[None, :]
                 + jax.random.randint(ks[2], (BATCH, 1), 0, SEQ, dtype=jnp.int32))
    return {
        'x': x,
        'mem': mem,
        'positions': positions,
        'g_mix_norm': gain(ks[3], D_MODEL),
        'w_in': dense(ks[4], D_MODEL, IN_DIM),
        'g_q_norm': gain(ks[5], MLA_Q_RANK),
        'w_uq': dense(ks[6], MLA_Q_RANK, MLA_HEADS * (MLA_NOPE_DIM + MLA_ROPE_DIM)),
        'g_kv_norm': gain(ks[7], MLA_KV_RANK),
        'w_ukv': dense(ks[8], MLA_KV_RANK, MLA_HEADS * (MLA_NOPE_DIM + MLA_V_DIM)),
        'w_o_mla': dense(ks[9], MLA_HEADS * MLA_V_DIM, D_MODEL),
        'lambda_q1': 0.1 * jax.random.normal(ks[10], (DEPTH, DIFF_HEAD_DIM), f32),
        'lambda_k1': 0.1 * jax.random.normal(ks[11], (DEPTH, DIFF_HEAD_DIM), f32),
        'lambda_q2': 0.1 * jax.random.normal(ks[12], (DEPTH, DIFF_HEAD_DIM), f32),
        'lambda_k2': 0.1 * jax.random.normal(ks[13], (DEPTH, DIFF_HEAD_DIM), f32),
        'g_diff_sub': gain(ks[14], DIFF_V_DIM),
        'w_o_diff': dense(ks[15], DIFF_HEADS * DIFF_V_DIM, D_MODEL),
        'w_out': dense(ks[16], D_MODEL, D_MODEL),
        'g_cross_norm': gain(ks[17], D_MODEL),
        'g_mem_norm': gain(ks[18], D_MODEL),
        'w_cross_q': dense(ks[19], D_MODEL, XATTN_HEADS * XATTN_HEAD_DIM),
        'w_cross_kv': dense(ks[20], D_MODEL, 2 * XATTN_HEADS * XATTN_HEAD_DIM),
        'w_cross_o': dense(ks[21], XATTN_HEADS * XATTN_HEAD_DIM, D_MODEL),
        'g_ffn_norm': gain(ks[22], D_MODEL),
        'w_up': dense(ks[23], D_MODEL, 2 * FFN_DIM),
        'conv_w': jax.random.normal(ks[24], (DEPTH, CONV_WIDTH, 2 * FFN_DIM), f32) * CONV_WIDTH ** -0.5,
        'conv_b': 0.01 * jax.random.normal(ks[25], (DEPTH, 2 * FFN_DIM), f32),
        'w_down': dense(ks[26], FFN_DIM, D_MODEL),
        'g_final': 1.0 + 0.02 * jax.random.normal(ks[27], (D_MODEL,), f32),
    }


def reference(x, mem, positions, g_mix_norm, w_in, g_q_norm, w_uq, g_kv_norm, w_ukv, w_o_mla,
              lambda_q1, lambda_k1, lambda_q2, lambda_k2, g_diff_sub, w_o_diff, w_out,
              g_cross_norm, g_mem_norm, w_cross_q, w_cross_kv, w_cross_o,
              g_ffn_norm, w_up, conv_w, conv_b, w_down, g_final):
    cos_m, sin_m = rope_tables(positions, MLA_ROPE_DIM, x.dtype)
    cos_d, sin_d = rope_tables(positions, DIFF_HEAD_DIM, x.dtype)
    for l in range(DEPTH):
        lam_init = 0.8 - 0.6 * math.exp(-0.3 * l)
        x = x + hybrid_mixer(
            rms_norm(x, g_mix_norm[l]), cos_m, sin_m, cos_d, sin_d, lam_init,
            w_in[l], g_q_norm[l], w_uq[l], g_kv_norm[l], w_ukv[l], w_o_mla[l],
            lambda_q1[l], lambda_k1[l], lambda_q2[l], lambda_k2[l], g_diff_sub[l], w_o_diff[l], w_out[l])
        x = x + memory_cross_attention(
            rms_norm(x, g_cross_norm[l]), rms_norm(mem, g_mem_norm[l]),
            w_cross_q[l], w_cross_kv[l], w_cross_o[l])
        x = x + conv_ffn(rms_norm(x, g_ffn_norm[l]), w_up[l], conv_w[l], conv_b[l], w_down[l])
    return rms_norm(x, g_final)
```

```python
import math
import os
from contextlib import ExitStack

import numpy as np
import concourse.bass as bass
import concourse.mybir as mybir
from concourse.bass_utils import run_bass_kernel_spmd

F32 = mybir.dt.float32
BF16 = mybir.dt.bfloat16
I32 = mybir.dt.int32
U8 = mybir.dt.uint8
AF = mybir.ActivationFunctionType
ALU = mybir.AluOpType
AX = mybir.AxisListType

ENG_NAMES = ["tensor", "vector", "scalar", "gpsimd", "sync"]
SAME_ENGINE_SYNC = True
_DBG_EMIT = False

D = 2048
KC = 16
T = 2048
Q = 1025
QG = [(0, 342), (342, 342), (684, 341)]
KG = [(i * 512, 512) for i in range(4)]
FFN = 5632
EPS = 1e-6
C_CQ, C_CKV, C_KPE, C_DQ, C_DK, C_DV, C_GA, C_GB = 0, 512, 1024, 1088, 2112, 3136, 4160, 6208
CS_GMIX, CS_GCROSS, CS_GMEM, CS_GFFN, CS_GFIN = 0, 16, 32, 48, 64
CS_GQ, CS_GKV, CS_GDIFF, CS_INV, CS_SIGN = 80, 84, 88, 89, 90
CS_CW, CS_CB = 91, 91 + 264
CS_EPS1, CS_EPS2 = 91 + 264 + 88, 91 + 264 + 89
NCST = 91 + 264 + 90
LAM_INIT = 0.8 - 0.6 * math.exp(-0.3 * 0)


def _bufname(k):
    return k if isinstance(k, str) else k[0]


class Op:
    __slots__ = ("id", "eng", "fn", "deps", "is_dma", "dma_sem", "dma_val", "sig", "cnt")


class Prog:
    def __init__(self):
        self.ops = []
        self.by_eng = {e: [] for e in ENG_NAMES}
        self.lastw = {}
        self.readers = {}
        self.dma_cnt = {}
        self.dma_last = {}
        self.out_dmas = []
        self.pending = {}
        self.bufkeys = {}

    def add(self, eng, fn, reads=(), writes=(), dma_sem=None, is_output=False, extra_deps=()):
        op = Op()
        op.id = len(self.ops)
        op.eng = eng
        op.fn = fn
        deps = set(extra_deps)
        if eng != "tensor":
            writes = list(writes) + [r for r in reads if isinstance(r, tuple) and r[0] == "ps" and r not in writes]
        for r in reads:
            w = self.lastw.get(r)
            if w is not None:
                deps.add(w)
            pd = self.pending.get(_bufname(r))
            if pd:
                deps |= pd
        for k in writes:
            w = self.lastw.get(k)
            if w is not None:
                deps.add(w)
            for rd in self.readers.get(k, ()):
                deps.add(rd)
            pd = self.pending.get(_bufname(k))
            if pd:
                deps |= pd
        op.is_dma = dma_sem is not None
        op.dma_sem = dma_sem
        op.dma_val = 0
        if op.is_dma:
            n = self.dma_cnt.get(dma_sem, 0) + 1
            self.dma_cnt[dma_sem] = n
            op.dma_val = 16 * n
            prev = self.dma_last.get(dma_sem)
            if prev is not None:
                deps.add(prev)
            self.dma_last[dma_sem] = op.id
            if is_output:
                self.out_dmas.append(op.id)
        deps.discard(op.id)
        op.deps = deps
        op.sig = False
        op.cnt = 0
        for k in writes:
            self.lastw[k] = op.id
            self.readers[k] = []
            self.bufkeys.setdefault(_bufname(k), set()).add(k)
        for r in reads:
            if r not in writes:
                self.readers.setdefault(r, []).append(op.id)
            self.bufkeys.setdefault(_bufname(r), set()).add(r)
        self.ops.append(op)
        self.by_eng[eng].append(op)
        return op

    def touch_ops(self, name):
        s = set()
        for k in self.bufkeys.get(name, ()):
            w = self.lastw.pop(k, None)
            if w is not None:
                s.add(w)
            for r in self.readers.pop(k, ()):
                s.add(r)
        s |= self.pending.pop(name, set())
        self.bufkeys.pop(name, None)
        return s

    def finish(self, eng="sync"):
        self.add(eng, None, extra_deps=list(self.out_dmas))

    def emit(self, nc):
        needed = set()
        for op in self.ops:
            agg = {}
            for d in op.deps:
                p = self.ops[d]
                if p.is_dma:
                    key = ("d", p.dma_sem)
                else:
                    if p.fn is None:
                        continue
                    if p.eng == op.eng and p.eng == "tensor" and not op.is_dma:
                        continue
                    if p.eng == op.eng and not op.is_dma and not SAME_ENGINE_SYNC:
                        continue
                    key = ("e", p.eng)
                if key not in agg or agg[key] < d:
                    agg[key] = d
            op.deps = set(agg.values())
            for d in op.deps:
                if not self.ops[d].is_dma:
                    needed.add(d)
        for e in ENG_NAMES:
            c = 0
            for op in self.by_eng[e]:
                if op.is_dma or op.fn is None:
                    continue
                if op.id in needed:
                    c += 1
                    op.cnt = c
                    op.sig = True
        with ExitStack() as st:
            esem = {e: st.enter_context(nc.semaphore("es_" + e)) for e in ENG_NAMES}
            dsem = {}
            for i, k in enumerate(self.dma_cnt):
                dsem[k] = st.enter_context(nc.semaphore("ds_%d" % i))
            block = st.enter_context(nc.Block())
            prog = self

            def run(engh, e):
                waited = {}
                for op in prog.by_eng[e]:
                    for d in sorted(op.deps):
                        p = prog.ops[d]
                        if p.is_dma:
                            key, val, sem = ("d", p.dma_sem), p.dma_val, dsem[p.dma_sem]
                        else:
                            key, val, sem = ("e", p.eng), p.cnt, esem[p.eng]
                        if waited.get(key, 0) >= val:
                            continue
                        engh.wait_ge(sem, val)
                        waited[key] = val
                        if _DBG_EMIT:
                            print("  [%s] op%d wait %s >= %d" % (e, op.id, key, val))
                    if _DBG_EMIT:
                        print("[%s] op%d %s sig=%s cnt=%d dma=%s" % (e, op.id, "none" if op.fn is None else "", op.sig, op.cnt, op.dma_sem if op.is_dma else ""))
                    if op.fn is None:
                        continue
                    ins = op.fn(engh)
                    if op.is_dma:
                        ins.then_inc(dsem[op.dma_sem], 16)
                    elif op.sig:
                        ins.then_inc(esem[e], 1)

            @block.tensor
            def _(eng):
                run(eng, "tensor")

            @block.vector
            def _(eng):
                run(eng, "vector")

            @block.scalar
            def _(eng):
                run(eng, "scalar")

            @block.gpsimd
            def _(eng):
                run(eng, "gpsimd")

            @block.sync
            def _(eng):
                run(eng, "sync")


_DT_SIZE = {F32: 4, BF16: 2, I32: 4, U8: 1}


class Arena:
    def __init__(self, nc, P, base, size):
        self.nc, self.P = nc, P
        self.free = [(base, size)]
        self.live = {}
        self.retired = []
        self.n = 0

    def alloc(self, name, shape, dtype, top=False):
        nbytes = _DT_SIZE[dtype]
        for s in shape[1:]:
            nbytes *= s
        nbytes = (nbytes + 63) // 64 * 64
        order = range(len(self.free) - 1, -1, -1) if top else range(len(self.free))
        for i in order:
            (o, s) = self.free[i]
            if s >= nbytes:
                if s == nbytes:
                    self.free.pop(i)
                elif top:
                    self.free[i] = (o, s - nbytes)
                    o = o + s - nbytes
                else:
                    self.free[i] = (o + nbytes, s - nbytes)
                break
        else:
            raise RuntimeError("SBUF arena full allocating %s (%d B); live=%s free=%s" % (
                name, nbytes, {k: v[1] for k, v in self.live.items()}, self.free))
        self.live[name] = (o, nbytes)
        deps = set()
        keep = []
        for (ro, rs, ops) in self.retired:
            if ro < o + nbytes and o < ro + rs:
                deps |= ops
            keep.append((ro, rs, ops))
        self.retired = keep
        if deps:
            self.P.pending[name] = deps
        self.n += 1
        return self.nc.alloc_sbuf_tensor_at("%s_%d" % (name, self.n), list(shape), dtype, offset=o)

    def release(self, name):
        o, s = self.live.pop(name)
        ops = self.P.touch_ops(name)
        self.retired.append((o, s, ops))
        fl = sorted(self.free + [(o, s)])
        merged = []
        for (a, b) in fl:
            if merged and merged[-1][0] + merged[-1][1] == a:
                merged[-1] = (merged[-1][0], merged[-1][1] + b)
            else:
                merged.append((a, b))
        self.free = merged


class _Stop(Exception):
    pass


def build_nc(stop_after=None):
    nc = bass.Bass("TRN2", target_bir_lowering=False)
    P = Prog()
    try:
        _build_body(nc, P, stop_after)
    except _Stop:
        pass
    return nc


def _build_body(nc, P, stop_after):
    def checkpoint(k, items):
        if stop_after != k:
            return
        last = [ops[-1].id for e, ops in P.by_eng.items() if ops]
        for (name, t) in items:
            d = nc.dram_tensor("dbg_" + name, list(t.shape), t.dtype, kind="ExternalOutput").ap()
            idx = tuple(slice(None) for _ in t.shape)
            P.add("sync", lambda e, d=d, t=t, idx=idx: e.dma_start(out=d[idx], in_=t[idx]),
                  dma_sem=("dbg", name), is_output=True, extra_deps=last)
        P.finish("sync")
        P.emit(nc)
        raise _Stop()


    def din(name, shape, dt=F32):
        return nc.dram_tensor(name, list(shape), dt, kind="ExternalInput").ap()

    xkvT = din("xkvT", [D, T])
    xqT = din("xqT", [D, Q])
    memT = din("memT", [D, 256])
    poskv_d = din("poskv", [128, T], I32)
    posq_d = din("posq", [128, Q], I32)
    cst_d = din("cst", [128, NCST])
    lamv_d = din("lamv", [128, 256])
    cmat_d = din("cmat", [128, 256])
    w_in = din("w_in", [D, 8256])
    w_uq = din("w_uq", [512, 1536])
    w_ukv = din("w_ukv", [512, 2048])
    w_o_mla = din("w_o_mla", [1024, D])
    w_o_diff = din("w_o_diff", [1024, D])
    w_out = din("w_out", [D, D])
    w_cq = din("w_cross_q", [D, 512])
    w_ckv = din("w_cross_kv", [D, 1024])
    w_co = din("w_cross_o", [512, D])
    w_up = din("w_up", [D, 2 * FFN])
    w_down = din("w_down", [FFN, D])
    outT = nc.dram_tensor("outT", [D, Q], F32, kind="ExternalOutput").ap()
    dbg_list = []

    base = (nc.sbuf_base + 63) // 64 * 64
    asize = (nc.sbuf_top - base) // 64 * 64
    nc.alloc_sbuf_tensor("arena", [128, asize], U8)
    A = Arena(nc, P, base, asize)
    ps = nc.alloc_psum_tensor("ps", [128, 8, 512], F32)

    def PSK(b):
        return ("ps", b)

    def act(out, in_, func, reads, writes, bias=None, scale=None):
        kw = {}
        if bias is not None:
            kw["bias"] = bias
        if scale is not None:
            kw["scale"] = scale
        P.add("scalar", lambda e: e.activation(out=out, in_=in_, func=func, **kw), reads=reads, writes=writes)

    def tt(out, in0, in1, op, reads, writes):
        P.add("vector", lambda e: e.tensor_tensor(out=out, in0=in0, in1=in1, op=op), reads=reads, writes=writes)

    def ts(out, in0, s1, s2, op0, op1, reads, writes):
        if op1 is None:
            P.add("vector", lambda e: e.tensor_scalar(out=out, in0=in0, scalar1=s1, scalar2=None, op0=op0),
                  reads=reads, writes=writes)
        else:
            P.add("vector", lambda e: e.tensor_scalar(out=out, in0=in0, scalar1=s1, scalar2=s2, op0=op0, op1=op1),
                  reads=reads, writes=writes)

    def stt(out, in0, scalar, in1, op0, op1, reads, writes):
        P.add("vector", lambda e: e.scalar_tensor_tensor(out=out, in0=in0, scalar=scalar, in1=in1, op0=op0, op1=op1),
              reads=reads, writes=writes)

    def vcopy(out, in_, reads, writes):
        P.add("vector", lambda e: e.tensor_copy(out=out, in_=in_), reads=reads, writes=writes)

    def recip(out, in_, reads, writes):
        P.add("vector", lambda e: e.reciprocal(out=out, in_=in_), reads=reads, writes=writes)

    def mm(out, lhsT, rhs, start, stop, reads, bank):
        P.add("tensor", lambda e: e.matmul(out, lhsT=lhsT, rhs=rhs, start=start, stop=stop),
              reads=reads, writes=[PSK(bank)])

    def dma(eng, out, in_, reads, writes, sem, is_output=False):
        P.add(eng, lambda e: e.dma_start(out=out, in_=in_), reads=reads, writes=writes, dma_sem=sem,
              is_output=is_output)

    cst = A.alloc("cst", [128, NCST], F32)
    cmat = A.alloc("cmat", [128, 256], BF16)
    ones = A.alloc("ones", [128, 128], BF16)
    lamc = A.alloc("lamc", [128, 4], F32)
    dma("sync", cst[:], cst_d[:, :], [], ["cst"], "cst")
    dma("gpsimd", cmat[:], cmat_d[:, :], [], ["cmat"], "cmat")
    P.add("vector", lambda e: e.memset(ones[:], 1.0), writes=["ones"])
    Rm = cmat[:, 0:128]

    def ccol(c0, n=1):
        return cst[:, c0:c0 + n]

    NSLOT = 3
    SLOT_ELEMS = 4096
    wstate = {"i": 0, "wsl": None, "ns": NSLOT}

    def wload(src, nk, ncols):
        assert nk * ncols <= SLOT_ELEMS, (nk, ncols)
        s = wstate["i"] % wstate["ns"]
        wstate["i"] += 1
        wsl = wstate["wsl"]
        view = wsl[:, s, 0:nk * ncols].rearrange("p (k n) -> p k n", k=nk)
        dma("gpsimd", view, src.rearrange("(k p) n -> p k n", p=128), [], [("wsl", s)], ("wsl", s))
        return ("wsl", s), view

    TWO_PI = 2.0 * math.pi
    HI = 6.28125
    LO = TWO_PI - HI
    PI_S = 3.141592

    def rope_tables(pos_d, N, cosn, sinn):
        cos_t = A.alloc(cosn, [128, N], F32)
        sin_t = A.alloc(sinn, [128, N], F32)
        pi_ = A.alloc("rt_pi", [128, N], I32)
        r = A.alloc("rt_r", [128, N], F32)
        m = A.alloc("rt_m", [128, N], F32)
        dma("sync", pi_[:], pos_d[:, :], [], ["rt_pi"], "rt_pi")
        vcopy(r[:], pi_[:], ["rt_pi"], ["rt_r"])
        ts(r[:], r[:], ccol(CS_INV), None, ALU.mult, None, ["rt_r", "cst"], ["rt_r"])
        ts(pi_[:], r[:], 1.0 / TWO_PI, None, ALU.mult, None, ["rt_r"], ["rt_pi"])
        vcopy(m[:], pi_[:], ["rt_pi"], ["rt_m"])
        stt(r[:], m[:], -HI, r[:], ALU.mult, ALU.add, ["rt_m", "rt_r"], ["rt_r"])
        stt(r[:], m[:], -LO, r[:], ALU.mult, ALU.add, ["rt_m", "rt_r"], ["rt_r"])

        def wrap(v, vn):
            ts(m[:], v[:], math.pi, -TWO_PI, ALU.is_gt, ALU.mult, [vn], ["rt_m"])
            tt(v[:], v[:], m[:], ALU.add, [vn, "rt_m"], [vn])
            ts(m[:], v[:], -math.pi, TWO_PI, ALU.is_lt, ALU.mult, [vn], ["rt_m"])
            tt(v[:], v[:], m[:], ALU.add, [vn, "rt_m"], [vn])
            ts(v[:], v[:], -PI_S, PI_S, ALU.max, ALU.min, [vn], [vn])

        wrap(r, "rt_r")
        act(sin_t[:], r[:], AF.Sin, ["rt_r", "cst"], [sinn], scale=ccol(CS_SIGN))
        ts(r[:], r[:], math.pi / 2, None, ALU.add, None, ["rt_r"], ["rt_r"])
        wrap(r, "rt_r")
        act(cos_t[:], r[:], AF.Sin, ["rt_r"], [cosn])
        for n_ in ("rt_pi", "rt_r", "rt_m"):
            A.release(n_)
        return cos_t, sin_t

    def norm_T(src, srck, dst, dstk, nk, groups, gcol0, dn, sqn, bank_list, extra=1.0, per_group_keys=True):
        N = groups[-1][0] + groups[-1][1]
        rstd = A.alloc(sqn + "_rstd", [128, N], F32)
        sq = A.alloc(sqn + "_sq", [128, nk, 512], BF16)
        for gi, (c0, n) in enumerate(groups):
            bank = bank_list[gi % len(bank_list)]
            for kc in range(nk):
                act(sq[:, kc, 0:n], src(kc, c0, n), AF.Square, [srck(kc, gi)], [(sqn + "_sq", kc)])
            for kc in range(nk):
                mm(ps[:, bank, 0:n], ones[:, :], sq[:, kc, 0:n], kc == 0, kc == nk - 1,
                   ["ones", (sqn + "_sq", kc)], bank)
            act(rstd[:, c0:c0 + n], ps[:, bank, 0:n], AF.Sqrt, [PSK(bank), "cst"], [(sqn + "_rstd", gi)],
                bias=ccol(CS_EPS1 if extra == 1.0 else CS_EPS2), scale=float(1.0 / (dn * extra * extra)))
            recip(rstd[:, c0:c0 + n], rstd[:, c0:c0 + n], [(sqn + "_rstd", gi)], [(sqn + "_rstd", gi)])
            for kc in range(nk):
                stt(dst(kc, c0, n), src(kc, c0, n), ccol(gcol0 + kc), rstd[:, c0:c0 + n], ALU.mult, ALU.mult,
                    [srck(kc, gi), "cst", (sqn + "_rstd", gi)], [dstk(kc, gi)])
        A.release(sqn + "_rstd")
        A.release(sqn + "_sq")

    rstate = {"i": 0, "t": None}

    def rope(bank, bank2, np_, n, cos_ap, sin_ap, tabkeys, out, outk):
        b = rstate["i"] % int(os.environ.get("RB", "2"))
        rstate["i"] += 1
        ropet = rstate["t"]
        qa = ropet[:, b, 512:768].bitcast(BF16)[0:np_, 0:n]
        t_ = ropet[0:np_, b, 0:n]
        act(qa, ps[0:np_, bank, 0:n], AF.Copy, [PSK(bank)], [("ropet", b, 0)])
        mm(ps[0:np_, bank2, 0:n], cmat[0:np_, 0:np_], qa, True, True, ["cmat", ("ropet", b, 0)], bank2)
        tt(t_, ps[0:np_, bank, 0:n], cos_ap, ALU.mult, [PSK(bank), ("ropet", b, 0)] + tabkeys, [("ropet", b, 1)])
        u_ = ropet[0:np_, b, 768:768 + n]
        tt(u_, ps[0:np_, bank2, 0:n], sin_ap, ALU.mult, [PSK(bank2)] + tabkeys, [("ropet", b, 2)])
        if isinstance(out, list):
            for (psl, o_ap) in out:
                tt(o_ap, ropet[psl, b, 768:768 + n], ropet[psl, b, 0:n], ALU.add,
                   [("ropet", b, 1), ("ropet", b, 2)], [outk])
        else:
            tt(out, u_, t_, ALU.add, [("ropet", b, 1), ("ropet", b, 2)], [outk])

    checkpoint(-1, [("cst", cst), ("cmat", cmat), ("ones", ones)])
    bankc = {"i": 0}

    def nbank(lst):
        b = lst[bankc["i"] % len(lst)]
        bankc["i"] += 1
        return b

    hkv = A.alloc("hkv", [128, KC, T], BF16)
    xblk = A.alloc("xblk", [128, 2, KC, 512], F32)
    ckv32 = A.alloc("ckv32", [128, 4, T], F32)
    wstate["wsl"] = A.alloc("wsl", [128, NSLOT, SLOT_ELEMS], BF16, top=True)
    ckvw = [wload(w_in[:, C_CKV + blk2 * 256:C_CKV + (blk2 + 1) * 256], KC, 256) for blk2 in range(2)]
    for blk in range(4):
        xb = blk % 2
        dma("sync", xblk[:, xb], xkvT[:, blk * 512:(blk + 1) * 512].rearrange("(k p) t -> p k t", p=128),
            [], [("xblk", xb)], ("xblk", xb))
        norm_T(lambda kc, c0, n, xb=xb: xblk[:, xb, kc, c0:c0 + n], lambda kc, gi, xb=xb: ("xblk", xb),
               lambda kc, c0, n, blk=blk: hkv[:, kc, blk * 512 + c0:blk * 512 + c0 + n],
               lambda kc, gi, blk=blk: ("hkv", blk, kc),
               KC, [(0, 512)], CS_GMIX, D, "n1", [blk % 2])
        g, (c0, n) = blk, KG[blk]
        for cc in range(4):
            wk, wv = ckvw[cc // 2]
            cl = cc % 2
            bank = nbank([2, 3, 4, 5])
            for kc in range(KC):
                mm(ps[:, bank, 0:n], wv[:, kc, cl * 128:(cl + 1) * 128], hkv[:, kc, c0:c0 + n],
                   kc == 0, kc == KC - 1, [wk, ("hkv", g, kc)], bank)
            act(ckv32[:, cc, c0:c0 + n], ps[:, bank, 0:n], AF.Copy, [PSK(bank)], [("ckv32", cc, g)])
    checkpoint(1, [("hkv", hkv)])
    A.release("xblk")

    ckvn = A.alloc("ckvn", [128, 4, T], BF16)
    kpe = A.alloc("kpe", [128, T], BF16)
    P.add("vector", lambda e: e.memset(kpe[64:128, :], 0.0), writes=[("kpe", "z")])
    norm_T(lambda kc, c0, n: ckv32[:, kc, c0:c0 + n], lambda kc, gi: ("ckv32", kc, gi),
           lambda kc, c0, n: ckvn[:, kc, c0:c0 + n], lambda kc, gi: ("ckvn", kc, gi),
           4, KG, CS_GKV, 512, "n2", [0, 1])
    checkpoint(21, [("ckvn", ckvn)])
    A.release("ckv32")

    dv = A.alloc("dv", [128, 16, 1024], BF16)
    for cb in range(4):
        wk, wv = wload(w_in[:, C_DV + cb * 256:C_DV + (cb + 1) * 256], KC, 256)
        for tp in range(8):
            bank = nbank([2, 3, 4, 5])
            for sub in range(2):
                tc = tp * 2 + sub
                for kc in range(KC):
                    mm(ps[:, bank, sub * 256:(sub + 1) * 256], hkv[:, kc, tc * 128:(tc + 1) * 128], wv[:, kc, :],
                       kc == 0, kc == KC - 1, [wk, ("hkv", tc // 4, kc)], bank)
            act(dv[:, tp * 2:tp * 2 + 2, cb * 256:(cb + 1) * 256],
                ps[:, bank, :].rearrange("p (a b) -> p a b", a=2), AF.Copy, [PSK(bank)], [("dv", tp, cb)])
    cos_kv, sin_kv = rope_tables(poskv_d, T, "cos_kv", "sin_kv")
    lamv = A.alloc("lamv", [128, 256], F32)
    lamt = A.alloc("lamt", [128, 128], F32)
    dma("sync", lamv[:], lamv_d[:, :], [], ["lamv"], "lamv")
    tt(lamt[:, 0:64], lamv[:, 0:64], lamv[:, 64:128], ALU.mult, ["lamv"], [("lamt", 0)])
    tt(lamt[:, 64:128], lamv[:, 128:192], lamv[:, 192:256], ALU.mult, ["lamv"], [("lamt", 1)])
    P.add("vector", lambda e: e.reduce_sum(out=lamc[:, 2:3], in_=lamt[:, 0:64], axis=AX.X),
          reads=[("lamt", 0)], writes=[("lamc", 2)])
    P.add("vector", lambda e: e.reduce_sum(out=lamc[:, 3:4], in_=lamt[:, 64:128], axis=AX.X),
          reads=[("lamt", 1)], writes=[("lamc", 3)])
    act(lamc[:, 2:4], lamc[:, 2:4], AF.Exp, [("lamc", 2), ("lamc", 3)], [("lamc", 2), ("lamc", 3)])
    tt(lamc[:, 0:1], lamc[:, 2:3], lamc[:, 3:4], ALU.subtract, [("lamc", 2), ("lamc", 3)], [("lamc", 0)])
    ts(lamc[:, 0:1], lamc[:, 0:1], float(LAM_INIT), None, ALU.add, None, [("lamc", 0)], [("lamc", 0)])
    ts(lamc[:, 1:2], lamc[:, 0:1], -1.0, None, ALU.mult, None, [("lamc", 0)], [("lamc", 1)])
    A.release("lamv")
    A.release("lamt")
    rstate["t"] = A.alloc("ropet", [128, 2, 1280], F32)
    dk = A.alloc("dk", [128, 8, T], BF16)

    wk, wv = wload(w_in[:, C_KPE:C_KPE + 64], KC, 64)
    for g in [int(c) for c in os.environ.get("KPEG", "0123")]:
        c0, n = KG[g]
        bank = nbank([int(c) for c in os.environ.get("KB", "2345")])
        KCX = int(os.environ.get("KCX", "16"))
        for kc in range(KCX):
            mm(ps[0:64, bank, 0:n], wv[:, kc, 0:64], hkv[:, kc, c0:c0 + n], kc == 0, kc == KCX - 1,
               [wk, ("hkv", g, kc)], bank)
        rope(bank, int(os.environ["B2X"][rstate["i"] % len(os.environ["B2X"])]) if os.environ.get("B2X") else 6 + g % 2, 64, n,
             cos_kv[0:64, c0:c0 + n], sin_kv[0:64, c0:c0 + n], ["cos_kv", "sin_kv"],
             kpe[0:64, c0:c0 + n], ("kpe", g))

    checkpoint(22, [("ckvn", ckvn), ("kpe", kpe)])
    for blk2 in range(4):
        wk, wv = wload(w_in[:, C_DK + blk2 * 256:C_DK + (blk2 + 1) * 256], KC, 256)
        for cl in range(2):
            h = blk2 * 2 + cl
            for g, (c0, n) in enumerate(KG):
                bank = nbank([2, 3, 4, 5])
                for kc in range(KC):
                    mm(ps[:, bank, 0:n], wv[:, kc, cl * 128:(cl + 1) * 128], hkv[:, kc, c0:c0 + n],
                       kc == 0, kc == KC - 1, [wk, ("hkv", g, kc)], bank)
                rope(bank, 6 + g % 2, 128, n, cos_kv[:, c0:c0 + n], sin_kv[:, c0:c0 + n], ["cos_kv", "sin_kv"],
                     dk[:, h, c0:c0 + n], ("dk", h, g))

    checkpoint(2, [("ckvn", ckvn), ("kpe", kpe), ("dk", dk), ("dv", dv)])
    A.release("hkv")
    A.release("cos_kv")
    A.release("sin_kv")

    cos_q, sin_q = rope_tables(posq_d, Q, "cos_q", "sin_q")
    hq = A.alloc("hq", [128, KC, Q], BF16)
    xqb = A.alloc("xqb", [128, KC, 342], F32)
    for gi, (c0, n) in enumerate(QG):
        dma("sync", xqb[:, :, 0:n], xqT[:, c0:c0 + n].rearrange("(k p) t -> p k t", p=128), [], ["xqb"], "xqb")
        norm_T(lambda kc, c0_, n_: xqb[:, kc, 0:n_], lambda kc, gi_: "xqb",
               lambda kc, c0_, n_, c0=c0: hq[:, kc, c0:c0 + n_], lambda kc, gi_, gi=gi: ("hq", gi, kc),
               KC, [(0, n)], CS_GMIX, D, "n3", [gi % 2])
    checkpoint(3, [("hq", hq)])
    A.release("xqb")

    def hq_keys(g):
        return [("hq", g, kc) for kc in range(KC)]

    O_BANK, SUM_BANK = 6, 7
    LOOK = 2
    ast = {"i": 0, "c": 0}
    deferred = []

    def attn_alloc():
        return (A.alloc("pT", [128, 3, 2, 342], BF16), A.alloc("osb", [128, 2, 342], F32),
                A.alloc("ssb", [128, 2, 342], F32))

    def attn_core(bufs, nkc, n, qk_list, v_of, scale, after):
        pT, osb, ssb = bufs
        npairs = nkc // 2
        pend = []

        def s_stage(p):
            sp = p % 3
            for j in range(2):
                kc = 2 * p + j
                sb = 2 * sp + j
                for i, (lo, rhs, ko) in enumerate(qk_list):
                    mm(ps[:, sb, 0:n], lo(kc), rhs[0], i == 0, i == len(qk_list) - 1, ko(kc) + rhs[1], sb)
            act(pT[:, sp, :, 0:n], ps[:, 2 * sp:2 * sp + 2, 0:n], AF.Exp, [PSK(2 * sp), PSK(2 * sp + 1)],
                [("pT", sp)], scale=float(scale))
            return sp

        def pv_stage(p, sp):
            for j in range(2):
                kc = 2 * p + j
                vl, vk = v_of(kc)
                mm(ps[:, O_BANK, 0:n], vl, pT[:, sp, j, 0:n], kc == 0, kc == nkc - 1, vk + [("pT", sp)], O_BANK)
                mm(ps[:, SUM_BANK, 0:n], ones[:, :], pT[:, sp, j, 0:n], kc == 0, kc == nkc - 1,
                   ["ones", ("pT", sp)], SUM_BANK)

        for p in range(npairs):
            pend.append((p, s_stage(p)))
            if p == min(4, npairs - 1):
                while deferred:
                    deferred.pop(0)()
            if len(pend) > LOOK:
                pv_stage(*pend.pop(0))
        while pend:
            pv_stage(*pend.pop(0))
        cb = ast["c"] % 2
        ast["c"] += 1
        vcopy(osb[:, cb, 0:n], ps[:, O_BANK, 0:n], [PSK(O_BANK)], [("osb", cb)])
        vcopy(ssb[:, cb, 0:n], ps[:, SUM_BANK, 0:n], [PSK(SUM_BANK)], [("ssb", cb)])
        recip(ssb[:, cb, 0:n], ssb[:, cb, 0:n], [("ssb", cb)], [("ssb", cb)])
        after(osb[:, cb, 0:n], ("osb", cb), ssb[:, cb, 0:n], ("ssb", cb))

    def attn_stream(bufs, calls):
        pT, osb, ssb = bufs
        jobs = [(ci, p) for ci, c in enumerate(calls) for p in range(c["nkc"] // 2)]

        def s_stage(ji):
            ci, p = jobs[ji]
            c = calls[ci]
            n = c["n"]
            if p == 0 and c.get("pre"):
                c["pre"]()
            sp = ji % 3
            for j in range(2):
                kc = 2 * p + j
                sb = 2 * sp + j
                for i, (lo, rhs, ko) in enumerate(c["qk"]):
                    mm(ps[:, sb, 0:n], lo(kc), rhs[0], i == 0, i == len(c["qk"]) - 1, ko(kc) + rhs[1], sb)
            act(pT[:, sp, :, 0:n], ps[:, 2 * sp:2 * sp + 2, 0:n], AF.Exp, [PSK(2 * sp), PSK(2 * sp + 1)],
                [("pT", sp)], scale=float(c["scale"]))

        def pv_stage(ji):
            ci, p = jobs[ji]
            c = calls[ci]
            n, nkc = c["n"], c["nkc"]
            sp = ji % 3
            for j in range(2):
                kc = 2 * p + j
                vl, vk = c["v_of"](kc)
                mm(ps[:, O_BANK, 0:n], vl, pT[:, sp, j, 0:n], kc == 0, kc == nkc - 1, vk + [("pT", sp)], O_BANK)
                mm(ps[:, SUM_BANK, 0:n], ones[:, :], pT[:, sp, j, 0:n], kc == 0, kc == nkc - 1,
                   ["ones", ("pT", sp)], SUM_BANK)
            if p == nkc // 2 - 1:
                cb = ast["c"] % 2
                ast["c"] += 1
                vcopy(osb[:, cb, 0:n], ps[:, O_BANK, 0:n], [PSK(O_BANK)], [("osb", cb)])
                vcopy(ssb[:, cb, 0:n], ps[:, SUM_BANK, 0:n], [PSK(SUM_BANK)], [("ssb", cb)])
                recip(ssb[:, cb, 0:n], ssb[:, cb, 0:n], [("ssb", cb)], [("ssb", cb)])
                c["after"](osb[:, cb, 0:n], ("osb", cb), ssb[:, cb, 0:n], ("ssb", cb))

        pend = []
        for ji in range(len(jobs)):
            pend.append(ji)
            s_stage(ji)
            if len(pend) > LOOK:
                jx = pend.pop(0)
                pv_stage(jx)
                if jobs[ji][1] == 4:
                    while deferred:
                        deferred.pop(0)(2 * (jx % 3))
        while pend:
            pv_stage(pend.pop(0))

    dq = A.alloc("dq", [128, 8, Q], BF16)
    for blk2 in range(4):
        wk, wv = wload(w_in[:, C_DQ + blk2 * 256:C_DQ + (blk2 + 1) * 256], KC, 256)
        for cl in range(2):
            h = blk2 * 2 + cl
            for g, (c0, n) in enumerate(QG):
                bank = nbank([5, 6])
                for kc in range(KC):
                    mm(ps[:, bank, 0:n], wv[:, kc, cl * 128:(cl + 1) * 128], hq[:, kc, c0:c0 + n],
                       kc == 0, kc == KC - 1, [wk, ("hq", g, kc)], bank)
                rope(bank, 7, 128, n, cos_q[:, c0:c0 + n], sin_q[:, c0:c0 + n], ["cos_q", "sin_q"],
                     dq[:, h, c0:c0 + n], ("dq", h, g))

    A.release("cos_q")
    A.release("sin_q")
    A.release("ropet")
    A.release("wsl")
    ob = A.alloc("ob", [128, 8, Q], BF16)
    dtmp = A.alloc("dtmp", [128, 2, 3, 342], F32)
    dsq = A.alloc("dsq", [128, 2, 342], BF16)
    abufs = attn_alloc()
    dqm = A.alloc("dqm", [128, 2, 2, Q], BF16)
    for hb_ in range(2):
        P.add("vector", lambda e, hb_=hb_: e.memset(dqm[64:128, hb_, 0, :], 0.0), writes=[("dqm", hb_, "z0")])
        P.add("vector", lambda e, hb_=hb_: e.memset(dqm[0:64, hb_, 1, :], 0.0), writes=[("dqm", hb_, "z1")])
    dpar = {"i": 0}
    dcalls = []
    for h in range(8):
        hb_ = h % 2

        def pre(h=h, hb_=hb_):
            vcopy(dqm[0:64, hb_, 0, :], dq[0:64, h, :], [("dq", h, g_) for g_ in range(3)], [("dqm", hb_, 0)])
            vcopy(dqm[64:128, hb_, 1, :], dq[64:128, h, :], [("dq", h, g_) for g_ in range(3)], [("dqm", hb_, 1)])

        for g, (c0, n) in enumerate(QG):
            pp = dpar["i"] % 2
            dpar["i"] += 1
            for c in range(2):
                def after(O, ok, rs, rsk, c=c, h=h, g=g, c0=c0, n=n, pp=pp):
                    if c == 0:
                        tt(dtmp[:, pp, 0, 0:n], O, rs, ALU.mult, [ok, rsk], [("dtmp", pp, 0)])
                    else:
                        tt(dtmp[:, pp, 1, 0:n], O, rs, ALU.mult, [ok, rsk], [("dtmp", pp, 1)])
                        stt(dtmp[:, pp, 1, 0:n], dtmp[:, pp, 1, 0:n], lamc[:, 1:2], dtmp[:, pp, 0, 0:n], ALU.mult,
                            ALU.add, [("dtmp", pp, 0), ("dtmp", pp, 1), ("lamc", 1)], [("dtmp", pp, 1)])
                        tt(dsq[:, pp, 0:n], dtmp[:, pp, 1, 0:n], dtmp[:, pp, 1, 0:n], ALU.mult, [("dtmp", pp, 1)],
                           [("dsq", pp)])

                        def fin(bank, h=h, g=g, c0=c0, n=n, pp=pp):
                            mm(ps[:, bank, 0:n], ones[:, :], dsq[:, pp, 0:n], True, True, ["ones", ("dsq", pp)], bank)
                            ex = 1.0 - LAM_INIT
                            act(dtmp[:, pp, 2, 0:n], ps[:, bank, 0:n], AF.Ln, [PSK(bank), "cst"], [("dtmp", pp, 2)],
                                bias=ccol(CS_EPS2), scale=float(1.0 / (128.0 * ex * ex)))
                            act(dtmp[:, pp, 2, 0:n], dtmp[:, pp, 2, 0:n], AF.Exp, [("dtmp", pp, 2)], [("dtmp", pp, 2)],
                                scale=-0.5)
                            stt(ob[:, h, c0:c0 + n], dtmp[:, pp, 1, 0:n], ccol(CS_GDIFF), dtmp[:, pp, 2, 0:n],
                                ALU.mult, ALU.mult, [("dtmp", pp, 1), ("dtmp", pp, 2), "cst"], [("ob", h, g)])

                        deferred.append(fin)

                dcalls.append(dict(
                    nkc=16, n=n, scale=64.0 ** -0.5, after=after, pre=(pre if (g == 0 and c == 0) else None),
                    qk=[(lambda kc, h=h: dk[:, h, kc * 128:(kc + 1) * 128],
                         (dqm[:, hb_, c, c0:c0 + n], [("dqm", hb_, c), ("dqm", hb_, "z0"), ("dqm", hb_, "z1")]),
                         lambda kc, h=h: [("dk", h, kc // 4)])],
                    v_of=lambda kc, h=h: (dv[:, kc, h * 128:(h + 1) * 128], [("dv", kc // 2, h // 2)])))
    attn_stream(abufs, dcalls)
    while deferred:
        deferred.pop(0)(0)
    checkpoint(4, [("ob", ob)])
    for n_ in ("dq", "dqm", "dk", "dv", "dtmp", "dsq", "pT", "osb", "ssb"):
        A.release(n_)
    wstate["wsl"] = A.alloc("wsl", [128, NSLOT, SLOT_ELEMS], BF16)
    rstate["t"] = A.alloc("ropet", [128, 2, 1280], F32)

    cos_q, sin_q = rope_tables(posq_d, Q, "cos_q", "sin_q")
    cq32 = A.alloc("cq32", [128, 4, Q], F32)
    for blk2 in range(2):
        wk, wv = wload(w_in[:, C_CQ + blk2 * 256:C_CQ + (blk2 + 1) * 256], KC, 256)
        for cl in range(2):
            cc = blk2 * 2 + cl
            for g, (c0, n) in enumerate(QG):
                bank = nbank([0, 1, 2, 3, 4, 5])
                for kc in range(KC):
                    mm(ps[:, bank, 0:n], wv[:, kc, cl * 128:(cl + 1) * 128], hq[:, kc, c0:c0 + n],
                       kc == 0, kc == KC - 1, [wk, ("hq", g, kc)], bank)
                act(cq32[:, cc, c0:c0 + n], ps[:, bank, 0:n], AF.Copy, [PSK(bank)], [("cq32", cc, g)])
    cqn = A.alloc("cqn", [128, 4, Q], BF16)
    norm_T(lambda kc, c0, n: cq32[:, kc, c0:c0 + n], lambda kc, gi: ("cq32", kc, gi),
           lambda kc, c0, n: cqn[:, kc, c0:c0 + n], lambda kc, gi: ("cqn", kc, gi),
           4, QG, CS_GQ, 512, "n4", [5, 6])
    A.release("cq32")

    oa = A.alloc("oa", [128, 8, Q], BF16)
    qn = A.alloc("qn", [128, 2, Q], BF16)
    qp = A.alloc("qp", [128, 2, Q], BF16)
    P.add("vector", lambda e: e.memset(qp[64:128, :, :], 0.0), writes=[("qp", "z")])
    kn = A.alloc("kn", [128, 2, T], BF16)
    vm = A.alloc("vm", [128, 2, 16, 128], BF16)

    abufs = attn_alloc()

    def mla_prep(h):
        hb = h % 2
        wk, wv = wload(w_uq[:, h * 192:(h + 1) * 192], 4, 192)
        for g, (c0, n) in enumerate(QG):
            bank = nbank([0, 1, 2, 3, 4])
            for kc in range(4):
                mm(ps[:, bank, 0:n], wv[:, kc, 0:128], cqn[:, kc, c0:c0 + n], kc == 0, kc == 3,
                   [wk, ("cqn", kc, g)], bank)
            act(qn[:, hb, c0:c0 + n], ps[:, bank, 0:n], AF.Copy, [PSK(bank)], [("qn", hb, g)])
            bank = nbank([0, 1, 2, 3, 4])
            for kc in range(4):
                mm(ps[0:64, bank, 0:n], wv[:, kc, 128:192], cqn[:, kc, c0:c0 + n], kc == 0, kc == 3,
                   [wk, ("cqn", kc, g)], bank)
            rope(bank, 5, 64, n, cos_q[0:64, c0:c0 + n], sin_q[0:64, c0:c0 + n], ["cos_q", "sin_q"],
                 qp[0:64, hb, c0:c0 + n], ("qp", hb, g))
        wk, wv = wload(w_ukv[:, h * 256:(h + 1) * 256], 4, 256)
        for g, (c0, n) in enumerate(KG):
            bank = nbank([0, 1, 2, 3, 4, 5])
            for kc in range(4):
                mm(ps[:, bank, 0:n], wv[:, kc, 0:128], ckvn[:, kc, c0:c0 + n], kc == 0, kc == 3,
                   [wk, ("ckvn", kc, g)], bank)
            act(kn[:, hb, c0:c0 + n], ps[:, bank, 0:n], AF.Copy, [PSK(bank)], [("kn", hb, g)])
        for tq in range(4):
            bank = nbank([0, 1, 2, 3, 4, 5])
            for sub in range(4):
                tc = tq * 4 + sub
                for kc in range(4):
                    mm(ps[:, bank, sub * 128:(sub + 1) * 128], ckvn[:, kc, tc * 128:(tc + 1) * 128],
                       wv[:, kc, 128:256], kc == 0, kc == 3, [wk, ("ckvn", kc, tq)], bank)
            act(vm[:, hb, tq * 4:(tq + 1) * 4, :], ps[:, bank, :].rearrange("p (a b) -> p a b", a=4), AF.Copy,
                [PSK(bank)], [("vm", hb, tq)])

    mla_prep(0)
    for h in range(8):
        hb = h % 2
        if h + 1 < 8:
            mla_prep(h + 1)
        mcalls = []
        for g, (c0, n) in enumerate(QG):
            def after(O, ok, rs, rsk, h=h, g=g, c0=c0, n=n):
                tt(oa[:, h, c0:c0 + n], O, rs, ALU.mult, [ok, rsk], [("oa", h, g)])

            mcalls.append(dict(
                nkc=16, n=n, scale=192.0 ** -0.5, after=after,
                qk=[(lambda kc, hb=hb: kn[:, hb, kc * 128:(kc + 1) * 128],
                     (qn[:, hb, c0:c0 + n], [("qn", hb, g)]),
                     lambda kc, hb=hb: [("kn", hb, kc // 4)]),
                    (lambda kc: kpe[:, kc * 128:(kc + 1) * 128],
                     (qp[:, hb, c0:c0 + n], [("qp", hb, g), ("qp", "z")]),
                     lambda kc: [("kpe", kc // 4), ("kpe", "z")])],
                v_of=lambda kc, hb=hb: (vm[:, hb, kc, :], [("vm", hb, kc // 4)])))
        attn_stream(abufs, mcalls)
    checkpoint(5, [("oa", oa), ("cqn", cqn)])
    for n_ in ("cqn", "qn", "qp", "kn", "vm", "ckvn", "kpe", "cos_q", "sin_q", "pT", "osb", "ssb"):
        A.release(n_)

    A.release("wsl")
    wstate["ns"] = 4
    wstate["wsl"] = A.alloc("wsl", [128, 4, SLOT_ELEMS], BF16)
    mt = A.alloc("mt", [128, KC, Q], BF16)
    sg = A.alloc("sg", [128, 2, 342], F32)
    mtmp = A.alloc("mtmp", [128, 2, 342], F32)
    t1 = A.alloc("t1", [128, 2, Q], F32)
    ALLB = [0, 1, 2, 3, 4, 5, 6, 7]
    ust = {"i": 0}
    for jb in range(8):
        for br in range(2):
            wo_d, gcol, src_t, srcn = ((w_o_mla, C_GA, oa, "oa"), (w_o_diff, C_GB, ob, "ob"))[br]
            wko, wvo = wload(wo_d[:, jb * 256:(jb + 1) * 256], 8, 256)
            wkg, wvg = wload(w_in[:, gcol + jb * 256:gcol + (jb + 1) * 256], KC, 256)
            for jl in range(2):
                j = jb * 2 + jl
                cs = slice(jl * 128, (jl + 1) * 128)
                for g, (c0, n) in enumerate(QG):
                    sb_ = ust["i"] % 2
                    ust["i"] += 1
                    bg = nbank(ALLB)
                    for kc in range(KC):
                        mm(ps[:, bg, 0:n], wvg[:, kc, cs], hq[:, kc, c0:c0 + n], kc == 0, kc == KC - 1,
                           [wkg, ("hq", g, kc)], bg)
                    act(sg[:, sb_, 0:n], ps[:, bg, 0:n], AF.Sigmoid, [PSK(bg)], [("sg", sb_)])
                    by = nbank(ALLB)
                    for kc in range(8):
                        mm(ps[:, by, 0:n], wvo[:, kc, cs], src_t[:, kc, c0:c0 + n], kc == 0, kc == 7,
                           [wko, (srcn, kc, g)], by)
                    if br == 0:
                        tt(t1[:, jl, c0:c0 + n], ps[:, by, 0:n], sg[:, sb_, 0:n], ALU.mult,
                           [PSK(by), ("sg", sb_)], [("t1", jl, g)])
                    else:
                        tt(mtmp[:, sb_, 0:n], ps[:, by, 0:n], sg[:, sb_, 0:n], ALU.mult,
                           [PSK(by), ("sg", sb_)], [("mtmp", sb_)])
                        tt(mt[:, j, c0:c0 + n], mtmp[:, sb_, 0:n], t1[:, jl, c0:c0 + n], ALU.add,
                           [("mtmp", sb_), ("t1", jl, g)], [("mt", j, g)])
    for n_ in ("hq", "oa", "ob", "sg", "mtmp", "t1", "ropet"):
        A.release(n_)

    xres = A.alloc("xres", [128, KC, Q], F32)
    for kq in range(4):
        dma("sync", xres[:, kq * 4:(kq + 1) * 4, :],
            xqT[kq * 512:(kq + 1) * 512, :].rearrange("(k p) t -> p k t", p=128), [],
            [("xres", kc, g) for kc in range(kq * 4, kq * 4 + 4) for g in range(3)], ("xres", kq))

    def proj_add(wsrc_of_block, nk, act_t, actkeys, nblocks=8):
        for jb in range(nblocks):
            wk, wv = wsrc_of_block(jb)
            for jl in range(2):
                j = jb * 2 + jl
                for g, (c0, n) in enumerate(QG):
                    bank = nbank([0, 1, 2, 3, 4, 5, 6, 7])
                    for kc in range(nk):
                        mm(ps[:, bank, 0:n], wv[:, kc, jl * 128:(jl + 1) * 128], act_t[:, kc, c0:c0 + n],
                           kc == 0, kc == nk - 1, [wk] + actkeys(kc, g), bank)
                    tt(xres[:, j, c0:c0 + n], ps[:, bank, 0:n], xres[:, j, c0:c0 + n], ALU.add,
                       [PSK(bank), ("xres", j, g)], [("xres", j, g)])

    checkpoint(6, [("mt", mt)])
    proj_add(lambda jb: wload(w_out[:, jb * 256:(jb + 1) * 256], KC, 256), KC, mt,
             lambda kc, g: [("mt", kc, g)])
    checkpoint(7, [("xres", xres)])
    A.release("mt")

    h2 = A.alloc("h2", [128, KC, Q], BF16)
    norm_T(lambda kc, c0, n: xres[:, kc, c0:c0 + n], lambda kc, gi: ("xres", kc, gi),
           lambda kc, c0, n: h2[:, kc, c0:c0 + n], lambda kc, gi: ("h2", kc, gi),
           KC, QG, CS_GCROSS, D, "n5", [0, 1])
    mem32 = A.alloc("mem32", [128, KC, 256], F32)
    memn = A.alloc("memn", [128, KC, 256], BF16)
    dma("sync", mem32[:], memT[:, :].rearrange("(k p) t -> p k t", p=128), [], ["mem32"], "mem32")
    norm_T(lambda kc, c0, n: mem32[:, kc, c0:c0 + n], lambda kc, gi: "mem32",
           lambda kc, c0, n: memn[:, kc, c0:c0 + n], lambda kc, gi: ("memn", kc),
           KC, [(0, 256)], CS_GMEM, D, "n6", [2])
    A.release("mem32")
    qx = A.alloc("qx", [128, 4, Q], BF16)
    kx = A.alloc("kx", [128, 4, 256], BF16)
    vx = A.alloc("vx", [128, 2, 512], BF16)
    oc = A.alloc("oc", [128, 4, Q], BF16)
    abufs = attn_alloc()
    for hb2 in range(2):
        wk, wv = wload(w_cq[:, hb2 * 256:(hb2 + 1) * 256], KC, 256)
        for hl in range(2):
            h = hb2 * 2 + hl
            for g, (c0, n) in enumerate(QG):
                bank = nbank([0, 1, 2, 3, 4, 5])
                for kc in range(KC):
                    mm(ps[:, bank, 0:n], wv[:, kc, hl * 128:(hl + 1) * 128], h2[:, kc, c0:c0 + n],
                       kc == 0, kc == KC - 1, [wk, ("h2", kc, g)], bank)
                act(qx[:, h, c0:c0 + n], ps[:, bank, 0:n], AF.Copy, [PSK(bank)], [("qx", h, g)])
    for h in range(4):
        wk, wv = wload(w_ckv[:, h * 256:(h + 1) * 256], KC, 256)
        bank = nbank([0, 1, 2, 3, 4, 5])
        for kc in range(KC):
            mm(ps[:, bank, 0:256], wv[:, kc, 0:128], memn[:, kc, :], kc == 0, kc == KC - 1,
               [wk, ("memn", kc)], bank)
        act(kx[:, h, :], ps[:, bank, 0:256], AF.Copy, [PSK(bank)], [("kx", h)])
        bank = nbank([0, 1, 2, 3, 4, 5])
        for tc in range(2):
            for kc in range(KC):
                mm(ps[:, bank, tc * 128:(tc + 1) * 128], memn[:, kc, tc * 128:(tc + 1) * 128], wv[:, kc, 128:256],
                   kc == 0, kc == KC - 1, [wk, ("memn", kc)], bank)
        act(vx[:, :, h * 128:(h + 1) * 128], ps[:, bank, 0:256].rearrange("p (a b) -> p a b", a=2), AF.Copy,
            [PSK(bank)], [("vx", h)])
    pTx, osbx, ssbx = abufs
    xcalls = [(h, g, c0, n) for h in range(4) for g, (c0, n) in enumerate(QG)]
    xscale = 128.0 ** -0.5

    def xs_stage(i):
        h, g, c0, n = xcalls[i]
        sp = i % 3
        for j in range(2):
            mm(ps[:, 2 * sp + j, 0:n], kx[:, h, j * 128:(j + 1) * 128], qx[:, h, c0:c0 + n], True, True,
               [("kx", h), ("qx", h, g)], 2 * sp + j)
        act(pTx[:, sp, :, 0:n], ps[:, 2 * sp:2 * sp + 2, 0:n], AF.Exp, [PSK(2 * sp), PSK(2 * sp + 1)],
            [("pT", sp)], scale=float(xscale))

    def xpv_stage(i):
        h, g, c0, n = xcalls[i]
        sp = i % 3
        cb = i % 2
        for j in range(2):
            mm(ps[:, O_BANK, 0:n], vx[:, j, h * 128:(h + 1) * 128], pTx[:, sp, j, 0:n], j == 0, j == 1,
               [("vx", h), ("pT", sp)], O_BANK)
            mm(ps[:, SUM_BANK, 0:n], ones[:, :], pTx[:, sp, j, 0:n], j == 0, j == 1, ["ones", ("pT", sp)], SUM_BANK)
        vcopy(osbx[:, cb, 0:n], ps[:, O_BANK, 0:n], [PSK(O_BANK)], [("osb", cb)])
        vcopy(ssbx[:, cb, 0:n], ps[:, SUM_BANK, 0:n], [PSK(SUM_BANK)], [("ssb", cb)])
        recip(ssbx[:, cb, 0:n], ssbx[:, cb, 0:n], [("ssb", cb)], [("ssb", cb)])
        tt(oc[:, h, c0:c0 + n], osbx[:, cb, 0:n], ssbx[:, cb, 0:n], ALU.mult, [("osb", cb), ("ssb", cb)],
           [("oc", h, g)])

    xpend = []
    for i in range(len(xcalls)):
        xpend.append(i)
        xs_stage(i)
        if len(xpend) > LOOK:
            xpv_stage(xpend.pop(0))
    while xpend:
        xpv_stage(xpend.pop(0))
    proj_add(lambda jb: wload(w_co[:, jb * 256:(jb + 1) * 256], 4, 256), 4, oc,
             lambda kc, g: [("oc", kc, g)])
    checkpoint(8, [("xres", xres), ("oc", oc)])
    for n_ in ("h2", "memn", "qx", "kx", "vx", "oc", "pT", "osb", "ssb"):
        A.release(n_)

    h3 = A.alloc("h3", [128, KC, Q], BF16)
    norm_T(lambda kc, c0, n: xres[:, kc, c0:c0 + n], lambda kc, gi: ("xres", kc, gi),
           lambda kc, c0, n: h3[:, kc, c0:c0 + n], lambda kc, gi: ("h3", kc, gi),
           KC, QG, CS_GFFN, D, "n7", [0, 1])
    NQ = 4
    CPQ = 11
    aT = A.alloc("aT", [128, CPQ, Q], BF16)
    ubuf = A.alloc("ubuf", [128, 2, 2, Q + 2], F32)
    cbuf = A.alloc("cbuf", [128, 2, 2, Q], F32)
    for pb in range(2):
        for s_ in range(2):
            P.add("vector", lambda e, pb=pb, s_=s_: e.memset(ubuf[:, pb, s_, 0:1], 0.0), writes=[("ubuf", pb, s_, "l")])
            P.add("vector", lambda e, pb=pb, s_=s_: e.memset(ubuf[:, pb, s_, Q + 1:Q + 2], 0.0),
                  writes=[("ubuf", pb, s_, "r")])
    for qd in range(NQ):
        for cl in range(CPQ):
            jj = qd * CPQ + cl
            pb = jj % 2
            wkg, wvg = wload(w_up[:, jj * 128:(jj + 1) * 128], KC, 128)
            wkv_, wvv = wload(w_up[:, FFN + jj * 128:FFN + (jj + 1) * 128], KC, 128)
            for s_, (wk, wv) in enumerate(((wkg, wvg), (wkv_, wvv))):
                for g, (c0, n) in enumerate(QG):
                    bank = nbank([0, 1, 2, 3, 4, 5, 6, 7])
                    for kc in range(KC):
                        mm(ps[:, bank, 0:n], wv[:, kc, :], h3[:, kc, c0:c0 + n], kc == 0, kc == KC - 1,
                           [wk, ("h3", kc, g)], bank)
                    act(ubuf[:, pb, s_, 1 + c0:1 + c0 + n], ps[:, bank, 0:n], AF.Copy, [PSK(bank)],
                        [("ubuf", pb, s_, g)])
                ukeys = [("ubuf", pb, s_, g) for g in range(3)] + [("ubuf", pb, s_, "l"), ("ubuf", pb, s_, "r")]
                col = s_ * 44 + jj
                ts(cbuf[:, pb, s_, :], ubuf[:, pb, s_, 1:Q + 1], ccol(CS_CW + 1 * 88 + col), ccol(CS_CB + col),
                   ALU.mult, ALU.add, ukeys + ["cst"], [("cbuf", pb, s_)])
                stt(cbuf[:, pb, s_, :], ubuf[:, pb, s_, 0:Q], ccol(CS_CW + 0 * 88 + col), cbuf[:, pb, s_, :],
                    ALU.mult, ALU.add, ukeys + ["cst", ("cbuf", pb, s_)], [("cbuf", pb, s_)])
                stt(cbuf[:, pb, s_, :], ubuf[:, pb, s_, 2:Q + 2], ccol(CS_CW + 2 * 88 + col), cbuf[:, pb, s_, :],
                    ALU.mult, ALU.add, ukeys + ["cst", ("cbuf", pb, s_)], [("cbuf", pb, s_)])
            act(cbuf[:, pb, 0, :], cbuf[:, pb, 0, :], AF.Silu, [("cbuf", pb, 0)], [("cbuf", pb, 0)])
            tt(aT[:, cl, :], cbuf[:, pb, 0, :], cbuf[:, pb, 1, :], ALU.mult, [("cbuf", pb, 0), ("cbuf", pb, 1)],
               [("aT", cl)])
        proj_add(lambda jb, qd=qd: wload(w_down[qd * CPQ * 128:(qd + 1) * CPQ * 128, jb * 256:(jb + 1) * 256],
                                         CPQ, 256),
                 CPQ, aT, lambda kc, g: [("aT", kc)])
    checkpoint(9, [("xres", xres)])
    for n_ in ("h3", "aT", "ubuf", "cbuf"):
        A.release(n_)

    norm_T(lambda kc, c0, n: xres[:, kc, c0:c0 + n], lambda kc, gi: ("xres", kc, gi),
           lambda kc, c0, n: xres[:, kc, c0:c0 + n], lambda kc, gi: ("xres", kc, gi),
           KC, QG, CS_GFIN, D, "n8", [0, 1])
    for kq in range(4):
        dma("sync", outT[kq * 512:(kq + 1) * 512, :].rearrange("(k p) t -> p k t", p=128),
            xres[:, kq * 4:(kq + 1) * 4, :],
            [("xres", kc, g) for kc in range(kq * 4, kq * 4 + 4) for g in range(3)], [], ("out", kq),
            is_output=True)
    P.finish("sync")
    P.emit(nc)


_NC_CACHE = {}


def _host_inputs(inp):
    x = np.asarray(inp["x"], dtype=np.float32)
    mem = np.asarray(inp["mem"], dtype=np.float32)
    pos = np.asarray(inp["positions"], dtype=np.int32)

    def col(v, n):
        return np.asarray(v, np.float32).reshape(n, 128).T

    cst = np.zeros((128, NCST), np.float32)
    cst[:, CS_GMIX:CS_GMIX + 16] = col(inp["g_mix_norm"][0], 16)
    cst[:, CS_GCROSS:CS_GCROSS + 16] = col(inp["g_cross_norm"][0], 16)
    cst[:, CS_GMEM:CS_GMEM + 16] = col(inp["g_mem_norm"][0], 16)
    cst[:, CS_GFFN:CS_GFFN + 16] = col(inp["g_ffn_norm"][0], 16)
    cst[:, CS_GFIN:CS_GFIN + 16] = col(inp["g_final"], 16)
    cst[:, CS_GQ:CS_GQ + 4] = col(inp["g_q_norm"][0], 4)
    cst[:, CS_GKV:CS_GKV + 4] = col(inp["g_kv_norm"][0], 4)
    cst[:, CS_GDIFF] = np.asarray(inp["g_diff_sub"][0], np.float32)
    p = np.arange(128)
    inv = (10000.0 ** (-np.arange(0, 64, 2, dtype=np.float32) / np.float32(64))).astype(np.float32)
    cst[:, CS_INV] = inv[p % 32]
    cst[:, CS_SIGN] = np.where((p % 64) < 32, -1.0, 1.0)
    cw = np.asarray(inp["conv_w"][0], np.float32)
    for k in range(3):
        cst[:, CS_CW + k * 88:CS_CW + (k + 1) * 88] = col(cw[k], 88)
    cst[:, CS_CB:CS_CB + 88] = col(inp["conv_b"][0], 88)
    cst[:, CS_EPS1] = EPS
    cst[:, CS_EPS2] = EPS / ((1.0 - LAM_INIT) ** 2)
    lamv = np.concatenate([np.asarray(inp[k][0], np.float32) for k in
                           ("lambda_q1", "lambda_k1", "lambda_q2", "lambda_k2")])[None, :].repeat(128, 0)
    cmat = np.zeros((128, 256), np.float32)
    perm = np.where((p % 64) < 32, p + 32, p - 32)
    cmat[perm, p] = 1.0
    cmat[p, 128 + p] = 1.0
    shared = {
        "cst": cst, "lamv": np.ascontiguousarray(lamv), "cmat": cmat,
        "w_in": np.ascontiguousarray(inp["w_in"][0]), "w_uq": np.ascontiguousarray(inp["w_uq"][0]),
        "w_ukv": np.ascontiguousarray(inp["w_ukv"][0]), "w_o_mla": np.ascontiguousarray(inp["w_o_mla"][0]),
        "w_o_diff": np.ascontiguousarray(inp["w_o_diff"][0]), "w_out": np.ascontiguousarray(inp["w_out"][0]),
        "w_cross_q": np.ascontiguousarray(inp["w_cross_q"][0]),
        "w_cross_kv": np.ascontiguousarray(inp["w_cross_kv"][0]),
        "w_cross_o": np.ascontiguousarray(inp["w_cross_o"][0]), "w_up": np.ascontiguousarray(inp["w_up"][0]),
        "w_down": np.ascontiguousarray(inp["w_down"][0]),
    }
    shared = {k: np.asarray(v, np.float32) for k, v in shared.items()}
    in_maps = []
    for c in range(8):
        b, half = divmod(c, 2)
        q0 = half * 1023
        m = dict(shared)
        m["xkvT"] = np.ascontiguousarray(x[b].T)
        m["xqT"] = np.ascontiguousarray(x[b, q0:q0 + Q].T)
        m["memT"] = np.ascontiguousarray(mem[b].T)
        m["poskv"] = np.ascontiguousarray(pos[b][None, :].repeat(128, 0))
        m["posq"] = np.ascontiguousarray(pos[b, q0:q0 + Q][None, :].repeat(128, 0))
        in_maps.append(m)
    return in_maps


def kernel(**inp):
    if "nc" not in _NC_CACHE:
        _NC_CACHE["nc"] = build_nc()
    nc = _NC_CACHE["nc"]
    in_maps = _host_inputs(inp)
    res = run_bass_kernel_spmd(nc, in_maps, core_ids=list(range(8)))
    out = np.empty((4, 2048, D), np.float32)
    for c in range(8):
        b, half = divmod(c, 2)
        o = res.results[c]["outT"]
        if half == 0:
            out[b, 0:1024, :] = o[:, 0:1024].T
        else:
            out[b, 1024:2048, :] = o[:, 1:1025].T
    return out
```

```python
import math
import os
from contextlib import ExitStack

import numpy as np
import concourse.bass as bass
import concourse.mybir as mybir
from concourse.bass_utils import run_bass_kernel_spmd

F32 = mybir.dt.float32
BF16 = mybir.dt.bfloat16
I32 = mybir.dt.int32
U8 = mybir.dt.uint8
AF = mybir.ActivationFunctionType
ALU = mybir.AluOpType
AX = mybir.AxisListType

ENG_NAMES = ["tensor", "vector", "scalar", "gpsimd", "sync"]
SAME_ENGINE_SYNC = True
_DBG_EMIT = False

D = 2048
KC = 16
T = 2048
Q = 1025
QG = [(0, 342), (342, 342), (684, 341)]
KG = [(i * 512, 512) for i in range(4)]
FFN = 5632
EPS = 1e-6
C_CQ, C_CKV, C_KPE, C_DQ, C_DK, C_DV, C_GA, C_GB = 0, 512, 1024, 1088, 2112, 3136, 4160, 6208
CS_GMIX, CS_GCROSS, CS_GMEM, CS_GFFN, CS_GFIN = 0, 16, 32, 48, 64
CS_GQ, CS_GKV, CS_GDIFF, CS_INV, CS_SIGN = 80, 84, 88, 89, 90
CS_CW, CS_CB = 91, 91 + 264
CS_EPS1, CS_EPS2 = 91 + 264 + 88, 91 + 264 + 89
NCST = 91 + 264 + 90
LAM_INIT = 0.8 - 0.6 * math.exp(-0.3 * 0)


def _bufname(k):
    return k if isinstance(k, str) else k[0]


class Op:
    __slots__ = ("id", "eng", "fn", "deps", "is_dma", "dma_sem", "dma_val", "sig", "cnt")


class Prog:
    def __init__(self):
        self.ops = []
        self.by_eng = {e: [] for e in ENG_NAMES}
        self.lastw = {}
        self.readers = {}
        self.dma_cnt = {}
        self.dma_last = {}
        self.out_dmas = []
        self.pending = {}
        self.bufkeys = {}

    def add(self, eng, fn, reads=(), writes=(), dma_sem=None, is_output=False, extra_deps=()):
        op = Op()
        op.id = len(self.ops)
        op.eng = eng
        op.fn = fn
        deps = set(extra_deps)
        if eng != "tensor":
            writes = list(writes) + [r for r in reads if isinstance(r, tuple) and r[0] == "ps" and r not in writes]
        for r in reads:
            w = self.lastw.get(r)
            if w is not None:
                deps.add(w)
            pd = self.pending.get(_bufname(r))
            if pd:
                deps |= pd
        for k in writes:
            w = self.lastw.get(k)
            if w is not None:
                deps.add(w)
            for rd in self.readers.get(k, ()):
                deps.add(rd)
            pd = self.pending.get(_bufname(k))
            if pd:
                deps |= pd
        op.is_dma = dma_sem is not None
        op.dma_sem = dma_sem
        op.dma_val = 0
        if op.is_dma:
            n = self.dma_cnt.get(dma_sem, 0) + 1
            self.dma_cnt[dma_sem] = n
            op.dma_val = 16 * n
            prev = self.dma_last.get(dma_sem)
            if prev is not None:
                deps.add(prev)
            self.dma_last[dma_sem] = op.id
            if is_output:
                self.out_dmas.append(op.id)
        deps.discard(op.id)
        op.deps = deps
        op.sig = False
        op.cnt = 0
        for k in writes:
            self.lastw[k] = op.id
            self.readers[k] = []
            self.bufkeys.setdefault(_bufname(k), set()).add(k)
        for r in reads:
            if r not in writes:
                self.readers.setdefault(r, []).append(op.id)
            self.bufkeys.setdefault(_bufname(r), set()).add(r)
        self.ops.append(op)
        self.by_eng[eng].append(op)
        return op

    def touch_ops(self, name):
        s = set()
        for k in self.bufkeys.get(name, ()):
            w = self.lastw.pop(k, None)
            if w is not None:
                s.add(w)
            for r in self.readers.pop(k, ()):
                s.add(r)
        s |= self.pending.pop(name, set())
        self.bufkeys.pop(name, None)
        return s

    def finish(self, eng="sync"):
        self.add(eng, None, extra_deps=list(self.out_dmas))

    def emit(self, nc):
        needed = set()
        for op in self.ops:
            agg = {}
            for d in op.deps:
                p = self.ops[d]
                if p.is_dma:
                    key = ("d", p.dma_sem)
                else:
                    if p.fn is None:
                        continue
                    if p.eng == op.eng and p.eng == "tensor" and not op.is_dma:
                        continue
                    if p.eng == op.eng and not op.is_dma and not SAME_ENGINE_SYNC:
                        continue
                    key = ("e", p.eng)
                if key not in agg or agg[key] < d:
                    agg[key] = d
            op.deps = set(agg.values())
            for d in op.deps:
                if not self.ops[d].is_dma:
                    needed.add(d)
        for e in ENG_NAMES:
            c = 0
            for op in self.by_eng[e]:
                if op.is_dma or op.fn is None:
                    continue
                if op.id in needed:
                    c += 1
                    op.cnt = c
                    op.sig = True
        with ExitStack() as st:
            esem = {e: st.enter_context(nc.semaphore("es_" + e)) for e in ENG_NAMES}
            dsem = {}
            for i, k in enumerate(self.dma_cnt):
                dsem[k] = st.enter_context(nc.semaphore("ds_%d" % i))
            block = st.enter_context(nc.Block())
            prog = self

            def run(engh, e):
                waited = {}
                for op in prog.by_eng[e]:
                    for d in sorted(op.deps):
                        p = prog.ops[d]
                        if p.is_dma:
                            key, val, sem = ("d", p.dma_sem), p.dma_val, dsem[p.dma_sem]
                        else:
                            key, val, sem = ("e", p.eng), p.cnt, esem[p.eng]
                        if waited.get(key, 0) >= val:
                            continue
                        engh.wait_ge(sem, val)
                        waited[key] = val
                        if _DBG_EMIT:
                            print("  [%s] op%d wait %s >= %d" % (e, op.id, key, val))
                    if _DBG_EMIT:
                        print("[%s] op%d %s sig=%s cnt=%d dma=%s" % (e, op.id, "none" if op.fn is None else "", op.sig, op.cnt, op.dma_sem if op.is_dma else ""))
                    if op.fn is None:
                        continue
                    ins = op.fn(engh)
                    if op.is_dma:
                        ins.then_inc(dsem[op.dma_sem], 16)
                    elif op.sig:
                        ins.then_inc(esem[e], 1)

            @block.tensor
            def _(eng):
                run(eng, "tensor")

            @block.vector
            def _(eng):
                run(eng, "vector")

            @block.scalar
            def _(eng):
                run(eng, "scalar")

            @block.gpsimd
            def _(eng):
                run(eng, "gpsimd")

            @block.sync
            def _(eng):
                run(eng, "sync")


_DT_SIZE = {F32: 4, BF16: 2, I32: 4, U8: 1}


class Arena:
    def __init__(self, nc, P, base, size):
        self.nc, self.P = nc, P
        self.free = [(base, size)]
        self.live = {}
        self.retired = []
        self.n = 0

    def alloc(self, name, shape, dtype, top=False):
        nbytes = _DT_SIZE[dtype]
        for s in shape[1:]:
            nbytes *= s
        nbytes = (nbytes + 63) // 64 * 64
        order = range(len(self.free) - 1, -1, -1) if top else range(len(self.free))
        for i in order:
            (o, s) = self.free[i]
            if s >= nbytes:
                if s == nbytes:
                    self.free.pop(i)
                elif top:
                    self.free[i] = (o, s - nbytes)
                    o = o + s - nbytes
                else:
                    self.free[i] = (o + nbytes, s - nbytes)
                break
        else:
            raise RuntimeError("SBUF arena full allocating %s (%d B); live=%s free=%s" % (
                name, nbytes, {k: v[1] for k, v in self.live.items()}, self.free))
        self.live[name] = (o, nbytes)
        deps = set()
        keep = []
        for (ro, rs, ops) in self.retired:
            if ro < o + nbytes and o < ro + rs:
                deps |= ops
            keep.append((ro, rs, ops))
        self.retired = keep
        if deps:
            self.P.pending[name] = deps
        self.n += 1
        return self.nc.alloc_sbuf_tensor_at("%s_%d" % (name, self.n), list(shape), dtype, offset=o)

    def release(self, name):
        o, s = self.live.pop(name)
        ops = self.P.touch_ops(name)
        self.retired.append((o, s, ops))
        fl = sorted(self.free + [(o, s)])
        merged = []
        for (a, b) in fl:
            if merged and merged[-1][0] + merged[-1][1] == a:
                merged[-1] = (merged[-1][0], merged[-1][1] + b)
            else:
                merged.append((a, b))
        self.free = merged


class _Stop(Exception):
    pass


def build_nc(stop_after=None):
    nc = bass.Bass("TRN2", target_bir_lowering=False)
    P = Prog()
    try:
        _build_body(nc, P, stop_after)
    except _Stop:
        pass
    return nc


def _build_body(nc, P, stop_after):
    def checkpoint(k, items):
        if stop_after != k:
            return
        last = [ops[-1].id for e, ops in P.by_eng.items() if ops]
        for (name, t) in items:
            d = nc.dram_tensor("dbg_" + name, list(t.shape), t.dtype, kind="ExternalOutput").ap()
            idx = tuple(slice(None) for _ in t.shape)
            P.add("sync", lambda e, d=d, t=t, idx=idx: e.dma_start(out=d[idx], in_=t[idx]),
                  dma_sem=("dbg", name), is_output=True, extra_deps=last)
        P.finish("sync")
        P.emit(nc)
        raise _Stop()


    def din(name, shape, dt=F32):
        return nc.dram_tensor(name, list(shape), dt, kind="ExternalInput").ap()

    xkvT = din("xkvT", [D, T])
    xqT = din("xqT", [D, Q])
    memT = din("memT", [D, 256])
    poskv_d = din("poskv", [128, T], I32)
    posq_d = din("posq", [128, Q], I32)
    cst_d = din("cst", [128, NCST])
    lamv_d = din("lamv", [128, 256])
    cmat_d = din("cmat", [128, 256])
    w_in = din("w_in", [D, 8256])
    w_uq = din("w_uq", [512, 1536])
    w_ukv = din("w_ukv", [512, 2048])
    w_o_mla = din("w_o_mla", [1024, D])
    w_o_diff = din("w_o_diff", [1024, D])
    w_out = din("w_out", [D, D])
    w_cq = din("w_cross_q", [D, 512])
    w_ckv = din("w_cross_kv", [D, 1024])
    w_co = din("w_cross_o", [512, D])
    w_up = din("w_up", [D, 2 * FFN])
    w_down = din("w_down", [FFN, D])
    outT = nc.dram_tensor("outT", [D, Q], F32, kind="ExternalOutput").ap()
    dbg_list = []

    base = (nc.sbuf_base + 63) // 64 * 64
    asize = (nc.sbuf_top - base) // 64 * 64
    nc.alloc_sbuf_tensor("arena", [128, asize], U8)
    A = Arena(nc, P, base, asize)
    ps = nc.alloc_psum_tensor("ps", [128, 8, 512], F32)

    def PSK(b):
        return ("ps", b)

    def act(out, in_, func, reads, writes, bias=None, scale=None):
        kw = {}
        if bias is not None:
            kw["bias"] = bias
        if scale is not None:
            kw["scale"] = scale
        P.add("scalar", lambda e: e.activation(out=out, in_=in_, func=func, **kw), reads=reads, writes=writes)

    def tt(out, in0, in1, op, reads, writes):
        P.add("vector", lambda e: e.tensor_tensor(out=out, in0=in0, in1=in1, op=op), reads=reads, writes=writes)

    def ts(out, in0, s1, s2, op0, op1, reads, writes):
        if op1 is None:
            P.add("vector", lambda e: e.tensor_scalar(out=out, in0=in0, scalar1=s1, scalar2=None, op0=op0),
                  reads=reads, writes=writes)
        else:
            P.add("vector", lambda e: e.tensor_scalar(out=out, in0=in0, scalar1=s1, scalar2=s2, op0=op0, op1=op1),
                  reads=reads, writes=writes)

    def stt(out, in0, scalar, in1, op0, op1, reads, writes):
        P.add("vector", lambda e: e.scalar_tensor_tensor(out=out, in0=in0, scalar=scalar, in1=in1, op0=op0, op1=op1),
              reads=reads, writes=writes)

    def vcopy(out, in_, reads, writes):
        P.add("vector", lambda e: e.tensor_copy(out=out, in_=in_), reads=reads, writes=writes)

    def recip(out, in_, reads, writes):
        P.add("vector", lambda e: e.reciprocal(out=out, in_=in_), reads=reads, writes=writes)

    def mm(out, lhsT, rhs, start, stop, reads, bank):
        P.add("tensor", lambda e: e.matmul(out, lhsT=lhsT, rhs=rhs, start=start, stop=stop),
              reads=reads, writes=[PSK(bank)])

    def dma(eng, out, in_, reads, writes, sem, is_output=False):
        P.add(eng, lambda e: e.dma_start(out=out, in_=in_), reads=reads, writes=writes, dma_sem=sem,
              is_output=is_output)

    cst = A.alloc("cst", [128, NCST], F32)
    cmat = A.alloc("cmat", [128, 256], BF16)
    ones = A.alloc("ones", [128, 128], BF16)
    lamc = A.alloc("lamc", [128, 4], F32)
    dma("sync", cst[:], cst_d[:, :], [], ["cst"], "cst")
    dma("gpsimd", cmat[:], cmat_d[:, :], [], ["cmat"], "cmat")
    P.add("vector", lambda e: e.memset(ones[:], 1.0), writes=["ones"])
    Rm = cmat[:, 0:128]

    def ccol(c0, n=1):
        return cst[:, c0:c0 + n]

    NSLOT = 3
    SLOT_ELEMS = 4096
    wstate = {"i": 0, "wsl": None, "ns": NSLOT}

    def wload(src, nk, ncols):
        assert nk * ncols <= SLOT_ELEMS, (nk, ncols)
        s = wstate["i"] % wstate["ns"]
        wstate["i"] += 1
        wsl = wstate["wsl"]
        view = wsl[:, s, 0:nk * ncols].rearrange("p (k n) -> p k n", k=nk)
        dma("gpsimd", view, src.rearrange("(k p) n -> p k n", p=128), [], [("wsl", s)], ("wsl", s))
        return ("wsl", s), view

    TWO_PI = 2.0 * math.pi
    HI = 6.28125
    LO = TWO_PI - HI
    PI_S = 3.141592

    def rope_tables(pos_d, N, cosn, sinn):
        cos_t = A.alloc(cosn, [128, N], F32)
        sin_t = A.alloc(sinn, [128, N], F32)
        pi_ = A.alloc("rt_pi", [128, N], I32)
        r = A.alloc("rt_r", [128, N], F32)
        m = A.alloc("rt_m", [128, N], F32)
        dma("sync", pi_[:], pos_d[:, :], [], ["rt_pi"], "rt_pi")
        vcopy(r[:], pi_[:], ["rt_pi"], ["rt_r"])
        ts(r[:], r[:], ccol(CS_INV), None, ALU.mult, None, ["rt_r", "cst"], ["rt_r"])
        ts(pi_[:], r[:], 1.0 / TWO_PI, None, ALU.mult, None, ["rt_r"], ["rt_pi"])
        vcopy(m[:], pi_[:], ["rt_pi"], ["rt_m"])
        stt(r[:], m[:], -HI, r[:], ALU.mult, ALU.add, ["rt_m", "rt_r"], ["rt_r"])
        stt(r[:], m[:], -LO, r[:], ALU.mult, ALU.add, ["rt_m", "rt_r"], ["rt_r"])

        def wrap(v, vn):
            ts(m[:], v[:], math.pi, -TWO_PI, ALU.is_gt, ALU.mult, [vn], ["rt_m"])
            tt(v[:], v[:], m[:], ALU.add, [vn, "rt_m"], [vn])
            ts(m[:], v[:], -math.pi, TWO_PI, ALU.is_lt, ALU.mult, [vn], ["rt_m"])
            tt(v[:], v[:], m[:], ALU.add, [vn, "rt_m"], [vn])
            ts(v[:], v[:], -PI_S, PI_S, ALU.max, ALU.min, [vn], [vn])

        wrap(r, "rt_r")
        act(sin_t[:], r[:], AF.Sin, ["rt_r", "cst"], [sinn], scale=ccol(CS_SIGN))
        ts(r[:], r[:], math.pi / 2, None, ALU.add, None, ["rt_r"], ["rt_r"])
        wrap(r, "rt_r")
        act(cos_t[:], r[:], AF.Sin, ["rt_r"], [cosn])
        for n_ in ("rt_pi", "rt_r", "rt_m"):
            A.release(n_)
        return cos_t, sin_t

    def norm_T(src, srck, dst, dstk, nk, groups, gcol0, dn, sqn, bank_list, extra=1.0, per_group_keys=True):
        N = groups[-1][0] + groups[-1][1]
        rstd = A.alloc(sqn + "_rstd", [128, N], F32)
        sq = A.alloc(sqn + "_sq", [128, nk, 512], BF16)
        for gi, (c0, n) in enumerate(groups):
            bank = bank_list[gi % len(bank_list)]
            for kc in range(nk):
                act(sq[:, kc, 0:n], src(kc, c0, n), AF.Square, [srck(kc, gi)], [(sqn + "_sq", kc)])
            for kc in range(nk):
                mm(ps[:, bank, 0:n], ones[:, :], sq[:, kc, 0:n], kc == 0, kc == nk - 1,
                   ["ones", (sqn + "_sq", kc)], bank)
            act(rstd[:, c0:c0 + n], ps[:, bank, 0:n], AF.Sqrt, [PSK(bank), "cst"], [(sqn + "_rstd", gi)],
                bias=ccol(CS_EPS1 if extra == 1.0 else CS_EPS2), scale=float(1.0 / (dn * extra * extra)))
            recip(rstd[:, c0:c0 + n], rstd[:, c0:c0 + n], [(sqn + "_rstd", gi)], [(sqn + "_rstd", gi)])
            for kc in range(nk):
                stt(dst(kc, c0, n), src(kc, c0, n), ccol(gcol0 + kc), rstd[:, c0:c0 + n], ALU.mult, ALU.mult,
                    [srck(kc, gi), "cst", (sqn + "_rstd", gi)], [dstk(kc, gi)])
        A.release(sqn + "_rstd")
        A.release(sqn + "_sq")

    rstate = {"i": 0, "t": None}

    rope_pend = []

    def rope_flush():
        while rope_pend:
            rope_pend.pop(0)()

    def rope(bank, bank2, np_, n, cos_ap, sin_ap, tabkeys, out, outk):
        b = rstate["i"] % 2
        rstate["i"] += 1
        ropet = rstate["t"]
        qa = ropet[:, b, 512:768].bitcast(BF16)[0:np_, 0:n]
        t_ = ropet[0:np_, b, 0:n]
        u_ = ropet[0:np_, b, 768:768 + n]
        act(qa, ps[0:np_, bank, 0:n], AF.Copy, [PSK(bank)], [("ropet", b, 0)])
        rope_flush()

        def part_b():
            mm(ps[0:np_, bank2, 0:n], cmat[0:np_, 0:np_], qa, True, True, ["cmat", ("ropet", b, 0)], bank2)
            tt(t_, ps[0:np_, bank, 0:n], cos_ap, ALU.mult, [PSK(bank), ("ropet", b, 0)] + tabkeys, [("ropet", b, 1)])
            tt(u_, ps[0:np_, bank2, 0:n], sin_ap, ALU.mult, [PSK(bank2)] + tabkeys, [("ropet", b, 2)])
            tt(out, u_, t_, ALU.add, [("ropet", b, 1), ("ropet", b, 2)], [outk])

        rope_pend.append(part_b)

    checkpoint(-1, [("cst", cst), ("cmat", cmat), ("ones", ones)])
    bankc = {"i": 0}

    def nbank(lst):
        b = lst[bankc["i"] % len(lst)]
        bankc["i"] += 1
        return b

    hkv = A.alloc("hkv", [128, KC, T], BF16)
    xblk = A.alloc("xblk", [128, 2, KC, 512], F32)
    ckv32 = A.alloc("ckv32", [128, 4, T], F32)
    wstate["wsl"] = A.alloc("wsl", [128, NSLOT, SLOT_ELEMS], BF16, top=True)
    ckvw = [wload(w_in[:, C_CKV + blk2 * 256:C_CKV + (blk2 + 1) * 256], KC, 256) for blk2 in range(2)]
    def norm_blk(blk):
        xb = blk % 2
        dma("sync", xblk[:, xb], xkvT[:, blk * 512:(blk + 1) * 512].rearrange("(k p) t -> p k t", p=128),
            [], [("xblk", xb)], ("xblk", xb))
        norm_T(lambda kc, c0, n, xb=xb: xblk[:, xb, kc, c0:c0 + n], lambda kc, gi, xb=xb: ("xblk", xb),
               lambda kc, c0, n, blk=blk: hkv[:, kc, blk * 512 + c0:blk * 512 + c0 + n],
               lambda kc, gi, blk=blk: ("hkv", blk, kc),
               KC, [(0, 512)], CS_GMIX, D, "n1", [blk % 2])

    def proj_blk(blk):
        g, (c0, n) = blk, KG[blk]
        for cc in range(4):
            wk, wv = ckvw[cc // 2]
            cl = cc % 2
            bank = nbank([2, 3, 4, 5])
            for kc in range(KC):
                mm(ps[:, bank, 0:n], wv[:, kc, cl * 128:(cl + 1) * 128], hkv[:, kc, c0:c0 + n],
                   kc == 0, kc == KC - 1, [wk, ("hkv", g, kc)], bank)
            act(ckv32[:, cc, c0:c0 + n], ps[:, bank, 0:n], AF.Copy, [PSK(bank)], [("ckv32", cc, g)])

    norm_blk(0)
    for blk in range(4):
        if blk + 1 < 4:
            norm_blk(blk + 1)
        proj_blk(blk)
    checkpoint(1, [("hkv", hkv)])
    A.release("xblk")

    ckvn = A.alloc("ckvn", [128, 4, T], BF16)
    kpe = A.alloc("kpe", [128, T], BF16)
    P.add("vector", lambda e: e.memset(kpe[64:128, :], 0.0), writes=[("kpe", "z")])
    norm_T(lambda kc, c0, n: ckv32[:, kc, c0:c0 + n], lambda kc, gi: ("ckv32", kc, gi),
           lambda kc, c0, n: ckvn[:, kc, c0:c0 + n], lambda kc, gi: ("ckvn", kc, gi),
           4, KG, CS_GKV, 512, "n2", [0, 1])
    checkpoint(21, [("ckvn", ckvn)])
    A.release("ckv32")

    dv = A.alloc("dv", [128, 16, 1024], BF16)
    for cb in range(4):
        wk, wv = wload(w_in[:, C_DV + cb * 256:C_DV + (cb + 1) * 256], KC, 256)
        for tp in range(8):
            bank = nbank([2, 3, 4, 5])
            for sub in range(2):
                tc = tp * 2 + sub
                for kc in range(KC):
                    mm(ps[:, bank, sub * 256:(sub + 1) * 256], hkv[:, kc, tc * 128:(tc + 1) * 128], wv[:, kc, :],
                       kc == 0, kc == KC - 1, [wk, ("hkv", tc // 4, kc)], bank)
            act(dv[:, tp * 2:tp * 2 + 2, cb * 256:(cb + 1) * 256],
                ps[:, bank, :].rearrange("p (a b) -> p a b", a=2), AF.Copy, [PSK(bank)], [("dv", tp, cb)])
    cos_kv, sin_kv = rope_tables(poskv_d, T, "cos_kv", "sin_kv")
    lamv = A.alloc("lamv", [128, 256], F32)
    lamt = A.alloc("lamt", [128, 128], F32)
    dma("sync", lamv[:], lamv_d[:, :], [], ["lamv"], "lamv")
    tt(lamt[:, 0:64], lamv[:, 0:64], lamv[:, 64:128], ALU.mult, ["lamv"], [("lamt", 0)])
    tt(lamt[:, 64:128], lamv[:, 128:192], lamv[:, 192:256], ALU.mult, ["lamv"], [("lamt", 1)])
    P.add("vector", lambda e: e.reduce_sum(out=lamc[:, 2:3], in_=lamt[:, 0:64], axis=AX.X),
          reads=[("lamt", 0)], writes=[("lamc", 2)])
    P.add("vector", lambda e: e.reduce_sum(out=lamc[:, 3:4], in_=lamt[:, 64:128], axis=AX.X),
          reads=[("lamt", 1)], writes=[("lamc", 3)])
    act(lamc[:, 2:4], lamc[:, 2:4], AF.Exp, [("lamc", 2), ("lamc", 3)], [("lamc", 2), ("lamc", 3)])
    tt(lamc[:, 0:1], lamc[:, 2:3], lamc[:, 3:4], ALU.subtract, [("lamc", 2), ("lamc", 3)], [("lamc", 0)])
    ts(lamc[:, 0:1], lamc[:, 0:1], float(LAM_INIT), None, ALU.add, None, [("lamc", 0)], [("lamc", 0)])
    ts(lamc[:, 1:2], lamc[:, 0:1], -1.0, None, ALU.mult, None, [("lamc", 0)], [("lamc", 1)])
    A.release("lamv")
    A.release("lamt")
    rstate["t"] = A.alloc("ropet", [128, 2, 1280], F32)
    dk = A.alloc("dk", [128, 8, T], BF16)

    wk, wv = wload(w_in[:, C_KPE:C_KPE + 64], KC, 64)
    for g in [int(c) for c in os.environ.get("KPEG", "0123")]:
        c0, n = KG[g]
        bank = nbank([int(c) for c in os.environ.get("KB", "2345")])
        KCX = int(os.environ.get("KCX", "16"))
        for kc in range(KCX):
            mm(ps[0:64, bank, 0:n], wv[:, kc, 0:64], hkv[:, kc, c0:c0 + n], kc == 0, kc == KCX - 1,
               [wk, ("hkv", g, kc)], bank)
        rope(bank, int(os.environ["B2X"][rstate["i"] % len(os.environ["B2X"])]) if os.environ.get("B2X") else 6 + g % 2, 64, n,
             cos_kv[0:64, c0:c0 + n], sin_kv[0:64, c0:c0 + n], ["cos_kv", "sin_kv"],
             kpe[0:64, c0:c0 + n], ("kpe", g))

    checkpoint(22, [("ckvn", ckvn), ("kpe", kpe)])
    rope_flush()
    for blk2 in range(4):
        wk, wv = wload(w_in[:, C_DK + blk2 * 256:C_DK + (blk2 + 1) * 256], KC, 256)
        for cl in range(2):
            h = blk2 * 2 + cl
            for g, (c0, n) in enumerate(KG):
                bank = nbank([2, 3, 4, 5])
                for kc in range(KC):
                    mm(ps[:, bank, 0:n], wv[:, kc, cl * 128:(cl + 1) * 128], hkv[:, kc, c0:c0 + n],
                       kc == 0, kc == KC - 1, [wk, ("hkv", g, kc)], bank)
                rope(bank, 6 + g % 2, 128, n, cos_kv[:, c0:c0 + n], sin_kv[:, c0:c0 + n], ["cos_kv", "sin_kv"],
                     dk[:, h, c0:c0 + n], ("dk", h, g))

    rope_flush()
    checkpoint(2, [("ckvn", ckvn), ("kpe", kpe), ("dk", dk), ("dv", dv)])
    A.release("hkv")
    A.release("cos_kv")
    A.release("sin_kv")

    cos_q, sin_q = rope_tables(posq_d, Q, "cos_q", "sin_q")
    hq = A.alloc("hq", [128, KC, Q], BF16)
    xqb = A.alloc("xqb", [128, KC, 342], F32)
    for gi, (c0, n) in enumerate(QG):
        dma("sync", xqb[:, :, 0:n], xqT[:, c0:c0 + n].rearrange("(k p) t -> p k t", p=128), [], ["xqb"], "xqb")
        norm_T(lambda kc, c0_, n_: xqb[:, kc, 0:n_], lambda kc, gi_: "xqb",
               lambda kc, c0_, n_, c0=c0: hq[:, kc, c0:c0 + n_], lambda kc, gi_, gi=gi: ("hq", gi, kc),
               KC, [(0, n)], CS_GMIX, D, "n3", [gi % 2])
    checkpoint(3, [("hq", hq)])
    A.release("xqb")

    def hq_keys(g):
        return [("hq", g, kc) for kc in range(KC)]

    O_BANK, SUM_BANK = 6, 7
    LOOK = 2
    ast = {"i": 0, "c": 0}
    deferred = []

    def attn_alloc():
        return (A.alloc("pT", [128, 3, 2, 342], BF16), A.alloc("osb", [128, 2, 342], F32),
                A.alloc("ssb", [128, 2, 342], F32))

    def attn_core(bufs, nkc, n, qk_list, v_of, scale, after):
        pT, osb, ssb = bufs
        npairs = nkc // 2
        pend = []

        def s_stage(p):
            sp = p % 3
            for j in range(2):
                kc = 2 * p + j
                sb = 2 * sp + j
                for i, (lo, rhs, ko) in enumerate(qk_list):
                    mm(ps[:, sb, 0:n], lo(kc), rhs[0], i == 0, i == len(qk_list) - 1, ko(kc) + rhs[1], sb)
            act(pT[:, sp, :, 0:n], ps[:, 2 * sp:2 * sp + 2, 0:n], AF.Exp, [PSK(2 * sp), PSK(2 * sp + 1)],
                [("pT", sp)], scale=float(scale))
            return sp

        def pv_stage(p, sp):
            for j in range(2):
                kc = 2 * p + j
                vl, vk = v_of(kc)
                mm(ps[:, O_BANK, 0:n], vl, pT[:, sp, j, 0:n], kc == 0, kc == nkc - 1, vk + [("pT", sp)], O_BANK)
                mm(ps[:, SUM_BANK, 0:n], ones[:, :], pT[:, sp, j, 0:n], kc == 0, kc == nkc - 1,
                   ["ones", ("pT", sp)], SUM_BANK)

        for p in range(npairs):
            pend.append((p, s_stage(p)))
            if p == min(4, npairs - 1):
                while deferred:
                    deferred.pop(0)()
            if len(pend) > LOOK:
                pv_stage(*pend.pop(0))
        while pend:
            pv_stage(*pend.pop(0))
        cb = ast["c"] % 2
        ast["c"] += 1
        vcopy(osb[:, cb, 0:n], ps[:, O_BANK, 0:n], [PSK(O_BANK)], [("osb", cb)])
        vcopy(ssb[:, cb, 0:n], ps[:, SUM_BANK, 0:n], [PSK(SUM_BANK)], [("ssb", cb)])
        recip(ssb[:, cb, 0:n], ssb[:, cb, 0:n], [("ssb", cb)], [("ssb", cb)])
        after(osb[:, cb, 0:n], ("osb", cb), ssb[:, cb, 0:n], ("ssb", cb))

    def attn_stream(bufs, calls):
        pT, osb, ssb = bufs
        jobs = [(ci, p) for ci, c in enumerate(calls) for p in range(c["nkc"] // 2)]

        def s_stage(ji):
            ci, p = jobs[ji]
            c = calls[ci]
            n = c["n"]
            if p == 0 and c.get("pre"):
                c["pre"]()
            sp = ji % 3
            for j in range(2):
                kc = 2 * p + j
                sb = 2 * sp + j
                for i, (lo, rhs, ko) in enumerate(c["qk"]):
                    mm(ps[:, sb, 0:n], lo(kc), rhs[0], i == 0, i == len(c["qk"]) - 1, ko(kc) + rhs[1], sb)
            act(pT[:, sp, :, 0:n], ps[:, 2 * sp:2 * sp + 2, 0:n], AF.Exp, [PSK(2 * sp), PSK(2 * sp + 1)],
                [("pT", sp)], scale=float(c["scale"]))

        def pv_stage(ji):
            ci, p = jobs[ji]
            c = calls[ci]
            n, nkc = c["n"], c["nkc"]
            sp = ji % 3
            for j in range(2):
                kc = 2 * p + j
                vl, vk = c["v_of"](kc)
                mm(ps[:, O_BANK, 0:n], vl, pT[:, sp, j, 0:n], kc == 0, kc == nkc - 1, vk + [("pT", sp)], O_BANK)
                mm(ps[:, SUM_BANK, 0:n], ones[:, :], pT[:, sp, j, 0:n], kc == 0, kc == nkc - 1,
                   ["ones", ("pT", sp)], SUM_BANK)
            if p == nkc // 2 - 1:
                cb = ast["c"] % 2
                ast["c"] += 1
                ast["job"] = ji
                vcopy(osb[:, cb, 0:n], ps[:, O_BANK, 0:n], [PSK(O_BANK)], [("osb", cb)])
                vcopy(ssb[:, cb, 0:n], ps[:, SUM_BANK, 0:n], [PSK(SUM_BANK)], [("ssb", cb)])
                recip(ssb[:, cb, 0:n], ssb[:, cb, 0:n], [("ssb", cb)], [("ssb", cb)])
                c["after"](osb[:, cb, 0:n], ("osb", cb), ssb[:, cb, 0:n], ("ssb", cb))

        pend = []
        for ji in range(len(jobs)):
            pend.append(ji)
            s_stage(ji)
            if len(pend) > LOOK:
                jx = pend.pop(0)
                pv_stage(jx)
                while deferred and ji - deferred[0][1] >= 9:
                    deferred.pop(0)[0](2 * (jx % 3))
        while pend:
            pv_stage(pend.pop(0))

    dq = A.alloc("dq", [128, 8, Q], BF16)
    for blk2 in range(4):
        wk, wv = wload(w_in[:, C_DQ + blk2 * 256:C_DQ + (blk2 + 1) * 256], KC, 256)
        for cl in range(2):
            h = blk2 * 2 + cl
            for g, (c0, n) in enumerate(QG):
                bank = nbank([5, 6])
                for kc in range(KC):
                    mm(ps[:, bank, 0:n], wv[:, kc, cl * 128:(cl + 1) * 128], hq[:, kc, c0:c0 + n],
                       kc == 0, kc == KC - 1, [wk, ("hq", g, kc)], bank)
                rope(bank, 7, 128, n, cos_q[:, c0:c0 + n], sin_q[:, c0:c0 + n], ["cos_q", "sin_q"],
                     dq[:, h, c0:c0 + n], ("dq", h, g))

    rope_flush()
    A.release("cos_q")
    A.release("sin_q")
    A.release("ropet")
    A.release("wsl")
    ob = A.alloc("ob", [128, 8, Q], BF16)
    dtmp = A.alloc("dtmp", [128, 2, 3, 342], F32)
    dsq = A.alloc("dsq", [128, 2, 342], BF16)
    abufs = attn_alloc()
    dqm = A.alloc("dqm", [128, 2, 2, Q], BF16)
    for hb_ in range(2):
        P.add("vector", lambda e, hb_=hb_: e.memset(dqm[64:128, hb_, 0, :], 0.0), writes=[("dqm", hb_, "z0")])
        P.add("vector", lambda e, hb_=hb_: e.memset(dqm[0:64, hb_, 1, :], 0.0), writes=[("dqm", hb_, "z1")])
    dpar = {"i": 0}
    dcalls = []
    for h in range(8):
        hb_ = h % 2

        def pre(h=h, hb_=hb_):
            vcopy(dqm[0:64, hb_, 0, :], dq[0:64, h, :], [("dq", h, g_) for g_ in range(3)], [("dqm", hb_, 0)])
            vcopy(dqm[64:128, hb_, 1, :], dq[64:128, h, :], [("dq", h, g_) for g_ in range(3)], [("dqm", hb_, 1)])

        for g, (c0, n) in enumerate(QG):
            pp = dpar["i"] % 2
            dpar["i"] += 1
            for c in range(2):
                def after(O, ok, rs, rsk, c=c, h=h, g=g, c0=c0, n=n, pp=pp):
                    if c == 0:
                        tt(dtmp[:, pp, 0, 0:n], O, rs, ALU.mult, [ok, rsk], [("dtmp", pp, 0)])
                    else:
                        tt(dtmp[:, pp, 1, 0:n], O, rs, ALU.mult, [ok, rsk], [("dtmp", pp, 1)])
                        stt(dtmp[:, pp, 1, 0:n], dtmp[:, pp, 1, 0:n], lamc[:, 1:2], dtmp[:, pp, 0, 0:n], ALU.mult,
                            ALU.add, [("dtmp", pp, 0), ("dtmp", pp, 1), ("lamc", 1)], [("dtmp", pp, 1)])
                        tt(dsq[:, pp, 0:n], dtmp[:, pp, 1, 0:n], dtmp[:, pp, 1, 0:n], ALU.mult, [("dtmp", pp, 1)],
                           [("dsq", pp)])

                        def fin(bank, h=h, g=g, c0=c0, n=n, pp=pp):
                            mm(ps[:, bank, 0:n], ones[:, :], dsq[:, pp, 0:n], True, True, ["ones", ("dsq", pp)], bank)
                            ex = 1.0 - LAM_INIT
                            act(dtmp[:, pp, 2, 0:n], ps[:, bank, 0:n], AF.Ln, [PSK(bank), "cst"], [("dtmp", pp, 2)],
                                bias=ccol(CS_EPS2), scale=float(1.0 / (128.0 * ex * ex)))
                            act(dtmp[:, pp, 2, 0:n], dtmp[:, pp, 2, 0:n], AF.Exp, [("dtmp", pp, 2)], [("dtmp", pp, 2)],
                                scale=-0.5)
                            stt(ob[:, h, c0:c0 + n], dtmp[:, pp, 1, 0:n], ccol(CS_GDIFF), dtmp[:, pp, 2, 0:n],
                                ALU.mult, ALU.mult, [("dtmp", pp, 1), ("dtmp", pp, 2), "cst"], [("ob", h, g)])

                        deferred.append((fin, ast["job"]))

                dcalls.append(dict(
                    nkc=16, n=n, scale=64.0 ** -0.5, after=after, pre=(pre if (g == 0 and c == 0) else None),
                    qk=[(lambda kc, h=h: dk[:, h, kc * 128:(kc + 1) * 128],
                         (dqm[:, hb_, c, c0:c0 + n], [("dqm", hb_, c), ("dqm", hb_, "z0"), ("dqm", hb_, "z1")]),
                         lambda kc, h=h: [("dk", h, kc // 4)])],
                    v_of=lambda kc, h=h: (dv[:, kc, h * 128:(h + 1) * 128], [("dv", kc // 2, h // 2)])))
    attn_stream(abufs, dcalls)
    while deferred:
        deferred.pop(0)[0](0)
    checkpoint(4, [("ob", ob)])
    for n_ in ("dq", "dqm", "dk", "dv", "dtmp", "dsq", "pT", "osb", "ssb"):
        A.release(n_)
    wstate["wsl"] = A.alloc("wsl", [128, NSLOT, SLOT_ELEMS], BF16)
    rstate["t"] = A.alloc("ropet", [128, 2, 1280], F32)

    cos_q, sin_q = rope_tables(posq_d, Q, "cos_q", "sin_q")
    cq32 = A.alloc("cq32", [128, 4, Q], F32)
    for blk2 in range(2):
        wk, wv = wload(w_in[:, C_CQ + blk2 * 256:C_CQ + (blk2 + 1) * 256], KC, 256)
        for cl in range(2):
            cc = blk2 * 2 + cl
            for g, (c0, n) in enumerate(QG):
                bank = nbank([0, 1, 2, 3, 4, 5])
                for kc in range(KC):
                    mm(ps[:, bank, 0:n], wv[:, kc, cl * 128:(cl + 1) * 128], hq[:, kc, c0:c0 + n],
                       kc == 0, kc == KC - 1, [wk, ("hq", g, kc)], bank)
                act(cq32[:, cc, c0:c0 + n], ps[:, bank, 0:n], AF.Copy, [PSK(bank)], [("cq32", cc, g)])
    cqn = A.alloc("cqn", [128, 4, Q], BF16)
    norm_T(lambda kc, c0, n: cq32[:, kc, c0:c0 + n], lambda kc, gi: ("cq32", kc, gi),
           lambda kc, c0, n: cqn[:, kc, c0:c0 + n], lambda kc, gi: ("cqn", kc, gi),
           4, QG, CS_GQ, 512, "n4", [5, 6])
    A.release("cq32")

    oa = A.alloc("oa", [128, 8, Q], BF16)
    qn = A.alloc("qn", [128, 2, Q], BF16)
    qp = A.alloc("qp", [128, 2, Q], BF16)
    P.add("vector", lambda e: e.memset(qp[64:128, :, :], 0.0), writes=[("qp", "z")])
    kn = A.alloc("kn", [128, 2, T], BF16)
    vm = A.alloc("vm", [128, 2, 16, 128], BF16)

    abufs = attn_alloc()

    def mla_prep(h):
        hb = h % 2
        wk, wv = wload(w_uq[:, h * 192:(h + 1) * 192], 4, 192)
        for g, (c0, n) in enumerate(QG):
            bank = nbank([0, 1, 2, 3, 4])
            for kc in range(4):
                mm(ps[:, bank, 0:n], wv[:, kc, 0:128], cqn[:, kc, c0:c0 + n], kc == 0, kc == 3,
                   [wk, ("cqn", kc, g)], bank)
            act(qn[:, hb, c0:c0 + n], ps[:, bank, 0:n], AF.Copy, [PSK(bank)], [("qn", hb, g)])
            bank = nbank([0, 1, 2, 3, 4])
            for kc in range(4):
                mm(ps[0:64, bank, 0:n], wv[:, kc, 128:192], cqn[:, kc, c0:c0 + n], kc == 0, kc == 3,
                   [wk, ("cqn", kc, g)], bank)
            rope(bank, 5, 64, n, cos_q[0:64, c0:c0 + n], sin_q[0:64, c0:c0 + n], ["cos_q", "sin_q"],
                 qp[0:64, hb, c0:c0 + n], ("qp", hb, g))
        rope_flush()
        wk, wv = wload(w_ukv[:, h * 256:(h + 1) * 256], 4, 256)
        for g, (c0, n) in enumerate(KG):
            bank = nbank([0, 1, 2, 3, 4])
            for kc in range(4):
                mm(ps[:, bank, 0:n], wv[:, kc, 0:128], ckvn[:, kc, c0:c0 + n], kc == 0, kc == 3,
                   [wk, ("ckvn", kc, g)], bank)
            act(kn[:, hb, c0:c0 + n], ps[:, bank, 0:n], AF.Copy, [PSK(bank)], [("kn", hb, g)])
        for tq in range(4):
            bank = nbank([0, 1, 2, 3, 4])
            for sub in range(4):
                tc = tq * 4 + sub
                for kc in range(4):
                    mm(ps[:, bank, sub * 128:(sub + 1) * 128], ckvn[:, kc, tc * 128:(tc + 1) * 128],
                       wv[:, kc, 128:256], kc == 0, kc == 3, [wk, ("ckvn", kc, tq)], bank)
            act(vm[:, hb, tq * 4:(tq + 1) * 4, :], ps[:, bank, :].rearrange("p (a b) -> p a b", a=4), AF.Copy,
                [PSK(bank)], [("vm", hb, tq)])

    mla_prep(0)
    for h in range(8):
        hb = h % 2
        if h + 1 < 8:
            mla_prep(h + 1)
        mcalls = []
        for g, (c0, n) in enumerate(QG):
            def after(O, ok, rs, rsk, h=h, g=g, c0=c0, n=n):
                tt(oa[:, h, c0:c0 + n], O, rs, ALU.mult, [ok, rsk], [("oa", h, g)])

            mcalls.append(dict(
                nkc=16, n=n, scale=192.0 ** -0.5, after=after,
                qk=[(lambda kc, hb=hb: kn[:, hb, kc * 128:(kc + 1) * 128],
                     (qn[:, hb, c0:c0 + n], [("qn", hb, g)]),
                     lambda kc, hb=hb: [("kn", hb, kc // 4)]),
                    (lambda kc: kpe[:, kc * 128:(kc + 1) * 128],
                     (qp[:, hb, c0:c0 + n], [("qp", hb, g), ("qp", "z")]),
                     lambda kc: [("kpe", kc // 4), ("kpe", "z")])],
                v_of=lambda kc, hb=hb: (vm[:, hb, kc, :], [("vm", hb, kc // 4)])))
        attn_stream(abufs, mcalls)
    checkpoint(5, [("oa", oa), ("cqn", cqn)])
    for n_ in ("cqn", "qn", "qp", "kn", "vm", "ckvn", "kpe", "cos_q", "sin_q", "pT", "osb", "ssb"):
        A.release(n_)

    A.release("wsl")
    wstate["ns"] = 4
    wstate["wsl"] = A.alloc("wsl", [128, 4, SLOT_ELEMS], BF16)
    mt = A.alloc("mt", [128, KC, Q], BF16)
    sg = A.alloc("sg", [128, 2, 342], F32)
    mtmp = A.alloc("mtmp", [128, 2, 342], F32)
    t1 = A.alloc("t1", [128, 2, Q], F32)
    ALLB = [0, 1, 2, 3, 4, 5, 6, 7]
    ust = {"i": 0}
    for jb in range(8):
        for br in range(2):
            wo_d, gcol, src_t, srcn = ((w_o_mla, C_GA, oa, "oa"), (w_o_diff, C_GB, ob, "ob"))[br]
            wko, wvo = wload(wo_d[:, jb * 256:(jb + 1) * 256], 8, 256)
            wkg, wvg = wload(w_in[:, gcol + jb * 256:gcol + (jb + 1) * 256], KC, 256)
            for jl in range(2):
                j = jb * 2 + jl
                cs = slice(jl * 128, (jl + 1) * 128)
                for g, (c0, n) in enumerate(QG):
                    sb_ = ust["i"] % 2
                    ust["i"] += 1
                    bg = nbank(ALLB)
                    for kc in range(KC):
                        mm(ps[:, bg, 0:n], wvg[:, kc, cs], hq[:, kc, c0:c0 + n], kc == 0, kc == KC - 1,
                           [wkg, ("hq", g, kc)], bg)
                    act(sg[:, sb_, 0:n], ps[:, bg, 0:n], AF.Sigmoid, [PSK(bg)], [("sg", sb_)])
                    by = nbank(ALLB)
                    for kc in range(8):
                        mm(ps[:, by, 0:n], wvo[:, kc, cs], src_t[:, kc, c0:c0 + n], kc == 0, kc == 7,
                           [wko, (srcn, kc, g)], by)
                    if br == 0:
                        tt(t1[:, jl, c0:c0 + n], ps[:, by, 0:n], sg[:, sb_, 0:n], ALU.mult,
                           [PSK(by), ("sg", sb_)], [("t1", jl, g)])
                    else:
                        tt(mtmp[:, sb_, 0:n], ps[:, by, 0:n], sg[:, sb_, 0:n], ALU.mult,
                           [PSK(by), ("sg", sb_)], [("mtmp", sb_)])
                        tt(mt[:, j, c0:c0 + n], mtmp[:, sb_, 0:n], t1[:, jl, c0:c0 + n], ALU.add,
                           [("mtmp", sb_), ("t1", jl, g)], [("mt", j, g)])
    for n_ in ("hq", "oa", "ob", "sg", "mtmp", "t1", "ropet"):
        A.release(n_)

    xres = A.alloc("xres", [128, KC, Q], F32)
    for kq in range(4):
        dma("sync", xres[:, kq * 4:(kq + 1) * 4, :],
            xqT[kq * 512:(kq + 1) * 512, :].rearrange("(k p) t -> p k t", p=128), [],
            [("xres", kc, g) for kc in range(kq * 4, kq * 4 + 4) for g in range(3)], ("xres", kq))

    def proj_add(wsrc_of_block, nk, act_t, actkeys, nblocks=8):
        for jb in range(nblocks):
            wk, wv = wsrc_of_block(jb)
            for jl in range(2):
                j = jb * 2 + jl
                for g, (c0, n) in enumerate(QG):
                    bank = nbank([0, 1, 2, 3, 4, 5, 6, 7])
                    for kc in range(nk):
                        mm(ps[:, bank, 0:n], wv[:, kc, jl * 128:(jl + 1) * 128], act_t[:, kc, c0:c0 + n],
                           kc == 0, kc == nk - 1, [wk] + actkeys(kc, g), bank)
                    tt(xres[:, j, c0:c0 + n], ps[:, bank, 0:n], xres[:, j, c0:c0 + n], ALU.add,
                       [PSK(bank), ("xres", j, g)], [("xres", j, g)])

    checkpoint(6, [("mt", mt)])
    proj_add(lambda jb: wload(w_out[:, jb * 256:(jb + 1) * 256], KC, 256), KC, mt,
             lambda kc, g: [("mt", kc, g)])
    checkpoint(7, [("xres", xres)])
    A.release("mt")

    h2 = A.alloc("h2", [128, KC, Q], BF16)
    norm_T(lambda kc, c0, n: xres[:, kc, c0:c0 + n], lambda kc, gi: ("xres", kc, gi),
           lambda kc, c0, n: h2[:, kc, c0:c0 + n], lambda kc, gi: ("h2", kc, gi),
           KC, QG, CS_GCROSS, D, "n5", [0, 1])
    mem32 = A.alloc("mem32", [128, KC, 256], F32)
    memn = A.alloc("memn", [128, KC, 256], BF16)
    dma("sync", mem32[:], memT[:, :].rearrange("(k p) t -> p k t", p=128), [], ["mem32"], "mem32")
    norm_T(lambda kc, c0, n: mem32[:, kc, c0:c0 + n], lambda kc, gi: "mem32",
           lambda kc, c0, n: memn[:, kc, c0:c0 + n], lambda kc, gi: ("memn", kc),
           KC, [(0, 256)], CS_GMEM, D, "n6", [2])
    A.release("mem32")
    qx = A.alloc("qx", [128, 4, Q], BF16)
    kx = A.alloc("kx", [128, 4, 256], BF16)
    vx = A.alloc("vx", [128, 2, 512], BF16)
    oc = A.alloc("oc", [128, 4, Q], BF16)
    abufs = attn_alloc()
    for hb2 in range(2):
        wk, wv = wload(w_cq[:, hb2 * 256:(hb2 + 1) * 256], KC, 256)
        for hl in range(2):
            h = hb2 * 2 + hl
            for g, (c0, n) in enumerate(QG):
                bank = nbank([0, 1, 2, 3, 4, 5])
                for kc in range(KC):
                    mm(ps[:, bank, 0:n], wv[:, kc, hl * 128:(hl + 1) * 128], h2[:, kc, c0:c0 + n],
                       kc == 0, kc == KC - 1, [wk, ("h2", kc, g)], bank)
                act(qx[:, h, c0:c0 + n], ps[:, bank, 0:n], AF.Copy, [PSK(bank)], [("qx", h, g)])
    for h in range(4):
        wk, wv = wload(w_ckv[:, h * 256:(h + 1) * 256], KC, 256)
        bank = nbank([0, 1, 2, 3, 4, 5])
        for kc in range(KC):
            mm(ps[:, bank, 0:256], wv[:, kc, 0:128], memn[:, kc, :], kc == 0, kc == KC - 1,
               [wk, ("memn", kc)], bank)
        act(kx[:, h, :], ps[:, bank, 0:256], AF.Copy, [PSK(bank)], [("kx", h)])
        bank = nbank([0, 1, 2, 3, 4, 5])
        for tc in range(2):
            for kc in range(KC):
                mm(ps[:, bank, tc * 128:(tc + 1) * 128], memn[:, kc, tc * 128:(tc + 1) * 128], wv[:, kc, 128:256],
                   kc == 0, kc == KC - 1, [wk, ("memn", kc)], bank)
        act(vx[:, :, h * 128:(h + 1) * 128], ps[:, bank, 0:256].rearrange("p (a b) -> p a b", a=2), AF.Copy,
            [PSK(bank)], [("vx", h)])
    pTx, osbx, ssbx = abufs
    xcalls = [(h, g, c0, n) for h in range(4) for g, (c0, n) in enumerate(QG)]
    xscale = 128.0 ** -0.5

    def xs_stage(i):
        h, g, c0, n = xcalls[i]
        sp = i % 3
        for j in range(2):
            mm(ps[:, 2 * sp + j, 0:n], kx[:, h, j * 128:(j + 1) * 128], qx[:, h, c0:c0 + n], True, True,
               [("kx", h), ("qx", h, g)], 2 * sp + j)
        act(pTx[:, sp, :, 0:n], ps[:, 2 * sp:2 * sp + 2, 0:n], AF.Exp, [PSK(2 * sp), PSK(2 * sp + 1)],
            [("pT", sp)], scale=float(xscale))

    def xpv_stage(i):
        h, g, c0, n = xcalls[i]
        sp = i % 3
        cb = i % 2
        for j in range(2):
            mm(ps[:, O_BANK, 0:n], vx[:, j, h * 128:(h + 1) * 128], pTx[:, sp, j, 0:n], j == 0, j == 1,
               [("vx", h), ("pT", sp)], O_BANK)
            mm(ps[:, SUM_BANK, 0:n], ones[:, :], pTx[:, sp, j, 0:n], j == 0, j == 1, ["ones", ("pT", sp)], SUM_BANK)
        vcopy(osbx[:, cb, 0:n], ps[:, O_BANK, 0:n], [PSK(O_BANK)], [("osb", cb)])
        vcopy(ssbx[:, cb, 0:n], ps[:, SUM_BANK, 0:n], [PSK(SUM_BANK)], [("ssb", cb)])
        recip(ssbx[:, cb, 0:n], ssbx[:, cb, 0:n], [("ssb", cb)], [("ssb", cb)])
        tt(oc[:, h, c0:c0 + n], osbx[:, cb, 0:n], ssbx[:, cb, 0:n], ALU.mult, [("osb", cb), ("ssb", cb)],
           [("oc", h, g)])

    xpend = []
    for i in range(len(xcalls)):
        xpend.append(i)
        xs_stage(i)
        if len(xpend) > LOOK:
            xpv_stage(xpend.pop(0))
    while xpend:
        xpv_stage(xpend.pop(0))
    proj_add(lambda jb: wload(w_co[:, jb * 256:(jb + 1) * 256], 4, 256), 4, oc,
             lambda kc, g: [("oc", kc, g)])
    checkpoint(8, [("xres", xres), ("oc", oc)])
    for n_ in ("h2", "memn", "qx", "kx", "vx", "oc", "pT", "osb", "ssb"):
        A.release(n_)

    h3 = A.alloc("h3", [128, KC, Q], BF16)
    norm_T(lambda kc, c0, n: xres[:, kc, c0:c0 + n], lambda kc, gi: ("xres", kc, gi),
           lambda kc, c0, n: h3[:, kc, c0:c0 + n], lambda kc, gi: ("h3", kc, gi),
           KC, QG, CS_GFFN, D, "n7", [0, 1])
    NQ = 4
    CPQ = 11
    aT = A.alloc("aT", [128, CPQ, Q], BF16)
    ubuf = A.alloc("ubuf", [128, 2, 2, Q + 2], F32)
    cbuf = A.alloc("cbuf", [128, 2, 2, Q], F32)
    for pb in range(2):
        for s_ in range(2):
            P.add("vector", lambda e, pb=pb, s_=s_: e.memset(ubuf[:, pb, s_, 0:1], 0.0), writes=[("ubuf", pb, s_, "l")])
            P.add("vector", lambda e, pb=pb, s_=s_: e.memset(ubuf[:, pb, s_, Q + 1:Q + 2], 0.0),
                  writes=[("ubuf", pb, s_, "r")])
    for qd in range(NQ):
        for cl in range(CPQ):
            jj = qd * CPQ + cl
            pb = jj % 2
            wkg, wvg = wload(w_up[:, jj * 128:(jj + 1) * 128], KC, 128)
            wkv_, wvv = wload(w_up[:, FFN + jj * 128:FFN + (jj + 1) * 128], KC, 128)
            for s_, (wk, wv) in enumerate(((wkg, wvg), (wkv_, wvv))):
                for g, (c0, n) in enumerate(QG):
                    bank = nbank([0, 1, 2, 3, 4, 5, 6, 7])
                    for kc in range(KC):
                        mm(ps[:, bank, 0:n], wv[:, kc, :], h3[:, kc, c0:c0 + n], kc == 0, kc == KC - 1,
                           [wk, ("h3", kc, g)], bank)
                    act(ubuf[:, pb, s_, 1 + c0:1 + c0 + n], ps[:, bank, 0:n], AF.Copy, [PSK(bank)],
                        [("ubuf", pb, s_, g)])
                ukeys = [("ubuf", pb, s_, g) for g in range(3)] + [("ubuf", pb, s_, "l"), ("ubuf", pb, s_, "r")]
                col = s_ * 44 + jj
                ts(cbuf[:, pb, s_, :], ubuf[:, pb, s_, 1:Q + 1], ccol(CS_CW + 1 * 88 + col), ccol(CS_CB + col),
                   ALU.mult, ALU.add, ukeys + ["cst"], [("cbuf", pb, s_)])
                stt(cbuf[:, pb, s_, :], ubuf[:, pb, s_, 0:Q], ccol(CS_CW + 0 * 88 + col), cbuf[:, pb, s_, :],
                    ALU.mult, ALU.add, ukeys + ["cst", ("cbuf", pb, s_)], [("cbuf", pb, s_)])
                stt(cbuf[:, pb, s_, :], ubuf[:, pb, s_, 2:Q + 2], ccol(CS_CW + 2 * 88 + col), cbuf[:, pb, s_, :],
                    ALU.mult, ALU.add, ukeys + ["cst", ("cbuf", pb, s_)], [("cbuf", pb, s_)])
            act(cbuf[:, pb, 0, :], cbuf[:, pb, 0, :], AF.Silu, [("cbuf", pb, 0)], [("cbuf", pb, 0)])
            tt(aT[:, cl, :], cbuf[:, pb, 0, :], cbuf[:, pb, 1, :], ALU.mult, [("cbuf", pb, 0), ("cbuf", pb, 1)],
               [("aT", cl)])
        proj_add(lambda jb, qd=qd: wload(w_down[qd * CPQ * 128:(qd + 1) * CPQ * 128, jb * 256:(jb + 1) * 256],
                                         CPQ, 256),
                 CPQ, aT, lambda kc, g: [("aT", kc)])
    checkpoint(9, [("xres", xres)])
    for n_ in ("h3", "aT", "ubuf", "cbuf"):
        A.release(n_)

    norm_T(lambda kc, c0, n: xres[:, kc, c0:c0 + n], lambda kc, gi: ("xres", kc, gi),
           lambda kc, c0, n: xres[:, kc, c0:c0 + n], lambda kc, gi: ("xres", kc, gi),
           KC, QG, CS_GFIN, D, "n8", [0, 1])
    for kq in range(4):
        dma("sync", outT[kq * 512:(kq + 1) * 512, :].rearrange("(k p) t -> p k t", p=128),
            xres[:, kq * 4:(kq + 1) * 4, :],
            [("xres", kc, g) for kc in range(kq * 4, kq * 4 + 4) for g in range(3)], [], ("out", kq),
            is_output=True)
    P.finish("sync")
    P.emit(nc)


_NC_CACHE = {}


def _host_inputs(inp):
    x = np.asarray(inp["x"], dtype=np.float32)
    mem = np.asarray(inp["mem"], dtype=np.float32)
    pos = np.asarray(inp["positions"], dtype=np.int32)

    def col(v, n):
        return np.asarray(v, np.float32).reshape(n, 128).T

    cst = np.zeros((128, NCST), np.float32)
    cst[:, CS_GMIX:CS_GMIX + 16] = col(inp["g_mix_norm"][0], 16)
    cst[:, CS_GCROSS:CS_GCROSS + 16] = col(inp["g_cross_norm"][0], 16)
    cst[:, CS_GMEM:CS_GMEM + 16] = col(inp["g_mem_norm"][0], 16)
    cst[:, CS_GFFN:CS_GFFN + 16] = col(inp["g_ffn_norm"][0], 16)
    cst[:, CS_GFIN:CS_GFIN + 16] = col(inp["g_final"], 16)
    cst[:, CS_GQ:CS_GQ + 4] = col(inp["g_q_norm"][0], 4)
    cst[:, CS_GKV:CS_GKV + 4] = col(inp["g_kv_norm"][0], 4)
    cst[:, CS_GDIFF] = np.asarray(inp["g_diff_sub"][0], np.float32)
    p = np.arange(128)
    inv = (10000.0 ** (-np.arange(0, 64, 2, dtype=np.float32) / np.float32(64))).astype(np.float32)
    cst[:, CS_INV] = inv[p % 32]
    cst[:, CS_SIGN] = np.where((p % 64) < 32, -1.0, 1.0)
    cw = np.asarray(inp["conv_w"][0], np.float32)
    for k in range(3):
        cst[:, CS_CW + k * 88:CS_CW + (k + 1) * 88] = col(cw[k], 88)
    cst[:, CS_CB:CS_CB + 88] = col(inp["conv_b"][0], 88)
    cst[:, CS_EPS1] = EPS
    cst[:, CS_EPS2] = EPS / ((1.0 - LAM_INIT) ** 2)
    lamv = np.concatenate([np.asarray(inp[k][0], np.float32) for k in
                           ("lambda_q1", "lambda_k1", "lambda_q2", "lambda_k2")])[None, :].repeat(128, 0)
    cmat = np.zeros((128, 256), np.float32)
    perm = np.where((p % 64) < 32, p + 32, p - 32)
    cmat[perm, p] = 1.0
    cmat[p, 128 + p] = 1.0
    shared = {
        "cst": cst, "lamv": np.ascontiguousarray(lamv), "cmat": cmat,
        "w_in": np.ascontiguousarray(inp["w_in"][0]), "w_uq": np.ascontiguousarray(inp["w_uq"][0]),
        "w_ukv": np.ascontiguousarray(inp["w_ukv"][0]), "w_o_mla": np.ascontiguousarray(inp["w_o_mla"][0]),
        "w_o_diff": np.ascontiguousarray(inp["w_o_diff"][0]), "w_out": np.ascontiguousarray(inp["w_out"][0]),
        "w_cross_q": np.ascontiguousarray(inp["w_cross_q"][0]),
        "w_cross_kv": np.ascontiguousarray(inp["w_cross_kv"][0]),
        "w_cross_o": np.ascontiguousarray(inp["w_cross_o"][0]), "w_up": np.ascontiguousarray(inp["w_up"][0]),
        "w_down": np.ascontiguousarray(inp["w_down"][0]),
    }
    shared = {k: np.asarray(v, np.float32) for k, v in shared.items()}
    in_maps = []
    for c in range(8):
        b, half = divmod(c, 2)
        q0 = half * 1023
        m = dict(shared)
        m["xkvT"] = np.ascontiguousarray(x[b].T)
        m["xqT"] = np.ascontiguousarray(x[b, q0:q0 + Q].T)
        m["memT"] = np.ascontiguousarray(mem[b].T)
        m["poskv"] = np.ascontiguousarray(pos[b][None, :].repeat(128, 0))
        m["posq"] = np.ascontiguousarray(pos[b, q0:q0 + Q][None, :].repeat(128, 0))
        in_maps.append(m)
    return in_maps


def kernel(**inp):
    if "nc" not in _NC_CACHE:
        _NC_CACHE["nc"] = build_nc()
    nc = _NC_CACHE["nc"]
    in_maps = _host_inputs(inp)
    res = run_bass_kernel_spmd(nc, in_maps, core_ids=list(range(8)))
    out = np.empty((4, 2048, D), np.float32)
    for c in range(8):
        b, half = divmod(c, 2)
        o = res.results[c]["outT"]
        if half == 0:
            out[b, 0:1024, :] = o[:, 0:1024].T
        else:
            out[b, 1024:2048, :] = o[:, 1:1025].T
    return out
```

```python
import math
import os
from contextlib import ExitStack

import numpy as np
import concourse.bass as bass
import concourse.mybir as mybir
from concourse.bass_utils import run_bass_kernel_spmd

F32 = mybir.dt.float32
BF16 = mybir.dt.bfloat16
I32 = mybir.dt.int32
U8 = mybir.dt.uint8
AF = mybir.ActivationFunctionType
ALU = mybir.AluOpType
AX = mybir.AxisListType

ENG_NAMES = ["tensor", "vector", "scalar", "gpsimd", "sync"]
SAME_ENGINE_SYNC = True
_DBG_EMIT = False

D = 2048
KC = 16
T = 2048
Q = 1025
QG = [(0, 342), (342, 342), (684, 341)]
KG = [(i * 512, 512) for i in range(4)]
FFN = 5632
EPS = 1e-6
C_CQ, C_CKV, C_KPE, C_DQ, C_DK, C_DV, C_GA, C_GB = 0, 512, 1024, 1088, 2112, 3136, 4160, 6208
CS_GMIX, CS_GCROSS, CS_GMEM, CS_GFFN, CS_GFIN = 0, 16, 32, 48, 64
CS_GQ, CS_GKV, CS_GDIFF, CS_INV, CS_SIGN = 80, 84, 88, 89, 90
CS_CW, CS_CB = 91, 91 + 264
CS_EPS1, CS_EPS2 = 91 + 264 + 88, 91 + 264 + 89
NCST = 91 + 264 + 90
LAM_INIT = 0.8 - 0.6 * math.exp(-0.3 * 0)


def _bufname(k):
    return k if isinstance(k, str) else k[0]


class Op:
    __slots__ = ("id", "eng", "fn", "deps", "is_dma", "dma_sem", "dma_val", "sig", "cnt")


class Prog:
    def __init__(self):
        self.ops = []
        self.by_eng = {e: [] for e in ENG_NAMES}
        self.lastw = {}
        self.readers = {}
        self.dma_cnt = {}
        self.dma_last = {}
        self.out_dmas = []
        self.pending = {}
        self.bufkeys = {}

    def add(self, eng, fn, reads=(), writes=(), dma_sem=None, is_output=False, extra_deps=()):
        op = Op()
        op.id = len(self.ops)
        op.eng = eng
        op.fn = fn
        deps = set(extra_deps)
        if eng != "tensor":
            writes = list(writes) + [r for r in reads if isinstance(r, tuple) and r[0] == "ps" and r not in writes]
        for r in reads:
            w = self.lastw.get(r)
            if w is not None:
                deps.add(w)
            pd = self.pending.get(_bufname(r))
            if pd:
                deps |= pd
        for k in writes:
            w = self.lastw.get(k)
            if w is not None:
                deps.add(w)
            for rd in self.readers.get(k, ()):
                deps.add(rd)
            pd = self.pending.get(_bufname(k))
            if pd:
                deps |= pd
        op.is_dma = dma_sem is not None
        op.dma_sem = dma_sem
        op.dma_val = 0
        if op.is_dma:
            n = self.dma_cnt.get(dma_sem, 0) + 1
            self.dma_cnt[dma_sem] = n
            op.dma_val = 16 * n
            prev = self.dma_last.get(dma_sem)
            if prev is not None:
                deps.add(prev)
            self.dma_last[dma_sem] = op.id
            if is_output:
                self.out_dmas.append(op.id)
        deps.discard(op.id)
        op.deps = deps
        op.sig = False
        op.cnt = 0
        for k in writes:
            self.lastw[k] = op.id
            self.readers[k] = []
            self.bufkeys.setdefault(_bufname(k), set()).add(k)
        for r in reads:
            if r not in writes:
                self.readers.setdefault(r, []).append(op.id)
            self.bufkeys.setdefault(_bufname(r), set()).add(r)
        self.ops.append(op)
        self.by_eng[eng].append(op)
        return op

    def touch_ops(self, name):
        s = set()
        for k in self.bufkeys.get(name, ()):
            w = self.lastw.pop(k, None)
            if w is not None:
                s.add(w)
            for r in self.readers.pop(k, ()):
                s.add(r)
        s |= self.pending.pop(name, set())
        self.bufkeys.pop(name, None)
        return s

    def finish(self, eng="sync"):
        self.add(eng, None, extra_deps=list(self.out_dmas))

    def emit(self, nc):
        needed = set()
        for op in self.ops:
            agg = {}
            for d in op.deps:
                p = self.ops[d]
                if p.is_dma:
                    key = ("d", p.dma_sem)
                else:
                    if p.fn is None:
                        continue
                    if p.eng == op.eng and p.eng == "tensor" and not op.is_dma:
                        continue
                    if p.eng == op.eng and not op.is_dma and not SAME_ENGINE_SYNC:
                        continue
                    key = ("e", p.eng)
                if key not in agg or agg[key] < d:
                    agg[key] = d
            op.deps = set(agg.values())
            for d in op.deps:
                if not self.ops[d].is_dma:
                    needed.add(d)
        for e in ENG_NAMES:
            c = 0
            for op in self.by_eng[e]:
                if op.is_dma or op.fn is None:
                    continue
                if op.id in needed:
                    c += 1
                    op.cnt = c
                    op.sig = True
        with ExitStack() as st:
            esem = {e: st.enter_context(nc.semaphore("es_" + e)) for e in ENG_NAMES}
            dsem = {}
            for i, k in enumerate(self.dma_cnt):
                dsem[k] = st.enter_context(nc.semaphore("ds_%d" % i))
            block = st.enter_context(nc.Block())
            prog = self

            def run(engh, e):
                waited = {}
                for op in prog.by_eng[e]:
                    for d in sorted(op.deps):
                        p = prog.ops[d]
                        if p.is_dma:
                            key, val, sem = ("d", p.dma_sem), p.dma_val, dsem[p.dma_sem]
                        else:
                            key, val, sem = ("e", p.eng), p.cnt, esem[p.eng]
                        if waited.get(key, 0) >= val:
                            continue
                        engh.wait_ge(sem, val)
                        waited[key] = val
                        if _DBG_EMIT:
                            print("  [%s] op%d wait %s >= %d" % (e, op.id, key, val))
                    if _DBG_EMIT:
                        print("[%s] op%d %s sig=%s cnt=%d dma=%s" % (e, op.id, "none" if op.fn is None else "", op.sig, op.cnt, op.dma_sem if op.is_dma else ""))
                    if op.fn is None:
                        continue
                    ins = op.fn(engh)
                    if op.is_dma:
                        ins.then_inc(dsem[op.dma_sem], 16)
                    elif op.sig:
                        ins.then_inc(esem[e], 1)

            @block.tensor
            def _(eng):
                run(eng, "tensor")

            @block.vector
            def _(eng):
                run(eng, "vector")

            @block.scalar
            def _(eng):
                run(eng, "scalar")

            @block.gpsimd
            def _(eng):
                run(eng, "gpsimd")

            @block.sync
            def _(eng):
                run(eng, "sync")


_DT_SIZE = {F32: 4, BF16: 2, I32: 4, U8: 1}


class Arena:
    def __init__(self, nc, P, base, size):
        self.nc, self.P = nc, P
        self.free = [(base, size)]
        self.live = {}
        self.retired = []
        self.n = 0

    def alloc(self, name, shape, dtype, top=False):
        nbytes = _DT_SIZE[dtype]
        for s in shape[1:]:
            nbytes *= s
        nbytes = (nbytes + 63) // 64 * 64
        order = range(len(self.free) - 1, -1, -1) if top else range(len(self.free))
        for i in order:
            (o, s) = self.free[i]
            if s >= nbytes:
                if s == nbytes:
                    self.free.pop(i)
                elif top:
                    self.free[i] = (o, s - nbytes)
                    o = o + s - nbytes
                else:
                    self.free[i] = (o + nbytes, s - nbytes)
                break
        else:
            raise RuntimeError("SBUF arena full allocating %s (%d B); live=%s free=%s" % (
                name, nbytes, {k: v[1] for k, v in self.live.items()}, self.free))
        self.live[name] = (o, nbytes)
        deps = set()
        keep = []
        for (ro, rs, ops) in self.retired:
            if ro < o + nbytes and o < ro + rs:
                deps |= ops
            keep.append((ro, rs, ops))
        self.retired = keep
        if deps:
            self.P.pending[name] = deps
        self.n += 1
        return self.nc.alloc_sbuf_tensor_at("%s_%d" % (name, self.n), list(shape), dtype, offset=o)

    def release(self, name):
        o, s = self.live.pop(name)
        ops = self.P.touch_ops(name)
        self.retired.append((o, s, ops))
        fl = sorted(self.free + [(o, s)])
        merged = []
        for (a, b) in fl:
            if merged and merged[-1][0] + merged[-1][1] == a:
                merged[-1] = (merged[-1][0], merged[-1][1] + b)
            else:
                merged.append((a, b))
        self.free = merged


class _Stop(Exception):
    pass


def build_nc(stop_after=None):
    nc = bass.Bass("TRN2", target_bir_lowering=False)
    P = Prog()
    try:
        _build_body(nc, P, stop_after)
    except _Stop:
        pass
    return nc


def _build_body(nc, P, stop_after):
    def checkpoint(k, items):
        if stop_after != k:
            return
        last = [ops[-1].id for e, ops in P.by_eng.items() if ops]
        for (name, t) in items:
            d = nc.dram_tensor("dbg_" + name, list(t.shape), t.dtype, kind="ExternalOutput").ap()
            idx = tuple(slice(None) for _ in t.shape)
            P.add("sync", lambda e, d=d, t=t, idx=idx: e.dma_start(out=d[idx], in_=t[idx]),
                  dma_sem=("dbg", name), is_output=True, extra_deps=last)
        P.finish("sync")
        P.emit(nc)
        raise _Stop()


    def din(name, shape, dt=F32):
        return nc.dram_tensor(name, list(shape), dt, kind="ExternalInput").ap()

    xkvT = din("xkvT", [D, T])
    xqT = din("xqT", [D, Q])
    memT = din("memT", [D, 256])
    poskv_d = din("poskv", [128, T], I32)
    posq_d = din("posq", [128, Q], I32)
    cst_d = din("cst", [128, NCST])
    lamv_d = din("lamv", [128, 256])
    cmat_d = din("cmat", [128, 256])
    w_in = din("w_in", [D, 8256])
    w_uq = din("w_uq", [512, 1536])
    w_ukv = din("w_ukv", [512, 2048])
    w_o_mla = din("w_o_mla", [1024, D])
    w_o_diff = din("w_o_diff", [1024, D])
    w_out = din("w_out", [D, D])
    w_cq = din("w_cross_q", [D, 512])
    w_ckv = din("w_cross_kv", [D, 1024])
    w_co = din("w_cross_o", [512, D])
    w_up = din("w_up", [D, 2 * FFN])
    w_down = din("w_down", [FFN, D])
    outT = nc.dram_tensor("outT", [D, Q], F32, kind="ExternalOutput").ap()
    dbg_list = []

    base = (nc.sbuf_base + 63) // 64 * 64
    asize = (nc.sbuf_top - base) // 64 * 64
    nc.alloc_sbuf_tensor("arena", [128, asize], U8)
    A = Arena(nc, P, base, asize)
    ps = nc.alloc_psum_tensor("ps", [128, 8, 512], F32)

    def PSK(b):
        return ("ps", b)

    def act(out, in_, func, reads, writes, bias=None, scale=None):
        kw = {}
        if bias is not None:
            kw["bias"] = bias
        if scale is not None:
            kw["scale"] = scale
        P.add("scalar", lambda e: e.activation(out=out, in_=in_, func=func, **kw), reads=reads, writes=writes)

    def tt(out, in0, in1, op, reads, writes):
        P.add("vector", lambda e: e.tensor_tensor(out=out, in0=in0, in1=in1, op=op), reads=reads, writes=writes)

    def ts(out, in0, s1, s2, op0, op1, reads, writes):
        if op1 is None:
            P.add("vector", lambda e: e.tensor_scalar(out=out, in0=in0, scalar1=s1, scalar2=None, op0=op0),
                  reads=reads, writes=writes)
        else:
            P.add("vector", lambda e: e.tensor_scalar(out=out, in0=in0, scalar1=s1, scalar2=s2, op0=op0, op1=op1),
                  reads=reads, writes=writes)

    def stt(out, in0, scalar, in1, op0, op1, reads, writes):
        P.add("vector", lambda e: e.scalar_tensor_tensor(out=out, in0=in0, scalar=scalar, in1=in1, op0=op0, op1=op1),
              reads=reads, writes=writes)

    def vcopy(out, in_, reads, writes):
        P.add("vector", lambda e: e.tensor_copy(out=out, in_=in_), reads=reads, writes=writes)

    def recip(out, in_, reads, writes):
        P.add("vector", lambda e: e.reciprocal(out=out, in_=in_), reads=reads, writes=writes)

    def mm(out, lhsT, rhs, start, stop, reads, bank):
        P.add("tensor", lambda e: e.matmul(out, lhsT=lhsT, rhs=rhs, start=start, stop=stop),
              reads=reads, writes=[PSK(bank)])

    def dma(eng, out, in_, reads, writes, sem, is_output=False):
        P.add(eng, lambda e: e.dma_start(out=out, in_=in_), reads=reads, writes=writes, dma_sem=sem,
              is_output=is_output)

    cst = A.alloc("cst", [128, NCST], F32)
    cmat = A.alloc("cmat", [128, 256], BF16)
    ones = A.alloc("ones", [128, 128], BF16)
    lamc = A.alloc("lamc", [128, 4], F32)
    dma("sync", cst[:], cst_d[:, :], [], ["cst"], "cst")
    dma("gpsimd", cmat[:], cmat_d[:, :], [], ["cmat"], "cmat")
    P.add("vector", lambda e: e.memset(ones[:], 1.0), writes=["ones"])
    Rm = cmat[:, 0:128]

    def ccol(c0, n=1):
        return cst[:, c0:c0 + n]

    NSLOT = 3
    SLOT_ELEMS = 4096
    wstate = {"i": 0, "wsl": None, "ns": NSLOT}

    def wload(src, nk, ncols):
        assert nk * ncols <= SLOT_ELEMS, (nk, ncols)
        s = wstate["i"] % wstate["ns"]
        wstate["i"] += 1
        wsl = wstate["wsl"]
        view = wsl[:, s, 0:nk * ncols].rearrange("p (k n) -> p k n", k=nk)
        dma("gpsimd", view, src.rearrange("(k p) n -> p k n", p=128), [], [("wsl", s)], ("wsl", s))
        return ("wsl", s), view

    TWO_PI = 2.0 * math.pi
    HI = 6.28125
    LO = TWO_PI - HI
    PI_S = 3.141592

    def rope_tables(pos_d, N, cosn, sinn):
        cos_t = A.alloc(cosn, [128, N], F32)
        sin_t = A.alloc(sinn, [128, N], F32)
        pi_ = A.alloc("rt_pi", [128, N], I32)
        r = A.alloc("rt_r", [128, N], F32)
        m = A.alloc("rt_m", [128, N], F32)
        dma("sync", pi_[:], pos_d[:, :], [], ["rt_pi"], "rt_pi")
        vcopy(r[:], pi_[:], ["rt_pi"], ["rt_r"])
        ts(r[:], r[:], ccol(CS_INV), None, ALU.mult, None, ["rt_r", "cst"], ["rt_r"])
        ts(pi_[:], r[:], 1.0 / TWO_PI, None, ALU.mult, None, ["rt_r"], ["rt_pi"])
        vcopy(m[:], pi_[:], ["rt_pi"], ["rt_m"])
        stt(r[:], m[:], -HI, r[:], ALU.mult, ALU.add, ["rt_m", "rt_r"], ["rt_r"])
        stt(r[:], m[:], -LO, r[:], ALU.mult, ALU.add, ["rt_m", "rt_r"], ["rt_r"])

        def wrap(v, vn):
            ts(m[:], v[:], math.pi, -TWO_PI, ALU.is_gt, ALU.mult, [vn], ["rt_m"])
            tt(v[:], v[:], m[:], ALU.add, [vn, "rt_m"], [vn])
            ts(m[:], v[:], -math.pi, TWO_PI, ALU.is_lt, ALU.mult, [vn], ["rt_m"])
            tt(v[:], v[:], m[:], ALU.add, [vn, "rt_m"], [vn])
            ts(v[:], v[:], -PI_S, PI_S, ALU.max, ALU.min, [vn], [vn])

        wrap(r, "rt_r")
        act(sin_t[:], r[:], AF.Sin, ["rt_r", "cst"], [sinn], scale=ccol(CS_SIGN))
        ts(r[:], r[:], math.pi / 2, None, ALU.add, None, ["rt_r"], ["rt_r"])
        wrap(r, "rt_r")
        act(cos_t[:], r[:], AF.Sin, ["rt_r"], [cosn])
        for n_ in ("rt_pi", "rt_r", "rt_m"):
            A.release(n_)
        return cos_t, sin_t

    def norm_T(src, srck, dst, dstk, nk, groups, gcol0, dn, sqn, bank_list, extra=1.0, per_group_keys=True):
        N = groups[-1][0] + groups[-1][1]
        rstd = A.alloc(sqn + "_rstd", [128, N], F32)
        sq = A.alloc(sqn + "_sq", [128, nk, 512], BF16)
        for gi, (c0, n) in enumerate(groups):
            bank = bank_list[gi % len(bank_list)]
            for kc in range(nk):
                act(sq[:, kc, 0:n], src(kc, c0, n), AF.Square, [srck(kc, gi)], [(sqn + "_sq", kc)])
            for kc in range(nk):
                mm(ps[:, bank, 0:n], ones[:, :], sq[:, kc, 0:n], kc == 0, kc == nk - 1,
                   ["ones", (sqn + "_sq", kc)], bank)
            act(rstd[:, c0:c0 + n], ps[:, bank, 0:n], AF.Sqrt, [PSK(bank), "cst"], [(sqn + "_rstd", gi)],
                bias=ccol(CS_EPS1 if extra == 1.0 else CS_EPS2), scale=float(1.0 / (dn * extra * extra)))
            recip(rstd[:, c0:c0 + n], rstd[:, c0:c0 + n], [(sqn + "_rstd", gi)], [(sqn + "_rstd", gi)])
            for kc in range(nk):
                stt(dst(kc, c0, n), src(kc, c0, n), ccol(gcol0 + kc), rstd[:, c0:c0 + n], ALU.mult, ALU.mult,
                    [srck(kc, gi), "cst", (sqn + "_rstd", gi)], [dstk(kc, gi)])
        A.release(sqn + "_rstd")
        A.release(sqn + "_sq")

    rstate = {"i": 0, "t": None}

    rope_pend = []

    def rope_flush():
        while rope_pend:
            rope_pend.pop(0)()

    def rope(bank, bank2, np_, n, cos_ap, sin_ap, tabkeys, out, outk):
        b = rstate["i"] % 2
        rstate["i"] += 1
        ropet = rstate["t"]
        qa = ropet[:, b, 512:768].bitcast(BF16)[0:np_, 0:n]
        t_ = ropet[0:np_, b, 0:n]
        u_ = ropet[0:np_, b, 768:768 + n]
        act(qa, ps[0:np_, bank, 0:n], AF.Copy, [PSK(bank)], [("ropet", b, 0)])
        rope_flush()

        def part_b():
            mm(ps[0:np_, bank2, 0:n], cmat[0:np_, 0:np_], qa, True, True, ["cmat", ("ropet", b, 0)], bank2)
            tt(t_, ps[0:np_, bank, 0:n], cos_ap, ALU.mult, [PSK(bank), ("ropet", b, 0)] + tabkeys, [("ropet", b, 1)])
            tt(u_, ps[0:np_, bank2, 0:n], sin_ap, ALU.mult, [PSK(bank2)] + tabkeys, [("ropet", b, 2)])
            tt(out, u_, t_, ALU.add, [("ropet", b, 1), ("ropet", b, 2)], [outk])

        rope_pend.append(part_b)

    checkpoint(-1, [("cst", cst), ("cmat", cmat), ("ones", ones)])
    bankc = {"i": 0}

    def nbank(lst):
        b = lst[bankc["i"] % len(lst)]
        bankc["i"] += 1
        return b

    hkv = A.alloc("hkv", [128, KC, T], BF16)
    xblk = A.alloc("xblk", [128, 2, KC, 512], F32)
    ckv32 = A.alloc("ckv32", [128, 4, T], F32)
    wstate["wsl"] = A.alloc("wsl", [128, NSLOT, SLOT_ELEMS], BF16, top=True)
    ckvw = [wload(w_in[:, C_CKV + blk2 * 256:C_CKV + (blk2 + 1) * 256], KC, 256) for blk2 in range(2)]
    def norm_blk(blk):
        xb = blk % 2
        dma("sync", xblk[:, xb], xkvT[:, blk * 512:(blk + 1) * 512].rearrange("(k p) t -> p k t", p=128),
            [], [("xblk", xb)], ("xblk", xb))
        norm_T(lambda kc, c0, n, xb=xb: xblk[:, xb, kc, c0:c0 + n], lambda kc, gi, xb=xb: ("xblk", xb),
               lambda kc, c0, n, blk=blk: hkv[:, kc, blk * 512 + c0:blk * 512 + c0 + n],
               lambda kc, gi, blk=blk: ("hkv", blk, kc),
               KC, [(0, 512)], CS_GMIX, D, "n1", [blk % 2])

    def proj_blk(blk):
        g, (c0, n) = blk, KG[blk]
        for cc in range(4):
            wk, wv = ckvw[cc // 2]
            cl = cc % 2
            bank = nbank([2, 3, 4, 5])
            for kc in range(KC):
                mm(ps[:, bank, 0:n], wv[:, kc, cl * 128:(cl + 1) * 128], hkv[:, kc, c0:c0 + n],
                   kc == 0, kc == KC - 1, [wk, ("hkv", g, kc)], bank)
            act(ckv32[:, cc, c0:c0 + n], ps[:, bank, 0:n], AF.Copy, [PSK(bank)], [("ckv32", cc, g)])

    norm_blk(0)
    for blk in range(4):
        if blk + 1 < 4:
            norm_blk(blk + 1)
        proj_blk(blk)
    checkpoint(1, [("hkv", hkv)])
    A.release("xblk")

    ckvn = A.alloc("ckvn", [128, 4, T], BF16)
    kpe = A.alloc("kpe", [128, T], BF16)
    P.add("vector", lambda e: e.memset(kpe[64:128, :], 0.0), writes=[("kpe", "z")])
    norm_T(lambda kc, c0, n: ckv32[:, kc, c0:c0 + n], lambda kc, gi: ("ckv32", kc, gi),
           lambda kc, c0, n: ckvn[:, kc, c0:c0 + n], lambda kc, gi: ("ckvn", kc, gi),
           4, KG, CS_GKV, 512, "n2", [0, 1])
    checkpoint(21, [("ckvn", ckvn)])
    A.release("ckv32")

    dv = A.alloc("dv", [128, 16, 1024], BF16)
    for cb in range(4):
        wk, wv = wload(w_in[:, C_DV + cb * 256:C_DV + (cb + 1) * 256], KC, 256)
        for tp in range(8):
            bank = nbank([2, 3, 4, 5])
            for sub in range(2):
                tc = tp * 2 + sub
                for kc in range(KC):
                    mm(ps[:, bank, sub * 256:(sub + 1) * 256], hkv[:, kc, tc * 128:(tc + 1) * 128], wv[:, kc, :],
                       kc == 0, kc == KC - 1, [wk, ("hkv", tc // 4, kc)], bank)
            act(dv[:, tp * 2:tp * 2 + 2, cb * 256:(cb + 1) * 256],
                ps[:, bank, :].rearrange("p (a b) -> p a b", a=2), AF.Copy, [PSK(bank)], [("dv", tp, cb)])
    cos_kv, sin_kv = rope_tables(poskv_d, T, "cos_kv", "sin_kv")
    lamv = A.alloc("lamv", [128, 256], F32)
    lamt = A.alloc("lamt", [128, 128], F32)
    dma("sync", lamv[:], lamv_d[:, :], [], ["lamv"], "lamv")
    tt(lamt[:, 0:64], lamv[:, 0:64], lamv[:, 64:128], ALU.mult, ["lamv"], [("lamt", 0)])
    tt(lamt[:, 64:128], lamv[:, 128:192], lamv[:, 192:256], ALU.mult, ["lamv"], [("lamt", 1)])
    P.add("vector", lambda e: e.reduce_sum(out=lamc[:, 2:3], in_=lamt[:, 0:64], axis=AX.X),
          reads=[("lamt", 0)], writes=[("lamc", 2)])
    P.add("vector", lambda e: e.reduce_sum(out=lamc[:, 3:4], in_=lamt[:, 64:128], axis=AX.X),
          reads=[("lamt", 1)], writes=[("lamc", 3)])
    act(lamc[:, 2:4], lamc[:, 2:4], AF.Exp, [("lamc", 2), ("lamc", 3)], [("lamc", 2), ("lamc", 3)])
    tt(lamc[:, 0:1], lamc[:, 2:3], lamc[:, 3:4], ALU.subtract, [("lamc", 2), ("lamc", 3)], [("lamc", 0)])
    ts(lamc[:, 0:1], lamc[:, 0:1], float(LAM_INIT), None, ALU.add, None, [("lamc", 0)], [("lamc", 0)])
    ts(lamc[:, 1:2], lamc[:, 0:1], -1.0, None, ALU.mult, None, [("lamc", 0)], [("lamc", 1)])
    A.release("lamv")
    A.release("lamt")
    rstate["t"] = A.alloc("ropet", [128, 2, 1280], F32)
    dk = A.alloc("dk", [128, 8, T], BF16)

    wk, wv = wload(w_in[:, C_KPE:C_KPE + 64], KC, 64)
    for g in [int(c) for c in os.environ.get("KPEG", "0123")]:
        c0, n = KG[g]
        bank = nbank([int(c) for c in os.environ.get("KB", "2345")])
        KCX = int(os.environ.get("KCX", "16"))
        for kc in range(KCX):
            mm(ps[0:64, bank, 0:n], wv[:, kc, 0:64], hkv[:, kc, c0:c0 + n], kc == 0, kc == KCX - 1,
               [wk, ("hkv", g, kc)], bank)
        rope(bank, int(os.environ["B2X"][rstate["i"] % len(os.environ["B2X"])]) if os.environ.get("B2X") else 6 + g % 2, 64, n,
             cos_kv[0:64, c0:c0 + n], sin_kv[0:64, c0:c0 + n], ["cos_kv", "sin_kv"],
             kpe[0:64, c0:c0 + n], ("kpe", g))

    checkpoint(22, [("ckvn", ckvn), ("kpe", kpe)])
    rope_flush()
    for blk2 in range(4):
        wk, wv = wload(w_in[:, C_DK + blk2 * 256:C_DK + (blk2 + 1) * 256], KC, 256)
        for cl in range(2):
            h = blk2 * 2 + cl
            for g, (c0, n) in enumerate(KG):
                bank = nbank([2, 3, 4, 5])
                for kc in range(KC):
                    mm(ps[:, bank, 0:n], wv[:, kc, cl * 128:(cl + 1) * 128], hkv[:, kc, c0:c0 + n],
                       kc == 0, kc == KC - 1, [wk, ("hkv", g, kc)], bank)
                rope(bank, 6 + g % 2, 128, n, cos_kv[:, c0:c0 + n], sin_kv[:, c0:c0 + n], ["cos_kv", "sin_kv"],
                     dk[:, h, c0:c0 + n], ("dk", h, g))

    rope_flush()
    checkpoint(2, [("ckvn", ckvn), ("kpe", kpe), ("dk", dk), ("dv", dv)])
    rope_flush()
    A.release("ropet")
    A.release("wsl")
    cos_q = A.alloc("cos_q", [128, Q], F32, top=True)
    sin_q = A.alloc("sin_q", [128, Q], F32, top=True)
    vcopy(cos_q[:], cos_kv[:, 0:Q], ["cos_kv"], ["cos_q"])
    vcopy(sin_q[:], sin_kv[:, 0:Q], ["sin_kv"], ["sin_q"])
    A.release("cos_kv")
    A.release("sin_kv")
    hq_lo = A.alloc("hq", [128, 8, Q], BF16)
    hq_hi = A.alloc("hqh", [128, 8, Q], BF16)

    def hqv(kc, c0, n):
        return (hq_lo if kc < 8 else hq_hi)[:, kc % 8, c0:c0 + n]

    def hqk(g, kc):
        return ("hq" if kc < 8 else "hqh", g, kc)

    for gi, (c0, n) in enumerate(QG):
        blks = sorted(set([c0 // 512, (c0 + n - 1) // 512]))
        for half, t_ in enumerate((hq_lo, hq_hi)):
            vcopy(t_[:, :, c0:c0 + n], hkv[:, half * 8:half * 8 + 8, c0:c0 + n],
                  [("hkv", b_, kc) for b_ in blks for kc in range(half * 8, half * 8 + 8)],
                  [hqk(gi, kc) for kc in range(half * 8, half * 8 + 8)])
    A.release("hkv")
    wstate["wsl"] = A.alloc("wsl", [128, NSLOT, SLOT_ELEMS], BF16)
    rstate["t"] = A.alloc("ropet", [128, 2, 1280], F32)

    def hq_keys(g):
        return [("hq", g, kc) for kc in range(KC)]

    O_BANK, SUM_BANK = 6, 7
    LOOK = 2
    ast = {"i": 0, "c": 0}
    deferred = []

    def attn_alloc():
        return (A.alloc("pT", [128, 3, 2, 342], BF16), A.alloc("osb", [128, 2, 342], F32),
                A.alloc("ssb", [128, 2, 342], F32))

    def attn_core(bufs, nkc, n, qk_list, v_of, scale, after):
        pT, osb, ssb = bufs
        npairs = nkc // 2
        pend = []

        def s_stage(p):
            sp = p % 3
            for j in range(2):
                kc = 2 * p + j
                sb = 2 * sp + j
                for i, (lo, rhs, ko) in enumerate(qk_list):
                    mm(ps[:, sb, 0:n], lo(kc), rhs[0], i == 0, i == len(qk_list) - 1, ko(kc) + rhs[1], sb)
            act(pT[:, sp, :, 0:n], ps[:, 2 * sp:2 * sp + 2, 0:n], AF.Exp, [PSK(2 * sp), PSK(2 * sp + 1)],
                [("pT", sp)], scale=float(scale))
            return sp

        def pv_stage(p, sp):
            for j in range(2):
                kc = 2 * p + j
                vl, vk = v_of(kc)
                mm(ps[:, O_BANK, 0:n], vl, pT[:, sp, j, 0:n], kc == 0, kc == nkc - 1, vk + [("pT", sp)], O_BANK)
                mm(ps[:, SUM_BANK, 0:n], ones[:, :], pT[:, sp, j, 0:n], kc == 0, kc == nkc - 1,
                   ["ones", ("pT", sp)], SUM_BANK)

        for p in range(npairs):
            pend.append((p, s_stage(p)))
            if p == min(4, npairs - 1):
                while deferred:
                    deferred.pop(0)()
            if len(pend) > LOOK:
                pv_stage(*pend.pop(0))
        while pend:
            pv_stage(*pend.pop(0))
        cb = ast["c"] % 2
        ast["c"] += 1
        vcopy(osb[:, cb, 0:n], ps[:, O_BANK, 0:n], [PSK(O_BANK)], [("osb", cb)])
        vcopy(ssb[:, cb, 0:n], ps[:, SUM_BANK, 0:n], [PSK(SUM_BANK)], [("ssb", cb)])
        recip(ssb[:, cb, 0:n], ssb[:, cb, 0:n], [("ssb", cb)], [("ssb", cb)])
        after(osb[:, cb, 0:n], ("osb", cb), ssb[:, cb, 0:n], ("ssb", cb))

    def attn_stream(bufs, calls):
        pT, osb, ssb = bufs
        jobs = [(ci, p) for ci, c in enumerate(calls) for p in range(c["nkc"] // 2)]

        def s_stage(ji):
            ci, p = jobs[ji]
            c = calls[ci]
            n = c["n"]
            if p == 0 and c.get("pre"):
                c["pre"]()
            sp = ji % 3
            for j in range(2):
                kc = 2 * p + j
                sb = 2 * sp + j
                for i, (lo, rhs, ko) in enumerate(c["qk"]):
                    mm(ps[:, sb, 0:n], lo(kc), rhs[0], i == 0, i == len(c["qk"]) - 1, ko(kc) + rhs[1], sb)
            act(pT[:, sp, :, 0:n], ps[:, 2 * sp:2 * sp + 2, 0:n], AF.Exp, [PSK(2 * sp), PSK(2 * sp + 1)],
                [("pT", sp)], scale=float(c["scale"]))

        def pv_stage(ji):
            ci, p = jobs[ji]
            c = calls[ci]
            n, nkc = c["n"], c["nkc"]
            sp = ji % 3
            for j in range(2):
                kc = 2 * p + j
                vl, vk = c["v_of"](kc)
                mm(ps[:, O_BANK, 0:n], vl, pT[:, sp, j, 0:n], kc == 0, kc == nkc - 1, vk + [("pT", sp)], O_BANK)
                mm(ps[:, SUM_BANK, 0:n], ones[:, :], pT[:, sp, j, 0:n], kc == 0, kc == nkc - 1,
                   ["ones", ("pT", sp)], SUM_BANK)
            if p == nkc // 2 - 1:
                cb = ast["c"] % 2
                ast["c"] += 1
                ast["job"] = ji
                vcopy(osb[:, cb, 0:n], ps[:, O_BANK, 0:n], [PSK(O_BANK)], [("osb", cb)])
                vcopy(ssb[:, cb, 0:n], ps[:, SUM_BANK, 0:n], [PSK(SUM_BANK)], [("ssb", cb)])
                recip(ssb[:, cb, 0:n], ssb[:, cb, 0:n], [("ssb", cb)], [("ssb", cb)])
                c["after"](osb[:, cb, 0:n], ("osb", cb), ssb[:, cb, 0:n], ("ssb", cb))

        pend = []
        for ji in range(len(jobs)):
            pend.append(ji)
            s_stage(ji)
            if len(pend) > LOOK:
                jx = pend.pop(0)
                pv_stage(jx)
                while deferred and ji - deferred[0][1] >= 9:
                    deferred.pop(0)[0](2 * (jx % 3))
        while pend:
            pv_stage(pend.pop(0))

    dq = A.alloc("dq", [128, 8, Q], BF16)
    for blk2 in range(4):
        wk, wv = wload(w_in[:, C_DQ + blk2 * 256:C_DQ + (blk2 + 1) * 256], KC, 256)
        for cl in range(2):
            h = blk2 * 2 + cl
            for g, (c0, n) in enumerate(QG):
                bank = nbank([5, 6])
                for kc in range(KC):
                    mm(ps[:, bank, 0:n], wv[:, kc, cl * 128:(cl + 1) * 128], hqv(kc, c0, n),
                       kc == 0, kc == KC - 1, [wk, hqk(g, kc)], bank)
                rope(bank, 7, 128, n, cos_q[:, c0:c0 + n], sin_q[:, c0:c0 + n], ["cos_q", "sin_q"],
                     dq[:, h, c0:c0 + n], ("dq", h, g))

    rope_flush()
    A.release("ropet")
    A.release("wsl")
    ob = A.alloc("ob", [128, 8, Q], BF16)
    dtmp = A.alloc("dtmp", [128, 2, 3, 342], F32)
    dsq = A.alloc("dsq", [128, 2, 342], BF16)
    abufs = attn_alloc()
    dqm = A.alloc("dqm", [128, 2, 2, Q], BF16)
    for hb_ in range(2):
        P.add("vector", lambda e, hb_=hb_: e.memset(dqm[64:128, hb_, 0, :], 0.0), writes=[("dqm", hb_, "z0")])
        P.add("vector", lambda e, hb_=hb_: e.memset(dqm[0:64, hb_, 1, :], 0.0), writes=[("dqm", hb_, "z1")])
    dpar = {"i": 0}
    dcalls = []
    for h in range(8):
        hb_ = h % 2

        def pre(h=h, hb_=hb_):
            vcopy(dqm[0:64, hb_, 0, :], dq[0:64, h, :], [("dq", h, g_) for g_ in range(3)], [("dqm", hb_, 0)])
            vcopy(dqm[64:128, hb_, 1, :], dq[64:128, h, :], [("dq", h, g_) for g_ in range(3)], [("dqm", hb_, 1)])

        for g, (c0, n) in enumerate(QG):
            pp = dpar["i"] % 2
            dpar["i"] += 1
            for c in range(2):
                def after(O, ok, rs, rsk, c=c, h=h, g=g, c0=c0, n=n, pp=pp):
                    if c == 0:
                        tt(dtmp[:, pp, 0, 0:n], O, rs, ALU.mult, [ok, rsk], [("dtmp", pp, 0)])
                    else:
                        tt(dtmp[:, pp, 1, 0:n], O, rs, ALU.mult, [ok, rsk], [("dtmp", pp, 1)])
                        stt(dtmp[:, pp, 1, 0:n], dtmp[:, pp, 1, 0:n], lamc[:, 1:2], dtmp[:, pp, 0, 0:n], ALU.mult,
                            ALU.add, [("dtmp", pp, 0), ("dtmp", pp, 1), ("lamc", 1)], [("dtmp", pp, 1)])
                        tt(dsq[:, pp, 0:n], dtmp[:, pp, 1, 0:n], dtmp[:, pp, 1, 0:n], ALU.mult, [("dtmp", pp, 1)],
                           [("dsq", pp)])

                        def fin(bank, h=h, g=g, c0=c0, n=n, pp=pp):
                            mm(ps[:, bank, 0:n], ones[:, :], dsq[:, pp, 0:n], True, True, ["ones", ("dsq", pp)], bank)
                            ex = 1.0 - LAM_INIT
                            act(dtmp[:, pp, 2, 0:n], ps[:, bank, 0:n], AF.Ln, [PSK(bank), "cst"], [("dtmp", pp, 2)],
                                bias=ccol(CS_EPS2), scale=float(1.0 / (128.0 * ex * ex)))
                            act(dtmp[:, pp, 2, 0:n], dtmp[:, pp, 2, 0:n], AF.Exp, [("dtmp", pp, 2)], [("dtmp", pp, 2)],
                                scale=-0.5)
                            stt(ob[:, h, c0:c0 + n], dtmp[:, pp, 1, 0:n], ccol(CS_GDIFF), dtmp[:, pp, 2, 0:n],
                                ALU.mult, ALU.mult, [("dtmp", pp, 1), ("dtmp", pp, 2), "cst"], [("ob", h, g)])

                        deferred.append((fin, ast["job"]))

                dcalls.append(dict(
                    nkc=16, n=n, scale=64.0 ** -0.5, after=after, pre=(pre if (g == 0 and c == 0) else None),
                    qk=[(lambda kc, h=h: dk[:, h, kc * 128:(kc + 1) * 128],
                         (dqm[:, hb_, c, c0:c0 + n], [("dqm", hb_, c), ("dqm", hb_, "z0"), ("dqm", hb_, "z1")]),
                         lambda kc, h=h: [("dk", h, kc // 4)])],
                    v_of=lambda kc, h=h: (dv[:, kc, h * 128:(h + 1) * 128], [("dv", kc // 2, h // 2)])))
    attn_stream(abufs, dcalls)
    while deferred:
        deferred.pop(0)[0](0)
    checkpoint(4, [("ob", ob)])
    for n_ in ("dq", "dqm", "dk", "dv", "dtmp", "dsq", "pT", "osb", "ssb"):
        A.release(n_)
    wstate["wsl"] = A.alloc("wsl", [128, NSLOT, SLOT_ELEMS], BF16)
    rstate["t"] = A.alloc("ropet", [128, 2, 1280], F32)

    cq32 = A.alloc("cq32", [128, 4, Q], F32)
    for blk2 in range(2):
        wk, wv = wload(w_in[:, C_CQ + blk2 * 256:C_CQ + (blk2 + 1) * 256], KC, 256)
        for cl in range(2):
            cc = blk2 * 2 + cl
            for g, (c0, n) in enumerate(QG):
                bank = nbank([0, 1, 2, 3, 4, 5])
                for kc in range(KC):
                    mm(ps[:, bank, 0:n], wv[:, kc, cl * 128:(cl + 1) * 128], hqv(kc, c0, n),
                       kc == 0, kc == KC - 1, [wk, hqk(g, kc)], bank)
                act(cq32[:, cc, c0:c0 + n], ps[:, bank, 0:n], AF.Copy, [PSK(bank)], [("cq32", cc, g)])
    cqn = A.alloc("cqn", [128, 4, Q], BF16)
    norm_T(lambda kc, c0, n: cq32[:, kc, c0:c0 + n], lambda kc, gi: ("cq32", kc, gi),
           lambda kc, c0, n: cqn[:, kc, c0:c0 + n], lambda kc, gi: ("cqn", kc, gi),
           4, QG, CS_GQ, 512, "n4", [5, 6])
    A.release("cq32")

    oa = A.alloc("oa", [128, 8, Q], BF16)
    qn = A.alloc("qn", [128, 2, Q], BF16)
    qp = A.alloc("qp", [128, 2, Q], BF16)
    P.add("vector", lambda e: e.memset(qp[64:128, :, :], 0.0), writes=[("qp", "z")])
    kn = A.alloc("kn", [128, 2, T], BF16)
    vm = A.alloc("vm", [128, 2, 16, 128], BF16)

    abufs = attn_alloc()

    def mla_prep(h):
        hb = h % 2
        wk, wv = wload(w_uq[:, h * 192:(h + 1) * 192], 4, 192)
        for g, (c0, n) in enumerate(QG):
            bank = nbank([0, 1, 2, 3, 4])
            for kc in range(4):
                mm(ps[:, bank, 0:n], wv[:, kc, 0:128], cqn[:, kc, c0:c0 + n], kc == 0, kc == 3,
                   [wk, ("cqn", kc, g)], bank)
            act(qn[:, hb, c0:c0 + n], ps[:, bank, 0:n], AF.Copy, [PSK(bank)], [("qn", hb, g)])
            bank = nbank([0, 1, 2, 3, 4])
            for kc in range(4):
                mm(ps[0:64, bank, 0:n], wv[:, kc, 128:192], cqn[:, kc, c0:c0 + n], kc == 0, kc == 3,
                   [wk, ("cqn", kc, g)], bank)
            rope(bank, 5, 64, n, cos_q[0:64, c0:c0 + n], sin_q[0:64, c0:c0 + n], ["cos_q", "sin_q"],
                 qp[0:64, hb, c0:c0 + n], ("qp", hb, g))
        rope_flush()
        wk, wv = wload(w_ukv[:, h * 256:(h + 1) * 256], 4, 256)
        for g, (c0, n) in enumerate(KG):
            bank = nbank([0, 1, 2, 3, 4])
            for kc in range(4):
                mm(ps[:, bank, 0:n], wv[:, kc, 0:128], ckvn[:, kc, c0:c0 + n], kc == 0, kc == 3,
                   [wk, ("ckvn", kc, g)], bank)
            act(kn[:, hb, c0:c0 + n], ps[:, bank, 0:n], AF.Copy, [PSK(bank)], [("kn", hb, g)])
        for tq in range(4):
            bank = nbank([0, 1, 2, 3, 4])
            for sub in range(4):
                tc = tq * 4 + sub
                for kc in range(4):
                    mm(ps[:, bank, sub * 128:(sub + 1) * 128], ckvn[:, kc, tc * 128:(tc + 1) * 128],
                       wv[:, kc, 128:256], kc == 0, kc == 3, [wk, ("ckvn", kc, tq)], bank)
            act(vm[:, hb, tq * 4:(tq + 1) * 4, :], ps[:, bank, :].rearrange("p (a b) -> p a b", a=4), AF.Copy,
                [PSK(bank)], [("vm", hb, tq)])

    mla_prep(0)
    for h in range(8):
        hb = h % 2
        if h + 1 < 8:
            mla_prep(h + 1)
        mcalls = []
        for g, (c0, n) in enumerate(QG):
            def after(O, ok, rs, rsk, h=h, g=g, c0=c0, n=n):
                tt(oa[:, h, c0:c0 + n], O, rs, ALU.mult, [ok, rsk], [("oa", h, g)])

            mcalls.append(dict(
                nkc=16, n=n, scale=192.0 ** -0.5, after=after,
                qk=[(lambda kc, hb=hb: kn[:, hb, kc * 128:(kc + 1) * 128],
                     (qn[:, hb, c0:c0 + n], [("qn", hb, g)]),
                     lambda kc, hb=hb: [("kn", hb, kc // 4)]),
                    (lambda kc: kpe[:, kc * 128:(kc + 1) * 128],
                     (qp[:, hb, c0:c0 + n], [("qp", hb, g), ("qp", "z")]),
                     lambda kc: [("kpe", kc // 4), ("kpe", "z")])],
                v_of=lambda kc, hb=hb: (vm[:, hb, kc, :], [("vm", hb, kc // 4)])))
        attn_stream(abufs, mcalls)
    checkpoint(5, [("oa", oa), ("cqn", cqn)])
    for n_ in ("cqn", "qn", "qp", "kn", "vm", "ckvn", "kpe", "cos_q", "sin_q", "pT", "osb", "ssb"):
        A.release(n_)

    A.release("wsl")
    wstate["ns"] = 4
    wstate["wsl"] = A.alloc("wsl", [128, 4, SLOT_ELEMS], BF16, top=True)
    mt = A.alloc("mt", [128, KC, Q], BF16, top=True)
    sg = A.alloc("sg", [128, 2, 342], F32)
    mtmp = A.alloc("mtmp", [128, 2, 342], F32)
    t1 = A.alloc("t1", [128, 2, Q], F32)
    ALLB = [0, 1, 2, 3, 4, 5, 6, 7]
    ust = {"i": 0}
    for jb in range(8):
        for br in range(2):
            wo_d, gcol, src_t, srcn = ((w_o_mla, C_GA, oa, "oa"), (w_o_diff, C_GB, ob, "ob"))[br]
            wko, wvo = wload(wo_d[:, jb * 256:(jb + 1) * 256], 8, 256)
            wkg, wvg = wload(w_in[:, gcol + jb * 256:gcol + (jb + 1) * 256], KC, 256)
            for jl in range(2):
                j = jb * 2 + jl
                cs = slice(jl * 128, (jl + 1) * 128)
                for g, (c0, n) in enumerate(QG):
                    sb_ = ust["i"] % 2
                    ust["i"] += 1
                    bg = nbank(ALLB)
                    for kc in range(KC):
                        mm(ps[:, bg, 0:n], wvg[:, kc, cs], hqv(kc, c0, n), kc == 0, kc == KC - 1,
                           [wkg, hqk(g, kc)], bg)
                    act(sg[:, sb_, 0:n], ps[:, bg, 0:n], AF.Sigmoid, [PSK(bg)], [("sg", sb_)])
                    by = nbank(ALLB)
                    for kc in range(8):
                        mm(ps[:, by, 0:n], wvo[:, kc, cs], src_t[:, kc, c0:c0 + n], kc == 0, kc == 7,
                           [wko, (srcn, kc, g)], by)
                    if br == 0:
                        tt(t1[:, jl, c0:c0 + n], ps[:, by, 0:n], sg[:, sb_, 0:n], ALU.mult,
                           [PSK(by), ("sg", sb_)], [("t1", jl, g)])
                    else:
                        tt(mtmp[:, sb_, 0:n], ps[:, by, 0:n], sg[:, sb_, 0:n], ALU.mult,
                           [PSK(by), ("sg", sb_)], [("mtmp", sb_)])
                        tt(mt[:, j, c0:c0 + n], mtmp[:, sb_, 0:n], t1[:, jl, c0:c0 + n], ALU.add,
                           [("mtmp", sb_), ("t1", jl, g)], [("mt", j, g)])
    for n_ in ("hq", "hqh", "oa", "ob", "sg", "mtmp", "t1", "ropet"):
        A.release(n_)

    A.release("wsl")
    xres = A.alloc("xres", [128, KC, Q], F32)
    wstate["wsl"] = A.alloc("wsl", [128, 4, SLOT_ELEMS], BF16)
    for kq in range(4):
        dma("sync", xres[:, kq * 4:(kq + 1) * 4, :],
            xqT[kq * 512:(kq + 1) * 512, :].rearrange("(k p) t -> p k t", p=128), [],
            [("xres", kc, g) for kc in range(kq * 4, kq * 4 + 4) for g in range(3)], ("xres", kq))

    def proj_add(wsrc_of_block, nk, act_t, actkeys, nblocks=8):
        for jb in range(nblocks):
            wk, wv = wsrc_of_block(jb)
            for jl in range(2):
                j = jb * 2 + jl
                for g, (c0, n) in enumerate(QG):
                    bank = nbank([0, 1, 2, 3, 4, 5, 6, 7])
                    for kc in range(nk):
                        mm(ps[:, bank, 0:n], wv[:, kc, jl * 128:(jl + 1) * 128], act_t[:, kc, c0:c0 + n],
                           kc == 0, kc == nk - 1, [wk] + actkeys(kc, g), bank)
                    tt(xres[:, j, c0:c0 + n], ps[:, bank, 0:n], xres[:, j, c0:c0 + n], ALU.add,
                       [PSK(bank), ("xres", j, g)], [("xres", j, g)])

    checkpoint(6, [("mt", mt)])
    proj_add(lambda jb: wload(w_out[:, jb * 256:(jb + 1) * 256], KC, 256), KC, mt,
             lambda kc, g: [("mt", kc, g)])
    checkpoint(7, [("xres", xres)])
    A.release("mt")

    h2 = A.alloc("h2", [128, KC, Q], BF16)
    norm_T(lambda kc, c0, n: xres[:, kc, c0:c0 + n], lambda kc, gi: ("xres", kc, gi),
           lambda kc, c0, n: h2[:, kc, c0:c0 + n], lambda kc, gi: ("h2", kc, gi),
           KC, QG, CS_GCROSS, D, "n5", [0, 1])
    mem32 = A.alloc("mem32", [128, KC, 256], F32)
    memn = A.alloc("memn", [128, KC, 256], BF16)
    dma("sync", mem32[:], memT[:, :].rearrange("(k p) t -> p k t", p=128), [], ["mem32"], "mem32")
    norm_T(lambda kc, c0, n: mem32[:, kc, c0:c0 + n], lambda kc, gi: "mem32",
           lambda kc, c0, n: memn[:, kc, c0:c0 + n], lambda kc, gi: ("memn", kc),
           KC, [(0, 256)], CS_GMEM, D, "n6", [2])
    A.release("mem32")
    qx = A.alloc("qx", [128, 4, Q], BF16)
    kx = A.alloc("kx", [128, 4, 256], BF16)
    vx = A.alloc("vx", [128, 2, 512], BF16)
    oc = A.alloc("oc", [128, 4, Q], BF16)
    abufs = attn_alloc()
    for hb2 in range(2):
        wk, wv = wload(w_cq[:, hb2 * 256:(hb2 + 1) * 256], KC, 256)
        for hl in range(2):
            h = hb2 * 2 + hl
            for g, (c0, n) in enumerate(QG):
                bank = nbank([0, 1, 2, 3, 4, 5])
                for kc in range(KC):
                    mm(ps[:, bank, 0:n], wv[:, kc, hl * 128:(hl + 1) * 128], h2[:, kc, c0:c0 + n],
                       kc == 0, kc == KC - 1, [wk, ("h2", kc, g)], bank)
                act(qx[:, h, c0:c0 + n], ps[:, bank, 0:n], AF.Copy, [PSK(bank)], [("qx", h, g)])
    for h in range(4):
        wk, wv = wload(w_ckv[:, h * 256:(h + 1) * 256], KC, 256)
        bank = nbank([0, 1, 2, 3, 4, 5])
        for kc in range(KC):
            mm(ps[:, bank, 0:256], wv[:, kc, 0:128], memn[:, kc, :], kc == 0, kc == KC - 1,
               [wk, ("memn", kc)], bank)
        act(kx[:, h, :], ps[:, bank, 0:256], AF.Copy, [PSK(bank)], [("kx", h)])
        bank = nbank([0, 1, 2, 3, 4, 5])
        for tc in range(2):
            for kc in range(KC):
                mm(ps[:, bank, tc * 128:(tc + 1) * 128], memn[:, kc, tc * 128:(tc + 1) * 128], wv[:, kc, 128:256],
                   kc == 0, kc == KC - 1, [wk, ("memn", kc)], bank)
        act(vx[:, :, h * 128:(h + 1) * 128], ps[:, bank, 0:256].rearrange("p (a b) -> p a b", a=2), AF.Copy,
            [PSK(bank)], [("vx", h)])
    pTx, osbx, ssbx = abufs
    xcalls = [(h, g, c0, n) for h in range(4) for g, (c0, n) in enumerate(QG)]
    xscale = 128.0 ** -0.5

    def xs_stage(i):
        h, g, c0, n = xcalls[i]
        sp = i % 3
        for j in range(2):
            mm(ps[:, 2 * sp + j, 0:n], kx[:, h, j * 128:(j + 1) * 128], qx[:, h, c0:c0 + n], True, True,
               [("kx", h), ("qx", h, g)], 2 * sp + j)
        act(pTx[:, sp, :, 0:n], ps[:, 2 * sp:2 * sp + 2, 0:n], AF.Exp, [PSK(2 * sp), PSK(2 * sp + 1)],
            [("pT", sp)], scale=float(xscale))

    def xpv_stage(i):
        h, g, c0, n = xcalls[i]
        sp = i % 3
        cb = i % 2
        for j in range(2):
            mm(ps[:, O_BANK, 0:n], vx[:, j, h * 128:(h + 1) * 128], pTx[:, sp, j, 0:n], j == 0, j == 1,
               [("vx", h), ("pT", sp)], O_BANK)
            mm(ps[:, SUM_BANK, 0:n], ones[:, :], pTx[:, sp, j, 0:n], j == 0, j == 1, ["ones", ("pT", sp)], SUM_BANK)
        vcopy(osbx[:, cb, 0:n], ps[:, O_BANK, 0:n], [PSK(O_BANK)], [("osb", cb)])
        vcopy(ssbx[:, cb, 0:n], ps[:, SUM_BANK, 0:n], [PSK(SUM_BANK)], [("ssb", cb)])
        recip(ssbx[:, cb, 0:n], ssbx[:, cb, 0:n], [("ssb", cb)], [("ssb", cb)])
        tt(oc[:, h, c0:c0 + n], osbx[:, cb, 0:n], ssbx[:, cb, 0:n], ALU.mult, [("osb", cb), ("ssb", cb)],
           [("oc", h, g)])

    xpend = []
    for i in range(len(xcalls)):
        xpend.append(i)
        xs_stage(i)
        if len(xpend) > LOOK:
            xpv_stage(xpend.pop(0))
    while xpend:
        xpv_stage(xpend.pop(0))
    proj_add(lambda jb: wload(w_co[:, jb * 256:(jb + 1) * 256], 4, 256), 4, oc,
             lambda kc, g: [("oc", kc, g)])
    checkpoint(8, [("xres", xres), ("oc", oc)])
    for n_ in ("h2", "memn", "qx", "kx", "vx", "oc", "pT", "osb", "ssb"):
        A.release(n_)

    h3 = A.alloc("h3", [128, KC, Q], BF16)
    norm_T(lambda kc, c0, n: xres[:, kc, c0:c0 + n], lambda kc, gi: ("xres", kc, gi),
           lambda kc, c0, n: h3[:, kc, c0:c0 + n], lambda kc, gi: ("h3", kc, gi),
           KC, QG, CS_GFFN, D, "n7", [0, 1])
    NQ = 4
    CPQ = 11
    aT = A.alloc("aT", [128, CPQ, Q], BF16)
    ubuf = A.alloc("ubuf", [128, 2, 2, Q + 2], F32)
    cbuf = A.alloc("cbuf", [128, 2, 2, Q], F32)
    for pb in range(2):
        for s_ in range(2):
            P.add("vector", lambda e, pb=pb, s_=s_: e.memset(ubuf[:, pb, s_, 0:1], 0.0), writes=[("ubuf", pb, s_, "l")])
            P.add("vector", lambda e, pb=pb, s_=s_: e.memset(ubuf[:, pb, s_, Q + 1:Q + 2], 0.0),
                  writes=[("ubuf", pb, s_, "r")])
    for qd in range(NQ):
        for cl in range(CPQ):
            jj = qd * CPQ + cl
            pb = jj % 2
            wkg, wvg = wload(w_up[:, jj * 128:(jj + 1) * 128], KC, 128)
            wkv_, wvv = wload(w_up[:, FFN + jj * 128:FFN + (jj + 1) * 128], KC, 128)
            for s_, (wk, wv) in enumerate(((wkg, wvg), (wkv_, wvv))):
                for g, (c0, n) in enumerate(QG):
                    bank = nbank([0, 1, 2, 3, 4, 5, 6, 7])
                    for kc in range(KC):
                        mm(ps[:, bank, 0:n], wv[:, kc, :], h3[:, kc, c0:c0 + n], kc == 0, kc == KC - 1,
                           [wk, ("h3", kc, g)], bank)
                    act(ubuf[:, pb, s_, 1 + c0:1 + c0 + n], ps[:, bank, 0:n], AF.Copy, [PSK(bank)],
                        [("ubuf", pb, s_, g)])
                ukeys = [("ubuf", pb, s_, g) for g in range(3)] + [("ubuf", pb, s_, "l"), ("ubuf", pb, s_, "r")]
                col = s_ * 44 + jj
                ts(cbuf[:, pb, s_, :], ubuf[:, pb, s_, 1:Q + 1], ccol(CS_CW + 1 * 88 + col), ccol(CS_CB + col),
                   ALU.mult, ALU.add, ukeys + ["cst"], [("cbuf", pb, s_)])
                stt(cbuf[:, pb, s_, :], ubuf[:, pb, s_, 0:Q], ccol(CS_CW + 0 * 88 + col), cbuf[:, pb, s_, :],
                    ALU.mult, ALU.add, ukeys + ["cst", ("cbuf", pb, s_)], [("cbuf", pb, s_)])
                stt(cbuf[:, pb, s_, :], ubuf[:, pb, s_, 2:Q + 2], ccol(CS_CW + 2 * 88 + col), cbuf[:, pb, s_, :],
                    ALU.mult, ALU.add, ukeys + ["cst", ("cbuf", pb, s_)], [("cbuf", pb, s_)])
            act(cbuf[:, pb, 0, :], cbuf[:, pb, 0, :], AF.Silu, [("cbuf", pb, 0)], [("cbuf", pb, 0)])
            tt(aT[:, cl, :], cbuf[:, pb, 0, :], cbuf[:, pb, 1, :], ALU.mult, [("cbuf", pb, 0), ("cbuf", pb, 1)],
               [("aT", cl)])
        proj_add(lambda jb, qd=qd: wload(w_down[qd * CPQ * 128:(qd + 1) * CPQ * 128, jb * 256:(jb + 1) * 256],
                                         CPQ, 256),
                 CPQ, aT, lambda kc, g: [("aT", kc)])
    checkpoint(9, [("xres", xres)])
    for n_ in ("h3", "aT", "ubuf", "cbuf"):
        A.release(n_)

    norm_T(lambda kc, c0, n: xres[:, kc, c0:c0 + n], lambda kc, gi: ("xres", kc, gi),
           lambda kc, c0, n: xres[:, kc, c0:c0 + n], lambda kc, gi: ("xres", kc, gi),
           KC, QG, CS_GFIN, D, "n8", [0, 1])
    for kq in range(4):
        dma("sync", outT[kq * 512:(kq + 1) * 512, :].rearrange("(k p) t -> p k t", p=128),
            xres[:, kq * 4:(kq + 1) * 4, :],
            [("xres", kc, g) for kc in range(kq * 4, kq * 4 + 4) for g in range(3)], [], ("out", kq),
            is_output=True)
    P.finish("sync")
    P.emit(nc)


_NC_CACHE = {}


def _host_inputs(inp):
    x = np.asarray(inp["x"], dtype=np.float32)
    mem = np.asarray(inp["mem"], dtype=np.float32)
    pos = np.asarray(inp["positions"], dtype=np.int32)

    def col(v, n):
        return np.asarray(v, np.float32).reshape(n, 128).T

    cst = np.zeros((128, NCST), np.float32)
    cst[:, CS_GMIX:CS_GMIX + 16] = col(inp["g_mix_norm"][0], 16)
    cst[:, CS_GCROSS:CS_GCROSS + 16] = col(inp["g_cross_norm"][0], 16)
    cst[:, CS_GMEM:CS_GMEM + 16] = col(inp["g_mem_norm"][0], 16)
    cst[:, CS_GFFN:CS_GFFN + 16] = col(inp["g_ffn_norm"][0], 16)
    cst[:, CS_GFIN:CS_GFIN + 16] = col(inp["g_final"], 16)
    cst[:, CS_GQ:CS_GQ + 4] = col(inp["g_q_norm"][0], 4)
    cst[:, CS_GKV:CS_GKV + 4] = col(inp["g_kv_norm"][0], 4)
    cst[:, CS_GDIFF] = np.asarray(inp["g_diff_sub"][0], np.float32)
    p = np.arange(128)
    inv = (10000.0 ** (-np.arange(0, 64, 2, dtype=np.float32) / np.float32(64))).astype(np.float32)
    cst[:, CS_INV] = inv[p % 32]
    cst[:, CS_SIGN] = np.where((p % 64) < 32, -1.0, 1.0)
    cw = np.asarray(inp["conv_w"][0], np.float32)
    for k in range(3):
        cst[:, CS_CW + k * 88:CS_CW + (k + 1) * 88] = col(cw[k], 88)
    cst[:, CS_CB:CS_CB + 88] = col(inp["conv_b"][0], 88)
    cst[:, CS_EPS1] = EPS
    cst[:, CS_EPS2] = EPS / ((1.0 - LAM_INIT) ** 2)
    lamv = np.concatenate([np.asarray(inp[k][0], np.float32) for k in
                           ("lambda_q1", "lambda_k1", "lambda_q2", "lambda_k2")])[None, :].repeat(128, 0)
    cmat = np.zeros((128, 256), np.float32)
    perm = np.where((p % 64) < 32, p + 32, p - 32)
    cmat[perm, p] = 1.0
    cmat[p, 128 + p] = 1.0
    shared = {
        "cst": cst, "lamv": np.ascontiguousarray(lamv), "cmat": cmat,
        "w_in": np.ascontiguousarray(inp["w_in"][0]), "w_uq": np.ascontiguousarray(inp["w_uq"][0]),
        "w_ukv": np.ascontiguousarray(inp["w_ukv"][0]), "w_o_mla": np.ascontiguousarray(inp["w_o_mla"][0]),
        "w_o_diff": np.ascontiguousarray(inp["w_o_diff"][0]), "w_out": np.ascontiguousarray(inp["w_out"][0]),
        "w_cross_q": np.ascontiguousarray(inp["w_cross_q"][0]),
        "w_cross_kv": np.ascontiguousarray(inp["w_cross_kv"][0]),
        "w_cross_o": np.ascontiguousarray(inp["w_cross_o"][0]), "w_up": np.ascontiguousarray(inp["w_up"][0]),
        "w_down": np.ascontiguousarray(inp["w_down"][0]),
    }
    shared = {k: np.asarray(v, np.float32) for k, v in shared.items()}
    in_maps = []
    for c in range(8):
        b, half = divmod(c, 2)
        q0 = half * 1023
        m = dict(shared)
        order = np.concatenate([np.arange(q0, q0 + Q), np.arange(0, q0), np.arange(q0 + Q, 2048)])
        m["xkvT"] = np.ascontiguousarray(x[b][order].T)
        m["xqT"] = np.ascontiguousarray(x[b, q0:q0 + Q].T)
        m["memT"] = np.ascontiguousarray(mem[b].T)
        m["poskv"] = np.ascontiguousarray(pos[b][order][None, :].repeat(128, 0))
        m["posq"] = np.ascontiguousarray(pos[b, q0:q0 + Q][None, :].repeat(128, 0))
        in_maps.append(m)
    return in_maps


def kernel(**inp):
    if "nc" not in _NC_CACHE:
        _NC_CACHE["nc"] = build_nc()
    nc = _NC_CACHE["nc"]
    in_maps = _host_inputs(inp)
    res = run_bass_kernel_spmd(nc, in_maps, core_ids=list(range(8)))
    out = np.empty((4, 2048, D), np.float32)
    for c in range(8):
        b, half = divmod(c, 2)
        o = res.results[c]["outT"]
        if half == 0:
            out[b, 0:1024, :] = o[:, 0:1024].T
        else:
            out[b, 1024:2048, :] = o[:, 1:1025].T
    return out
```

```python
import math
from contextlib import ExitStack

import numpy as np
import concourse.bass as bass
import concourse.mybir as mybir
from concourse.bass_utils import run_bass_kernel_spmd

F32 = mybir.dt.float32
BF16 = mybir.dt.bfloat16
I32 = mybir.dt.int32
U8 = mybir.dt.uint8
AF = mybir.ActivationFunctionType
ALU = mybir.AluOpType
AX = mybir.AxisListType

ENG_NAMES = ["tensor", "vector", "scalar", "gpsimd", "sync"]
SAME_ENGINE_SYNC = True
_DBG_EMIT = False

D = 2048
KC = 16
T = 2048
Q = 1025
QG = [(0, 342), (342, 342), (684, 341)]
KG = [(i * 512, 512) for i in range(4)]
FFN = 5632
EPS = 1e-6
C_CQ, C_CKV, C_KPE, C_DQ, C_DK, C_DV, C_GA, C_GB = 0, 512, 1024, 1088, 2112, 3136, 4160, 6208
CS_GMIX, CS_GCROSS, CS_GMEM, CS_GFFN, CS_GFIN = 0, 16, 32, 48, 64
CS_GQ, CS_GKV, CS_GDIFF, CS_INV, CS_SIGN = 80, 84, 88, 89, 90
CS_CW, CS_CB = 91, 91 + 264
CS_EPS1, CS_EPS2 = 91 + 264 + 88, 91 + 264 + 89
NCST = 91 + 264 + 90
LAM_INIT = 0.8 - 0.6 * math.exp(-0.3 * 0)


def _bufname(k):
    return k if isinstance(k, str) else k[0]


class Op:
    __slots__ = ("id", "eng", "fn", "deps", "is_dma", "dma_sem", "dma_val", "sig", "cnt")


class Prog:
    def __init__(self):
        self.ops = []
        self.by_eng = {e: [] for e in ENG_NAMES}
        self.lastw = {}
        self.readers = {}
        self.dma_cnt = {}
        self.dma_last = {}
        self.out_dmas = []
        self.pending = {}
        self.bufkeys = {}

    def add(self, eng, fn, reads=(), writes=(), dma_sem=None, is_output=False, extra_deps=()):
        op = Op()
        op.id = len(self.ops)
        op.eng = eng
        op.fn = fn
        deps = set(extra_deps)
        if eng != "tensor":
            writes = list(writes) + [r for r in reads if isinstance(r, tuple) and r[0] == "ps" and r not in writes]
        for r in reads:
            w = self.lastw.get(r)
            if w is not None:
                deps.add(w)
            pd = self.pending.get(_bufname(r))
            if pd:
                deps |= pd
        for k in writes:
            w = self.lastw.get(k)
            if w is not None:
                deps.add(w)
            for rd in self.readers.get(k, ()):
                deps.add(rd)
            pd = self.pending.get(_bufname(k))
            if pd:
                deps |= pd
        op.is_dma = dma_sem is not None
        op.dma_sem = dma_sem
        op.dma_val = 0
        if op.is_dma:
            n = self.dma_cnt.get(dma_sem, 0) + 1
            self.dma_cnt[dma_sem] = n
            op.dma_val = 16 * n
            prev = self.dma_last.get(dma_sem)
            if prev is not None:
                deps.add(prev)
            self.dma_last[dma_sem] = op.id
            if is_output:
                self.out_dmas.append(op.id)
        deps.discard(op.id)
        op.deps = deps
        op.sig = False
        op.cnt = 0
        for k in writes:
            self.lastw[k] = op.id
            self.readers[k] = []
            self.bufkeys.setdefault(_bufname(k), set()).add(k)
        for r in reads:
            if r not in writes:
                self.readers.setdefault(r, []).append(op.id)
            self.bufkeys.setdefault(_bufname(r), set()).add(r)
        self.ops.append(op)
        self.by_eng[eng].append(op)
        return op

    def touch_ops(self, name):
        s = set()
        for k in self.bufkeys.get(name, ()):
            w = self.lastw.pop(k, None)
            if w is not None:
                s.add(w)
            for r in self.readers.pop(k, ()):
                s.add(r)
        s |= self.pending.pop(name, set())
        self.bufkeys.pop(name, None)
        return s

    def finish(self, eng="sync"):
        self.add(eng, None, extra_deps=list(self.out_dmas))

    def emit(self, nc):
        needed = set()
        for op in self.ops:
            agg = {}
            for d in op.deps:
                p = self.ops[d]
                if p.is_dma:
                    key = ("d", p.dma_sem)
                else:
                    if p.fn is None:
                        continue
                    if p.eng == op.eng and p.eng == "tensor" and not op.is_dma:
                        continue
                    if p.eng == op.eng and not op.is_dma and not SAME_ENGINE_SYNC:
                        continue
                    key = ("e", p.eng)
                if key not in agg or agg[key] < d:
                    agg[key] = d
            op.deps = set(agg.values())
            for d in op.deps:
                if not self.ops[d].is_dma:
                    needed.add(d)
        for e in ENG_NAMES:
            c = 0
            for op in self.by_eng[e]:
                if op.is_dma or op.fn is None:
                    continue
                if op.id in needed:
                    c += 1
                    op.cnt = c
                    op.sig = True
        with ExitStack() as st:
            esem = {e: st.enter_context(nc.semaphore("es_" + e)) for e in ENG_NAMES}
            dsem = {}
            for i, k in enumerate(self.dma_cnt):
                dsem[k] = st.enter_context(nc.semaphore("ds_%d" % i))
            block = st.enter_context(nc.Block())
            prog = self

            def run(engh, e):
                waited = {}
                for op in prog.by_eng[e]:
                    for d in sorted(op.deps):
                        p = prog.ops[d]
                        if p.is_dma:
                            key, val, sem = ("d", p.dma_sem), p.dma_val, dsem[p.dma_sem]
                        else:
                            key, val, sem = ("e", p.eng), p.cnt, esem[p.eng]
                        if waited.get(key, 0) >= val:
                            continue
                        engh.wait_ge(sem, val)
                        waited[key] = val
                        if _DBG_EMIT:
                            print("  [%s] op%d wait %s >= %d" % (e, op.id, key, val))
                    if _DBG_EMIT:
                        print("[%s] op%d %s sig=%s cnt=%d dma=%s" % (e, op.id, "none" if op.fn is None else "", op.sig, op.cnt, op.dma_sem if op.is_dma else ""))
                    if op.fn is None:
                        continue
                    ins = op.fn(engh)
                    if op.is_dma:
                        ins.then_inc(dsem[op.dma_sem], 16)
                    elif op.sig:
                        ins.then_inc(esem[e], 1)

            @block.tensor
            def _(eng):
                run(eng, "tensor")

            @block.vector
            def _(eng):
                run(eng, "vector")

            @block.scalar
            def _(eng):
                run(eng, "scalar")

            @block.gpsimd
            def _(eng):
                run(eng, "gpsimd")

            @block.sync
            def _(eng):
                run(eng, "sync")


_DT_SIZE = {F32: 4, BF16: 2, I32: 4, U8: 1}


class Arena:
    def __init__(self, nc, P, base, size):
        self.nc, self.P = nc, P
        self.free = [(base, size)]
        self.live = {}
        self.retired = []
        self.n = 0

    def alloc(self, name, shape, dtype, top=False):
        nbytes = _DT_SIZE[dtype]
        for s in shape[1:]:
            nbytes *= s
        nbytes = (nbytes + 63) // 64 * 64
        order = range(len(self.free) - 1, -1, -1) if top else range(len(self.free))
        for i in order:
            (o, s) = self.free[i]
            if s >= nbytes:
                if s == nbytes:
                    self.free.pop(i)
                elif top:
                    self.free[i] = (o, s - nbytes)
                    o = o + s - nbytes
                else:
                    self.free[i] = (o + nbytes, s - nbytes)
                break
        else:
            raise RuntimeError("SBUF arena full allocating %s (%d B); live=%s free=%s" % (
                name, nbytes, {k: v[1] for k, v in self.live.items()}, self.free))
        self.live[name] = (o, nbytes)
        deps = set()
        keep = []
        for (ro, rs, ops) in self.retired:
            if ro < o + nbytes and o < ro + rs:
                deps |= ops
            keep.append((ro, rs, ops))
        self.retired = keep
        if deps:
            self.P.pending[name] = deps
        self.n += 1
        return self.nc.alloc_sbuf_tensor_at("%s_%d" % (name, self.n), list(shape), dtype, offset=o)

    def release(self, name):
        o, s = self.live.pop(name)
        ops = self.P.touch_ops(name)
        self.retired.append((o, s, ops))
        fl = sorted(self.free + [(o, s)])
        merged = []
        for (a, b) in fl:
            if merged and merged[-1][0] + merged[-1][1] == a:
                merged[-1] = (merged[-1][0], merged[-1][1] + b)
            else:
                merged.append((a, b))
        self.free = merged


class _Stop(Exception):
    pass


def build_nc(stop_after=None):
    nc = bass.Bass("TRN2", target_bir_lowering=False)
    P = Prog()
    try:
        _build_body(nc, P, stop_after)
    except _Stop:
        pass
    return nc


def _build_body(nc, P, stop_after):
    def checkpoint(k, items):
        if stop_after != k:
            return
        last = [ops[-1].id for e, ops in P.by_eng.items() if ops]
        for (name, t) in items:
            d = nc.dram_tensor("dbg_" + name, list(t.shape), t.dtype, kind="ExternalOutput").ap()
            idx = tuple(slice(None) for _ in t.shape)
            P.add("sync", lambda e, d=d, t=t, idx=idx: e.dma_start(out=d[idx], in_=t[idx]),
                  dma_sem=("dbg", name), is_output=True, extra_deps=last)
        P.finish("sync")
        P.emit(nc)
        raise _Stop()


    def din(name, shape, dt=F32):
        return nc.dram_tensor(name, list(shape), dt, kind="ExternalInput").ap()

    xkvT = din("xkvT", [D, T])
    xqT = din("xqT", [D, Q])
    memT = din("memT", [D, 256])
    poskv_d = din("poskv", [128, T], I32)
    posq_d = din("posq", [128, Q], I32)
    cst_d = din("cst", [128, NCST])
    lamv_d = din("lamv", [128, 256])
    cmat_d = din("cmat", [128, 256])
    w_in = din("w_in", [D, 8256])
    w_uq = din("w_uq", [512, 1536])
    w_ukv = din("w_ukv", [512, 2048])
    w_o_mla = din("w_o_mla", [1024, D])
    w_o_diff = din("w_o_diff", [1024, D])
    w_out = din("w_out", [D, D])
    w_cq = din("w_cross_q", [D, 512])
    w_ckv = din("w_cross_kv", [D, 1024])
    w_co = din("w_cross_o", [512, D])
    w_up = din("w_up", [D, 2 * FFN])
    w_down = din("w_down", [FFN, D])
    outT = nc.dram_tensor("outT", [D, Q], F32, kind="ExternalOutput").ap()
    dbg_list = []

    base = (nc.sbuf_base + 63) // 64 * 64
    asize = (nc.sbuf_top - base) // 64 * 64
    nc.alloc_sbuf_tensor("arena", [128, asize], U8)
    A = Arena(nc, P, base, asize)
    ps = nc.alloc_psum_tensor("ps", [128, 8, 512], F32)

    def PSK(b):
        return ("ps", b)

    def act(out, in_, func, reads, writes, bias=None, scale=None):
        kw = {}
        if bias is not None:
            kw["bias"] = bias
        if scale is not None:
            kw["scale"] = scale
        P.add("scalar", lambda e: e.activation(out=out, in_=in_, func=func, **kw), reads=reads, writes=writes)

    def tt(out, in0, in1, op, reads, writes):
        P.add("vector", lambda e: e.tensor_tensor(out=out, in0=in0, in1=in1, op=op), reads=reads, writes=writes)

    def ts(out, in0, s1, s2, op0, op1, reads, writes):
        if op1 is None:
            P.add("vector", lambda e: e.tensor_scalar(out=out, in0=in0, scalar1=s1, scalar2=None, op0=op0),
                  reads=reads, writes=writes)
        else:
            P.add("vector", lambda e: e.tensor_scalar(out=out, in0=in0, scalar1=s1, scalar2=s2, op0=op0, op1=op1),
                  reads=reads, writes=writes)

    def stt(out, in0, scalar, in1, op0, op1, reads, writes):
        P.add("vector", lambda e: e.scalar_tensor_tensor(out=out, in0=in0, scalar=scalar, in1=in1, op0=op0, op1=op1),
              reads=reads, writes=writes)

    def vcopy(out, in_, reads, writes):
        P.add("vector", lambda e: e.tensor_copy(out=out, in_=in_), reads=reads, writes=writes)

    def recip(out, in_, reads, writes):
        P.add("vector", lambda e: e.reciprocal(out=out, in_=in_), reads=reads, writes=writes)

    def mm(out, lhsT, rhs, start, stop, reads, bank):
        P.add("tensor", lambda e: e.matmul(out, lhsT=lhsT, rhs=rhs, start=start, stop=stop),
              reads=reads, writes=[PSK(bank)])

    def dma(eng, out, in_, reads, writes, sem, is_output=False):
        P.add(eng, lambda e: e.dma_start(out=out, in_=in_), reads=reads, writes=writes, dma_sem=sem,
              is_output=is_output)

    cst = A.alloc("cst", [128, NCST], F32)
    cmat = A.alloc("cmat", [128, 256], BF16)
    ones = A.alloc("ones", [128, 128], BF16)
    lamc = A.alloc("lamc", [128, 4], F32)
    dma("sync", cst[:], cst_d[:, :], [], ["cst"], "cst")
    dma("gpsimd", cmat[:], cmat_d[:, :], [], ["cmat"], "cmat")
    P.add("vector", lambda e: e.memset(ones[:], 1.0), writes=["ones"])
    Rm = cmat[:, 0:128]

    def ccol(c0, n=1):
        return cst[:, c0:c0 + n]

    NSLOT = 3
    SLOT_ELEMS = 4096
    wstate = {"i": 0, "wsl": None, "ns": NSLOT}

    def wload(src, nk, ncols):
        assert nk * ncols <= SLOT_ELEMS, (nk, ncols)
        s = wstate["i"] % wstate["ns"]
        wstate["i"] += 1
        wsl = wstate["wsl"]
        view = wsl[:, s, 0:nk * ncols].rearrange("p (k n) -> p k n", k=nk)
        dma("gpsimd", view, src.rearrange("(k p) n -> p k n", p=128), [], [("wsl", s)], ("wsl", s))
        return ("wsl", s), view

    TWO_PI = 2.0 * math.pi
    HI = 6.28125
    LO = TWO_PI - HI
    PI_S = 3.141592

    def rope_tables(pos_d, N, cosn, sinn):
        cos_t = A.alloc(cosn, [128, N], F32)
        sin_t = A.alloc(sinn, [128, N], F32)
        pi_ = A.alloc("rt_pi", [128, N], I32)
        r = A.alloc("rt_r", [128, N], F32)
        m = A.alloc("rt_m", [128, N], F32)
        dma("sync", pi_[:], pos_d[:, :], [], ["rt_pi"], "rt_pi")
        vcopy(r[:], pi_[:], ["rt_pi"], ["rt_r"])
        ts(r[:], r[:], ccol(CS_INV), None, ALU.mult, None, ["rt_r", "cst"], ["rt_r"])
        ts(pi_[:], r[:], 1.0 / TWO_PI, None, ALU.mult, None, ["rt_r"], ["rt_pi"])
        vcopy(m[:], pi_[:], ["rt_pi"], ["rt_m"])
        stt(r[:], m[:], -HI, r[:], ALU.mult, ALU.add, ["rt_m", "rt_r"], ["rt_r"])
        stt(r[:], m[:], -LO, r[:], ALU.mult, ALU.add, ["rt_m", "rt_r"], ["rt_r"])

        def wrap(v, vn):
            ts(m[:], v[:], math.pi, -TWO_PI, ALU.is_gt, ALU.mult, [vn], ["rt_m"])
            tt(v[:], v[:], m[:], ALU.add, [vn, "rt_m"], [vn])
            ts(m[:], v[:], -math.pi, TWO_PI, ALU.is_lt, ALU.mult, [vn], ["rt_m"])
            tt(v[:], v[:], m[:], ALU.add, [vn, "rt_m"], [vn])
            ts(v[:], v[:], -PI_S, PI_S, ALU.max, ALU.min, [vn], [vn])

        wrap(r, "rt_r")
        act(sin_t[:], r[:], AF.Sin, ["rt_r", "cst"], [sinn], scale=ccol(CS_SIGN))
        ts(r[:], r[:], math.pi / 2, None, ALU.add, None, ["rt_r"], ["rt_r"])
        wrap(r, "rt_r")
        act(cos_t[:], r[:], AF.Sin, ["rt_r"], [cosn])
        for n_ in ("rt_pi", "rt_r", "rt_m"):
            A.release(n_)
        return cos_t, sin_t

    def norm_T(src, srck, dst, dstk, nk, groups, gcol0, dn, sqn, bank_list, extra=1.0, per_group_keys=True):
        N = groups[-1][0] + groups[-1][1]
        rstd = A.alloc(sqn + "_rstd", [128, N], F32)
        sq = A.alloc(sqn + "_sq", [128, nk, 512], BF16)
        for gi, (c0, n) in enumerate(groups):
            bank = bank_list[gi % len(bank_list)]
            for kc in range(nk):
                act(sq[:, kc, 0:n], src(kc, c0, n), AF.Square, [srck(kc, gi)], [(sqn + "_sq", kc)])
            for kc in range(nk):
                mm(ps[:, bank, 0:n], ones[:, :], sq[:, kc, 0:n], kc == 0, kc == nk - 1,
                   ["ones", (sqn + "_sq", kc)], bank)
            act(rstd[:, c0:c0 + n], ps[:, bank, 0:n], AF.Sqrt, [PSK(bank), "cst"], [(sqn + "_rstd", gi)],
                bias=ccol(CS_EPS1 if extra == 1.0 else CS_EPS2), scale=float(1.0 / (dn * extra * extra)))
            recip(rstd[:, c0:c0 + n], rstd[:, c0:c0 + n], [(sqn + "_rstd", gi)], [(sqn + "_rstd", gi)])
            for kc in range(nk):
                stt(dst(kc, c0, n), src(kc, c0, n), ccol(gcol0 + kc), rstd[:, c0:c0 + n], ALU.mult, ALU.mult,
                    [srck(kc, gi), "cst", (sqn + "_rstd", gi)], [dstk(kc, gi)])
        A.release(sqn + "_rstd")
        A.release(sqn + "_sq")

    rstate = {"i": 0, "t": None}

    rope_pend = []

    def rope_flush():
        while rope_pend:
            rope_pend.pop(0)()

    def rope(bank, bank2, np_, n, cos_ap, sin_ap, tabkeys, out, outk):
        b = rstate["i"] % 2
        rstate["i"] += 1
        ropet = rstate["t"]
        qa = ropet[:, b, 512:768].bitcast(BF16)[0:np_, 0:n]
        t_ = ropet[0:np_, b, 0:n]
        u_ = ropet[0:np_, b, 768:768 + n]
        act(qa, ps[0:np_, bank, 0:n], AF.Copy, [PSK(bank)], [("ropet", b, 0)])
        rope_flush()

        def part_b():
            mm(ps[0:np_, bank2, 0:n], cmat[0:np_, 0:np_], qa, True, True, ["cmat", ("ropet", b, 0)], bank2)
            tt(t_, ps[0:np_, bank, 0:n], cos_ap, ALU.mult, [PSK(bank), ("ropet", b, 0)] + tabkeys, [("ropet", b, 1)])
            tt(u_, ps[0:np_, bank2, 0:n], sin_ap, ALU.mult, [PSK(bank2)] + tabkeys, [("ropet", b, 2)])
            tt(out, u_, t_, ALU.add, [("ropet", b, 1), ("ropet", b, 2)], [outk])

        rope_pend.append(part_b)

    checkpoint(-1, [("cst", cst), ("cmat", cmat), ("ones", ones)])
    bankc = {"i": 0}

    def nbank(lst):
        b = lst[bankc["i"] % len(lst)]
        bankc["i"] += 1
        return b

    hkv = A.alloc("hkv", [128, KC, T], BF16)
    xblk = A.alloc("xblk", [128, 2, KC, 512], F32)
    ckv32 = A.alloc("ckv32", [128, 4, T], F32)
    wstate["wsl"] = A.alloc("wsl", [128, NSLOT, SLOT_ELEMS], BF16, top=True)
    ckvw = [wload(w_in[:, C_CKV + blk2 * 256:C_CKV + (blk2 + 1) * 256], KC, 256) for blk2 in range(2)]
    def norm_blk(blk):
        xb = blk % 2
        dma("sync", xblk[:, xb], xkvT[:, blk * 512:(blk + 1) * 512].rearrange("(k p) t -> p k t", p=128),
            [], [("xblk", xb)], ("xblk", xb))
        norm_T(lambda kc, c0, n, xb=xb: xblk[:, xb, kc, c0:c0 + n], lambda kc, gi, xb=xb: ("xblk", xb),
               lambda kc, c0, n, blk=blk: hkv[:, kc, blk * 512 + c0:blk * 512 + c0 + n],
               lambda kc, gi, blk=blk: ("hkv", blk, kc),
               KC, [(0, 512)], CS_GMIX, D, "n1", [blk % 2])

    def proj_blk(blk):
        g, (c0, n) = blk, KG[blk]
        for cc in range(4):
            wk, wv = ckvw[cc // 2]
            cl = cc % 2
            bank = nbank([2, 3, 4, 5])
            for kc in range(KC):
                mm(ps[:, bank, 0:n], wv[:, kc, cl * 128:(cl + 1) * 128], hkv[:, kc, c0:c0 + n],
                   kc == 0, kc == KC - 1, [wk, ("hkv", g, kc)], bank)
            act(ckv32[:, cc, c0:c0 + n], ps[:, bank, 0:n], AF.Copy, [PSK(bank)], [("ckv32", cc, g)])

    norm_blk(0)
    for blk in range(4):
        if blk + 1 < 4:
            norm_blk(blk + 1)
        proj_blk(blk)
    checkpoint(1, [("hkv", hkv)])
    A.release("xblk")

    ckvn = A.alloc("ckvn", [128, 4, T], BF16)
    kpe = A.alloc("kpe", [128, T], BF16)
    P.add("vector", lambda e: e.memset(kpe[64:128, :], 0.0), writes=[("kpe", "z")])
    norm_T(lambda kc, c0, n: ckv32[:, kc, c0:c0 + n], lambda kc, gi: ("ckv32", kc, gi),
           lambda kc, c0, n: ckvn[:, kc, c0:c0 + n], lambda kc, gi: ("ckvn", kc, gi),
           4, KG, CS_GKV, 512, "n2", [0, 1])
    checkpoint(21, [("ckvn", ckvn)])
    A.release("ckv32")

    dv = A.alloc("dv", [128, 16, 1024], BF16)
    for cb in range(4):
        wk, wv = wload(w_in[:, C_DV + cb * 256:C_DV + (cb + 1) * 256], KC, 256)
        for tp in range(8):
            bank = nbank([2, 3, 4, 5])
            for sub in range(2):
                tc = tp * 2 + sub
                for kc in range(KC):
                    mm(ps[:, bank, sub * 256:(sub + 1) * 256], hkv[:, kc, tc * 128:(tc + 1) * 128], wv[:, kc, :],
                       kc == 0, kc == KC - 1, [wk, ("hkv", tc // 4, kc)], bank)
            act(dv[:, tp * 2:tp * 2 + 2, cb * 256:(cb + 1) * 256],
                ps[:, bank, :].rearrange("p (a b) -> p a b", a=2), AF.Copy, [PSK(bank)], [("dv", tp, cb)])
    cos_kv, sin_kv = rope_tables(poskv_d, T, "cos_kv", "sin_kv")
    lamv = A.alloc("lamv", [128, 256], F32)
    lamt = A.alloc("lamt", [128, 128], F32)
    dma("sync", lamv[:], lamv_d[:, :], [], ["lamv"], "lamv")
    tt(lamt[:, 0:64], lamv[:, 0:64], lamv[:, 64:128], ALU.mult, ["lamv"], [("lamt", 0)])
    tt(lamt[:, 64:128], lamv[:, 128:192], lamv[:, 192:256], ALU.mult, ["lamv"], [("lamt", 1)])
    P.add("vector", lambda e: e.reduce_sum(out=lamc[:, 2:3], in_=lamt[:, 0:64], axis=AX.X),
          reads=[("lamt", 0)], writes=[("lamc", 2)])
    P.add("vector", lambda e: e.reduce_sum(out=lamc[:, 3:4], in_=lamt[:, 64:128], axis=AX.X),
          reads=[("lamt", 1)], writes=[("lamc", 3)])
    act(lamc[:, 2:4], lamc[:, 2:4], AF.Exp, [("lamc", 2), ("lamc", 3)], [("lamc", 2), ("lamc", 3)])
    tt(lamc[:, 0:1], lamc[:, 2:3], lamc[:, 3:4], ALU.subtract, [("lamc", 2), ("lamc", 3)], [("lamc", 0)])
    ts(lamc[:, 0:1], lamc[:, 0:1], float(LAM_INIT), None, ALU.add, None, [("lamc", 0)], [("lamc", 0)])
    ts(lamc[:, 1:2], lamc[:, 0:1], -1.0, None, ALU.mult, None, [("lamc", 0)], [("lamc", 1)])
    A.release("lamv")
    A.release("lamt")
    rstate["t"] = A.alloc("ropet", [128, 2, 1280], F32)
    dk = A.alloc("dk", [128, 8, T], BF16)

    wk, wv = wload(w_in[:, C_KPE:C_KPE + 64], KC, 64)
    for g in range(4):
        c0, n = KG[g]
        bank = nbank([2, 3, 4, 5])
        KCX = KC
        for kc in range(KCX):
            mm(ps[0:64, bank, 0:n], wv[:, kc, 0:64], hkv[:, kc, c0:c0 + n], kc == 0, kc == KCX - 1,
               [wk, ("hkv", g, kc)], bank)
        rope(bank, 6 + g % 2, 64, n,
             cos_kv[0:64, c0:c0 + n], sin_kv[0:64, c0:c0 + n], ["cos_kv", "sin_kv"],
             kpe[0:64, c0:c0 + n], ("kpe", g))

    checkpoint(22, [("ckvn", ckvn), ("kpe", kpe)])
    rope_flush()
    for blk2 in range(4):
        wk, wv = wload(w_in[:, C_DK + blk2 * 256:C_DK + (blk2 + 1) * 256], KC, 256)
        for cl in range(2):
            h = blk2 * 2 + cl
            for g, (c0, n) in enumerate(KG):
                bank = nbank([2, 3, 4, 5])
                for kc in range(KC):
                    mm(ps[:, bank, 0:n], wv[:, kc, cl * 128:(cl + 1) * 128], hkv[:, kc, c0:c0 + n],
                       kc == 0, kc == KC - 1, [wk, ("hkv", g, kc)], bank)
                rope(bank, 6 + g % 2, 128, n, cos_kv[:, c0:c0 + n], sin_kv[:, c0:c0 + n], ["cos_kv", "sin_kv"],
                     dk[:, h, c0:c0 + n], ("dk", h, g))

    rope_flush()
    checkpoint(2, [("ckvn", ckvn), ("kpe", kpe), ("dk", dk), ("dv", dv)])
    rope_flush()
    A.release("ropet")
    A.release("wsl")
    cos_q = A.alloc("cos_q", [128, Q], F32, top=True)
    sin_q = A.alloc("sin_q", [128, Q], F32, top=True)
    vcopy(cos_q[:], cos_kv[:, 0:Q], ["cos_kv"], ["cos_q"])
    vcopy(sin_q[:], sin_kv[:, 0:Q], ["sin_kv"], ["sin_q"])
    A.release("cos_kv")
    A.release("sin_kv")
    hq_lo = A.alloc("hq", [128, 8, Q], BF16)
    hq_hi = A.alloc("hqh", [128, 8, Q], BF16)

    def hqv(kc, c0, n):
        return (hq_lo if kc < 8 else hq_hi)[:, kc % 8, c0:c0 + n]

    def hqk(g, kc):
        return ("hq" if kc < 8 else "hqh", g, kc)

    for gi, (c0, n) in enumerate(QG):
        blks = sorted(set([c0 // 512, (c0 + n - 1) // 512]))
        for half, t_ in enumerate((hq_lo, hq_hi)):
            vcopy(t_[:, :, c0:c0 + n], hkv[:, half * 8:half * 8 + 8, c0:c0 + n],
                  [("hkv", b_, kc) for b_ in blks for kc in range(half * 8, half * 8 + 8)],
                  [hqk(gi, kc) for kc in range(half * 8, half * 8 + 8)])
    A.release("hkv")
    wstate["wsl"] = A.alloc("wsl", [128, NSLOT, SLOT_ELEMS], BF16)
    rstate["t"] = A.alloc("ropet", [128, 2, 1280], F32)

    def hq_keys(g):
        return [("hq", g, kc) for kc in range(KC)]

    O_BANK, SUM_BANK = 6, 7
    LOOK = 2
    ast = {"i": 0, "c": 0}
    deferred = []

    def attn_alloc():
        return (A.alloc("pT", [128, 3, 2, 342], BF16), A.alloc("osb", [128, 2, 342], F32),
                A.alloc("ssb", [128, 2, 342], F32))

    def attn_core(bufs, nkc, n, qk_list, v_of, scale, after):
        pT, osb, ssb = bufs
        npairs = nkc // 2
        pend = []

        def s_stage(p):
            sp = p % 3
            for j in range(2):
                kc = 2 * p + j
                sb = 2 * sp + j
                for i, (lo, rhs, ko) in enumerate(qk_list):
                    mm(ps[:, sb, 0:n], lo(kc), rhs[0], i == 0, i == len(qk_list) - 1, ko(kc) + rhs[1], sb)
            act(pT[:, sp, :, 0:n], ps[:, 2 * sp:2 * sp + 2, 0:n], AF.Exp, [PSK(2 * sp), PSK(2 * sp + 1)],
                [("pT", sp)], scale=float(scale))
            return sp

        def pv_stage(p, sp):
            for j in range(2):
                kc = 2 * p + j
                vl, vk = v_of(kc)
                mm(ps[:, O_BANK, 0:n], vl, pT[:, sp, j, 0:n], kc == 0, kc == nkc - 1, vk + [("pT", sp)], O_BANK)
                mm(ps[:, SUM_BANK, 0:n], ones[:, :], pT[:, sp, j, 0:n], kc == 0, kc == nkc - 1,
                   ["ones", ("pT", sp)], SUM_BANK)

        for p in range(npairs):
            pend.append((p, s_stage(p)))
            if p == min(4, npairs - 1):
                while deferred:
                    deferred.pop(0)()
            if len(pend) > LOOK:
                pv_stage(*pend.pop(0))
        while pend:
            pv_stage(*pend.pop(0))
        cb = ast["c"] % 2
        ast["c"] += 1
        vcopy(osb[:, cb, 0:n], ps[:, O_BANK, 0:n], [PSK(O_BANK)], [("osb", cb)])
        vcopy(ssb[:, cb, 0:n], ps[:, SUM_BANK, 0:n], [PSK(SUM_BANK)], [("ssb", cb)])
        recip(ssb[:, cb, 0:n], ssb[:, cb, 0:n], [("ssb", cb)], [("ssb", cb)])
        after(osb[:, cb, 0:n], ("osb", cb), ssb[:, cb, 0:n], ("ssb", cb))

    def attn_stream(bufs, calls):
        pT, osb, ssb = bufs
        jobs = [(ci, p) for ci, c in enumerate(calls) for p in range(c["nkc"] // 2)]

        def s_stage(ji):
            ci, p = jobs[ji]
            c = calls[ci]
            n = c["n"]
            if p == 0 and c.get("pre"):
                c["pre"]()
            sp = ji % 3
            for j in range(2):
                kc = 2 * p + j
                sb = 2 * sp + j
                for i, (lo, rhs, ko) in enumerate(c["qk"]):
                    mm(ps[:, sb, 0:n], lo(kc), rhs[0], i == 0, i == len(c["qk"]) - 1, ko(kc) + rhs[1], sb)
            act(pT[:, sp, :, 0:n], ps[:, 2 * sp:2 * sp + 2, 0:n], AF.Exp, [PSK(2 * sp), PSK(2 * sp + 1)],
                [("pT", sp)], scale=float(c["scale"]))

        def pv_stage(ji):
            ci, p = jobs[ji]
            c = calls[ci]
            n, nkc = c["n"], c["nkc"]
            sp = ji % 3
            for j in range(2):
                kc = 2 * p + j
                vl, vk = c["v_of"](kc)
                mm(ps[:, O_BANK, 0:n], vl, pT[:, sp, j, 0:n], kc == 0, kc == nkc - 1, vk + [("pT", sp)], O_BANK)
                mm(ps[:, SUM_BANK, 0:n], ones[:, :], pT[:, sp, j, 0:n], kc == 0, kc == nkc - 1,
                   ["ones", ("pT", sp)], SUM_BANK)
            if p == nkc // 2 - 1:
                cb = ast["c"] % 2
                ast["c"] += 1
                ast["job"] = ji
                vcopy(osb[:, cb, 0:n], ps[:, O_BANK, 0:n], [PSK(O_BANK)], [("osb", cb)])
                vcopy(ssb[:, cb, 0:n], ps[:, SUM_BANK, 0:n], [PSK(SUM_BANK)], [("ssb", cb)])
                recip(ssb[:, cb, 0:n], ssb[:, cb, 0:n], [("ssb", cb)], [("ssb", cb)])
                c["after"](osb[:, cb, 0:n], ("osb", cb), ssb[:, cb, 0:n], ("ssb", cb))

        pend = []
        for ji in range(len(jobs)):
            pend.append(ji)
            s_stage(ji)
            if len(pend) > LOOK:
                jx = pend.pop(0)
                pv_stage(jx)
                while deferred and ji - deferred[0][1] >= 9:
                    deferred.pop(0)[0](2 * (jx % 3))
        while pend:
            pv_stage(pend.pop(0))

    dq = A.alloc("dq", [128, 8, Q], BF16)
    for blk2 in range(4):
        wk, wv = wload(w_in[:, C_DQ + blk2 * 256:C_DQ + (blk2 + 1) * 256], KC, 256)
        for cl in range(2):
            h = blk2 * 2 + cl
            for g, (c0, n) in enumerate(QG):
                bank = nbank([5, 6])
                for kc in range(KC):
                    mm(ps[:, bank, 0:n], wv[:, kc, cl * 128:(cl + 1) * 128], hqv(kc, c0, n),
                       kc == 0, kc == KC - 1, [wk, hqk(g, kc)], bank)
                rope(bank, 7, 128, n, cos_q[:, c0:c0 + n], sin_q[:, c0:c0 + n], ["cos_q", "sin_q"],
                     dq[:, h, c0:c0 + n], ("dq", h, g))

    rope_flush()
    A.release("ropet")
    A.release("wsl")
    ob = A.alloc("ob", [128, 8, Q], BF16)
    dtmp = A.alloc("dtmp", [128, 2, 3, 342], F32)
    dsq = A.alloc("dsq", [128, 2, 342], BF16)
    abufs = attn_alloc()
    dqm = A.alloc("dqm", [128, 2, 2, Q], BF16)
    for hb_ in range(2):
        P.add("vector", lambda e, hb_=hb_: e.memset(dqm[64:128, hb_, 0, :], 0.0), writes=[("dqm", hb_, "z0")])
        P.add("vector", lambda e, hb_=hb_: e.memset(dqm[0:64, hb_, 1, :], 0.0), writes=[("dqm", hb_, "z1")])
    dpar = {"i": 0}
    dcalls = []
    for h in range(8):
        hb_ = h % 2

        def pre(h=h, hb_=hb_):
            vcopy(dqm[0:64, hb_, 0, :], dq[0:64, h, :], [("dq", h, g_) for g_ in range(3)], [("dqm", hb_, 0)])
            vcopy(dqm[64:128, hb_, 1, :], dq[64:128, h, :], [("dq", h, g_) for g_ in range(3)], [("dqm", hb_, 1)])

        for g, (c0, n) in enumerate(QG):
            pp = dpar["i"] % 2
            dpar["i"] += 1
            for c in range(2):
                def after(O, ok, rs, rsk, c=c, h=h, g=g, c0=c0, n=n, pp=pp):
                    if c == 0:
                        tt(dtmp[:, pp, 0, 0:n], O, rs, ALU.mult, [ok, rsk], [("dtmp", pp, 0)])
                    else:
                        tt(dtmp[:, pp, 1, 0:n], O, rs, ALU.mult, [ok, rsk], [("dtmp", pp, 1)])
                        stt(dtmp[:, pp, 1, 0:n], dtmp[:, pp, 1, 0:n], lamc[:, 1:2], dtmp[:, pp, 0, 0:n], ALU.mult,
                            ALU.add, [("dtmp", pp, 0), ("dtmp", pp, 1), ("lamc", 1)], [("dtmp", pp, 1)])
                        tt(dsq[:, pp, 0:n], dtmp[:, pp, 1, 0:n], dtmp[:, pp, 1, 0:n], ALU.mult, [("dtmp", pp, 1)],
                           [("dsq", pp)])

                        def fin(bank, h=h, g=g, c0=c0, n=n, pp=pp):
                            mm(ps[:, bank, 0:n], ones[:, :], dsq[:, pp, 0:n], True, True, ["ones", ("dsq", pp)], bank)
                            ex = 1.0 - LAM_INIT
                            act(dtmp[:, pp, 2, 0:n], ps[:, bank, 0:n], AF.Ln, [PSK(bank), "cst"], [("dtmp", pp, 2)],
                                bias=ccol(CS_EPS2), scale=float(1.0 / (128.0 * ex * ex)))
                            act(dtmp[:, pp, 2, 0:n], dtmp[:, pp, 2, 0:n], AF.Exp, [("dtmp", pp, 2)], [("dtmp", pp, 2)],
                                scale=-0.5)
                            stt(ob[:, h, c0:c0 + n], dtmp[:, pp, 1, 0:n], ccol(CS_GDIFF), dtmp[:, pp, 2, 0:n],
                                ALU.mult, ALU.mult, [("dtmp", pp, 1), ("dtmp", pp, 2), "cst"], [("ob", h, g)])

                        deferred.append((fin, ast["job"]))

                dcalls.append(dict(
                    nkc=16, n=n, scale=64.0 ** -0.5, after=after, pre=(pre if (g == 0 and c == 0) else None),
                    qk=[(lambda kc, h=h: dk[:, h, kc * 128:(kc + 1) * 128],
                         (dqm[:, hb_, c, c0:c0 + n], [("dqm", hb_, c), ("dqm", hb_, "z0"), ("dqm", hb_, "z1")]),
                         lambda kc, h=h: [("dk", h, kc // 4)])],
                    v_of=lambda kc, h=h: (dv[:, kc, h * 128:(h + 1) * 128], [("dv", kc // 2, h // 2)])))
    attn_stream(abufs, dcalls)
    while deferred:
        deferred.pop(0)[0](0)
    checkpoint(4, [("ob", ob)])
    for n_ in ("dq", "dqm", "dk", "dv", "dtmp", "dsq", "pT", "osb", "ssb"):
        A.release(n_)
    wstate["wsl"] = A.alloc("wsl", [128, NSLOT, SLOT_ELEMS], BF16)
    rstate["t"] = A.alloc("ropet", [128, 2, 1280], F32)

    cq32 = A.alloc("cq32", [128, 4, Q], F32)
    for blk2 in range(2):
        wk, wv = wload(w_in[:, C_CQ + blk2 * 256:C_CQ + (blk2 + 1) * 256], KC, 256)
        for cl in range(2):
            cc = blk2 * 2 + cl
            for g, (c0, n) in enumerate(QG):
                bank = nbank([0, 1, 2, 3, 4, 5])
                for kc in range(KC):
                    mm(ps[:, bank, 0:n], wv[:, kc, cl * 128:(cl + 1) * 128], hqv(kc, c0, n),
                       kc == 0, kc == KC - 1, [wk, hqk(g, kc)], bank)
                act(cq32[:, cc, c0:c0 + n], ps[:, bank, 0:n], AF.Copy, [PSK(bank)], [("cq32", cc, g)])
    cqn = A.alloc("cqn", [128, 4, Q], BF16)
    norm_T(lambda kc, c0, n: cq32[:, kc, c0:c0 + n], lambda kc, gi: ("cq32", kc, gi),
           lambda kc, c0, n: cqn[:, kc, c0:c0 + n], lambda kc, gi: ("cqn", kc, gi),
           4, QG, CS_GQ, 512, "n4", [5, 6])
    A.release("cq32")

    oa = A.alloc("oa", [128, 8, Q], BF16)
    qn = A.alloc("qn", [128, 2, Q], BF16)
    qp = A.alloc("qp", [128, 2, Q], BF16)
    P.add("vector", lambda e: e.memset(qp[64:128, :, :], 0.0), writes=[("qp", "z")])
    kn = A.alloc("kn", [128, 2, T], BF16)
    vm = A.alloc("vm", [128, 2, 16, 128], BF16)

    abufs = attn_alloc()

    def mla_prep(h):
        hb = h % 2
        wk, wv = wload(w_uq[:, h * 192:(h + 1) * 192], 4, 192)
        for g, (c0, n) in enumerate(QG):
            bank = nbank([0, 1, 2, 3, 4])
            for kc in range(4):
                mm(ps[:, bank, 0:n], wv[:, kc, 0:128], cqn[:, kc, c0:c0 + n], kc == 0, kc == 3,
                   [wk, ("cqn", kc, g)], bank)
            act(qn[:, hb, c0:c0 + n], ps[:, bank, 0:n], AF.Copy, [PSK(bank)], [("qn", hb, g)])
            bank = nbank([0, 1, 2, 3, 4])
            for kc in range(4):
                mm(ps[0:64, bank, 0:n], wv[:, kc, 128:192], cqn[:, kc, c0:c0 + n], kc == 0, kc == 3,
                   [wk, ("cqn", kc, g)], bank)
            rope(bank, 5, 64, n, cos_q[0:64, c0:c0 + n], sin_q[0:64, c0:c0 + n], ["cos_q", "sin_q"],
                 qp[0:64, hb, c0:c0 + n], ("qp", hb, g))
        rope_flush()
        wk, wv = wload(w_ukv[:, h * 256:(h + 1) * 256], 4, 256)
        for g, (c0, n) in enumerate(KG):
            bank = nbank([0, 1, 2, 3, 4])
            for kc in range(4):
                mm(ps[:, bank, 0:n], wv[:, kc, 0:128], ckvn[:, kc, c0:c0 + n], kc == 0, kc == 3,
                   [wk, ("ckvn", kc, g)], bank)
            act(kn[:, hb, c0:c0 + n], ps[:, bank, 0:n], AF.Copy, [PSK(bank)], [("kn", hb, g)])
        for tq in range(4):
            bank = nbank([0, 1, 2, 3, 4])
            for sub in range(4):
                tc = tq * 4 + sub
                for kc in range(4):
                    mm(ps[:, bank, sub * 128:(sub + 1) * 128], ckvn[:, kc, tc * 128:(tc + 1) * 128],
                       wv[:, kc, 128:256], kc == 0, kc == 3, [wk, ("ckvn", kc, tq)], bank)
            act(vm[:, hb, tq * 4:(tq + 1) * 4, :], ps[:, bank, :].rearrange("p (a b) -> p a b", a=4), AF.Copy,
                [PSK(bank)], [("vm", hb, tq)])

    mla_prep(0)
    for h in range(8):
        hb = h % 2
        if h + 1 < 8:
            mla_prep(h + 1)
        mcalls = []
        for g, (c0, n) in enumerate(QG):
            def after(O, ok, rs, rsk, h=h, g=g, c0=c0, n=n):
                tt(oa[:, h, c0:c0 + n], O, rs, ALU.mult, [ok, rsk], [("oa", h, g)])

            mcalls.append(dict(
                nkc=16, n=n, scale=192.0 ** -0.5, after=after,
                qk=[(lambda kc, hb=hb: kn[:, hb, kc * 128:(kc + 1) * 128],
                     (qn[:, hb, c0:c0 + n], [("qn", hb, g)]),
                     lambda kc, hb=hb: [("kn", hb, kc // 4)]),
                    (lambda kc: kpe[:, kc * 128:(kc + 1) * 128],
                     (qp[:, hb, c0:c0 + n], [("qp", hb, g), ("qp", "z")]),
                     lambda kc: [("kpe", kc // 4), ("kpe", "z")])],
                v_of=lambda kc, hb=hb: (vm[:, hb, kc, :], [("vm", hb, kc // 4)])))
        attn_stream(abufs, mcalls)
    checkpoint(5, [("oa", oa), ("cqn", cqn)])
    for n_ in ("cqn", "qn", "qp", "kn", "vm", "ckvn", "kpe", "cos_q", "sin_q", "pT", "osb", "ssb"):
        A.release(n_)

    A.release("wsl")
    wstate["ns"] = 4
    wstate["wsl"] = A.alloc("wsl", [128, 4, SLOT_ELEMS], BF16, top=True)
    mt = A.alloc("mt", [128, KC, Q], BF16, top=True)
    sg = A.alloc("sg", [128, 2, 342], F32)
    mtmp = A.alloc("mtmp", [128, 2, 342], F32)
    t1 = A.alloc("t1", [128, 2, Q], F32)
    ALLB = [0, 1, 2, 3, 4, 5, 6, 7]
    ust = {"i": 0}
    for jb in range(8):
        for br in range(2):
            wo_d, gcol, src_t, srcn = ((w_o_mla, C_GA, oa, "oa"), (w_o_diff, C_GB, ob, "ob"))[br]
            wko, wvo = wload(wo_d[:, jb * 256:(jb + 1) * 256], 8, 256)
            wkg, wvg = wload(w_in[:, gcol + jb * 256:gcol + (jb + 1) * 256], KC, 256)
            for jl in range(2):
                j = jb * 2 + jl
                cs = slice(jl * 128, (jl + 1) * 128)
                for g, (c0, n) in enumerate(QG):
                    sb_ = ust["i"] % 2
                    ust["i"] += 1
                    bg = nbank(ALLB)
                    for kc in range(KC):
                        mm(ps[:, bg, 0:n], wvg[:, kc, cs], hqv(kc, c0, n), kc == 0, kc == KC - 1,
                           [wkg, hqk(g, kc)], bg)
                    act(sg[:, sb_, 0:n], ps[:, bg, 0:n], AF.Sigmoid, [PSK(bg)], [("sg", sb_)])
                    by = nbank(ALLB)
                    for kc in range(8):
                        mm(ps[:, by, 0:n], wvo[:, kc, cs], src_t[:, kc, c0:c0 + n], kc == 0, kc == 7,
                           [wko, (srcn, kc, g)], by)
                    if br == 0:
                        tt(t1[:, jl, c0:c0 + n], ps[:, by, 0:n], sg[:, sb_, 0:n], ALU.mult,
                           [PSK(by), ("sg", sb_)], [("t1", jl, g)])
                    else:
                        tt(mtmp[:, sb_, 0:n], ps[:, by, 0:n], sg[:, sb_, 0:n], ALU.mult,
                           [PSK(by), ("sg", sb_)], [("mtmp", sb_)])
                        tt(mt[:, j, c0:c0 + n], mtmp[:, sb_, 0:n], t1[:, jl, c0:c0 + n], ALU.add,
                           [("mtmp", sb_), ("t1", jl, g)], [("mt", j, g)])
    for n_ in ("hq", "hqh", "oa", "ob", "sg", "mtmp", "t1", "ropet"):
        A.release(n_)

    A.release("wsl")
    xres = A.alloc("xres", [128, KC, Q], F32)
    wstate["wsl"] = A.alloc("wsl", [128, 4, SLOT_ELEMS], BF16)
    for kc_ in range(KC):
        dma("sync", xres[:, kc_, :], xqT[kc_ * 128:(kc_ + 1) * 128, :], [],
            [("xres", kc_, g) for g in range(3)], ("xres", kc_ % 8))

    def proj_add(wsrc_of_block, nk, act_t, actkeys, nblocks=8):
        for jb in range(nblocks):
            wk, wv = wsrc_of_block(jb)
            for jl in range(2):
                j = jb * 2 + jl
                for g, (c0, n) in enumerate(QG):
                    bank = nbank([0, 1, 2, 3, 4, 5, 6, 7])
                    for kc in range(nk):
                        mm(ps[:, bank, 0:n], wv[:, kc, jl * 128:(jl + 1) * 128], act_t[:, kc, c0:c0 + n],
                           kc == 0, kc == nk - 1, [wk] + actkeys(kc, g), bank)
                    tt(xres[:, j, c0:c0 + n], ps[:, bank, 0:n], xres[:, j, c0:c0 + n], ALU.add,
                       [PSK(bank), ("xres", j, g)], [("xres", j, g)])

    checkpoint(6, [("mt", mt)])
    proj_add(lambda jb: wload(w_out[:, jb * 256:(jb + 1) * 256], KC, 256), KC, mt,
             lambda kc, g: [("mt", kc, g)])
    checkpoint(7, [("xres", xres)])
    A.release("mt")

    h2 = A.alloc("h2", [128, KC, Q], BF16)
    norm_T(lambda kc, c0, n: xres[:, kc, c0:c0 + n], lambda kc, gi: ("xres", kc, gi),
           lambda kc, c0, n: h2[:, kc, c0:c0 + n], lambda kc, gi: ("h2", kc, gi),
           KC, QG, CS_GCROSS, D, "n5", [0, 1])
    mem32 = A.alloc("mem32", [128, KC, 256], F32)
    memn = A.alloc("memn", [128, KC, 256], BF16)
    dma("sync", mem32[:], memT[:, :].rearrange("(k p) t -> p k t", p=128), [], ["mem32"], "mem32")
    norm_T(lambda kc, c0, n: mem32[:, kc, c0:c0 + n], lambda kc, gi: "mem32",
           lambda kc, c0, n: memn[:, kc, c0:c0 + n], lambda kc, gi: ("memn", kc),
           KC, [(0, 256)], CS_GMEM, D, "n6", [2])
    A.release("mem32")
    qx = A.alloc("qx", [128, 4, Q], BF16)
    kx = A.alloc("kx", [128, 4, 256], BF16)
    vx = A.alloc("vx", [128, 2, 512], BF16)
    oc = A.alloc("oc", [128, 4, Q], BF16)
    abufs = attn_alloc()
    for hb2 in range(2):
        wk, wv = wload(w_cq[:, hb2 * 256:(hb2 + 1) * 256], KC, 256)
        for hl in range(2):
            h = hb2 * 2 + hl
            for g, (c0, n) in enumerate(QG):
                bank = nbank([0, 1, 2, 3, 4, 5])
                for kc in range(KC):
                    mm(ps[:, bank, 0:n], wv[:, kc, hl * 128:(hl + 1) * 128], h2[:, kc, c0:c0 + n],
                       kc == 0, kc == KC - 1, [wk, ("h2", kc, g)], bank)
                act(qx[:, h, c0:c0 + n], ps[:, bank, 0:n], AF.Copy, [PSK(bank)], [("qx", h, g)])
    for h in range(4):
        wk, wv = wload(w_ckv[:, h * 256:(h + 1) * 256], KC, 256)
        bank = nbank([0, 1, 2, 3, 4, 5])
        for kc in range(KC):
            mm(ps[:, bank, 0:256], wv[:, kc, 0:128], memn[:, kc, :], kc == 0, kc == KC - 1,
               [wk, ("memn", kc)], bank)
        act(kx[:, h, :], ps[:, bank, 0:256], AF.Copy, [PSK(bank)], [("kx", h)])
        bank = nbank([0, 1, 2, 3, 4, 5])
        for tc in range(2):
            for kc in range(KC):
                mm(ps[:, bank, tc * 128:(tc + 1) * 128], memn[:, kc, tc * 128:(tc + 1) * 128], wv[:, kc, 128:256],
                   kc == 0, kc == KC - 1, [wk, ("memn", kc)], bank)
        act(vx[:, :, h * 128:(h + 1) * 128], ps[:, bank, 0:256].rearrange("p (a b) -> p a b", a=2), AF.Copy,
            [PSK(bank)], [("vx", h)])
    pTx, osbx, ssbx = abufs
    xcalls = [(h, g, c0, n) for h in range(4) for g, (c0, n) in enumerate(QG)]
    xscale = 128.0 ** -0.5

    def xs_stage(i):
        h, g, c0, n = xcalls[i]
        sp = i % 3
        for j in range(2):
            mm(ps[:, 2 * sp + j, 0:n], kx[:, h, j * 128:(j + 1) * 128], qx[:, h, c0:c0 + n], True, True,
               [("kx", h), ("qx", h, g)], 2 * sp + j)
        act(pTx[:, sp, :, 0:n], ps[:, 2 * sp:2 * sp + 2, 0:n], AF.Exp, [PSK(2 * sp), PSK(2 * sp + 1)],
            [("pT", sp)], scale=float(xscale))

    def xpv_stage(i):
        h, g, c0, n = xcalls[i]
        sp = i % 3
        cb = i % 2
        for j in range(2):
            mm(ps[:, O_BANK, 0:n], vx[:, j, h * 128:(h + 1) * 128], pTx[:, sp, j, 0:n], j == 0, j == 1,
               [("vx", h), ("pT", sp)], O_BANK)
            mm(ps[:, SUM_BANK, 0:n], ones[:, :], pTx[:, sp, j, 0:n], j == 0, j == 1, ["ones", ("pT", sp)], SUM_BANK)
        vcopy(osbx[:, cb, 0:n], ps[:, O_BANK, 0:n], [PSK(O_BANK)], [("osb", cb)])
        vcopy(ssbx[:, cb, 0:n], ps[:, SUM_BANK, 0:n], [PSK(SUM_BANK)], [("ssb", cb)])
        recip(ssbx[:, cb, 0:n], ssbx[:, cb, 0:n], [("ssb", cb)], [("ssb", cb)])
        tt(oc[:, h, c0:c0 + n], osbx[:, cb, 0:n], ssbx[:, cb, 0:n], ALU.mult, [("osb", cb), ("ssb", cb)],
           [("oc", h, g)])

    xpend = []
    for i in range(len(xcalls)):
        xpend.append(i)
        xs_stage(i)
        if len(xpend) > LOOK:
            xpv_stage(xpend.pop(0))
    while xpend:
        xpv_stage(xpend.pop(0))
    proj_add(lambda jb: wload(w_co[:, jb * 256:(jb + 1) * 256], 4, 256), 4, oc,
             lambda kc, g: [("oc", kc, g)])
    checkpoint(8, [("xres", xres), ("oc", oc)])
    for n_ in ("h2", "memn", "qx", "kx", "vx", "oc", "pT", "osb", "ssb"):
        A.release(n_)

    h3 = A.alloc("h3", [128, KC, Q], BF16)
    norm_T(lambda kc, c0, n: xres[:, kc, c0:c0 + n], lambda kc, gi: ("xres", kc, gi),
           lambda kc, c0, n: h3[:, kc, c0:c0 + n], lambda kc, gi: ("h3", kc, gi),
           KC, QG, CS_GFFN, D, "n7", [0, 1])
    NQ = 4
    CPQ = 11
    aT = A.alloc("aT", [128, CPQ, Q], BF16)
    ubuf = A.alloc("ubuf", [128, 2, 2, Q + 2], F32)
    cbuf = A.alloc("cbuf", [128, 2, 2, Q], F32)
    for pb in range(2):
        for s_ in range(2):
            P.add("vector", lambda e, pb=pb, s_=s_: e.memset(ubuf[:, pb, s_, 0:1], 0.0), writes=[("ubuf", pb, s_, "l")])
            P.add("vector", lambda e, pb=pb, s_=s_: e.memset(ubuf[:, pb, s_, Q + 1:Q + 2], 0.0),
                  writes=[("ubuf", pb, s_, "r")])
    for qd in range(NQ):
        for cl in range(CPQ):
            jj = qd * CPQ + cl
            pb = jj % 2
            wkg, wvg = wload(w_up[:, jj * 128:(jj + 1) * 128], KC, 128)
            wkv_, wvv = wload(w_up[:, FFN + jj * 128:FFN + (jj + 1) * 128], KC, 128)
            for s_, (wk, wv) in enumerate(((wkg, wvg), (wkv_, wvv))):
                for g, (c0, n) in enumerate(QG):
                    bank = nbank([0, 1, 2, 3, 4, 5, 6, 7])
                    for kc in range(KC):
                        mm(ps[:, bank, 0:n], wv[:, kc, :], h3[:, kc, c0:c0 + n], kc == 0, kc == KC - 1,
                           [wk, ("h3", kc, g)], bank)
                    act(ubuf[:, pb, s_, 1 + c0:1 + c0 + n], ps[:, bank, 0:n], AF.Copy, [PSK(bank)],
                        [("ubuf", pb, s_, g)])
                ukeys = [("ubuf", pb, s_, g) for g in range(3)] + [("ubuf", pb, s_, "l"), ("ubuf", pb, s_, "r")]
                col = s_ * 44 + jj
                ts(cbuf[:, pb, s_, :], ubuf[:, pb, s_, 1:Q + 1], ccol(CS_CW + 1 * 88 + col), ccol(CS_CB + col),
                   ALU.mult, ALU.add, ukeys + ["cst"], [("cbuf", pb, s_)])
                stt(cbuf[:, pb, s_, :], ubuf[:, pb, s_, 0:Q], ccol(CS_CW + 0 * 88 + col), cbuf[:, pb, s_, :],
                    ALU.mult, ALU.add, ukeys + ["cst", ("cbuf", pb, s_)], [("cbuf", pb, s_)])
                stt(cbuf[:, pb, s_, :], ubuf[:, pb, s_, 2:Q + 2], ccol(CS_CW + 2 * 88 + col), cbuf[:, pb, s_, :],
                    ALU.mult, ALU.add, ukeys + ["cst", ("cbuf", pb, s_)], [("cbuf", pb, s_)])
            act(cbuf[:, pb, 0, :], cbuf[:, pb, 0, :], AF.Silu, [("cbuf", pb, 0)], [("cbuf", pb, 0)])
            tt(aT[:, cl, :], cbuf[:, pb, 0, :], cbuf[:, pb, 1, :], ALU.mult, [("cbuf", pb, 0), ("cbuf", pb, 1)],
               [("aT", cl)])
        proj_add(lambda jb, qd=qd: wload(w_down[qd * CPQ * 128:(qd + 1) * CPQ * 128, jb * 256:(jb + 1) * 256],
                                         CPQ, 256),
                 CPQ, aT, lambda kc, g: [("aT", kc)])
    checkpoint(9, [("xres", xres)])
    for n_ in ("h3", "aT", "ubuf", "cbuf"):
        A.release(n_)

    norm_T(lambda kc, c0, n: xres[:, kc, c0:c0 + n], lambda kc, gi: ("xres", kc, gi),
           lambda kc, c0, n: xres[:, kc, c0:c0 + n], lambda kc, gi: ("xres", kc, gi),
           KC, QG, CS_GFIN, D, "n8", [0, 1])
    for kq in range(4):
        dma("sync", outT[kq * 512:(kq + 1) * 512, :].rearrange("(k p) t -> p k t", p=128),
            xres[:, kq * 4:(kq + 1) * 4, :],
            [("xres", kc, g) for kc in range(kq * 4, kq * 4 + 4) for g in range(3)], [], ("out", kq),
            is_output=True)
    P.finish("sync")
    P.emit(nc)


_NC_CACHE = {}


def _host_inputs(inp):
    x = np.asarray(inp["x"], dtype=np.float32)
    mem = np.asarray(inp["mem"], dtype=np.float32)
    pos = np.asarray(inp["positions"], dtype=np.int32)

    def col(v, n):
        return np.asarray(v, np.float32).reshape(n, 128).T

    cst = np.zeros((128, NCST), np.float32)
    cst[:, CS_GMIX:CS_GMIX + 16] = col(inp["g_mix_norm"][0], 16)
    cst[:, CS_GCROSS:CS_GCROSS + 16] = col(inp["g_cross_norm"][0], 16)
    cst[:, CS_GMEM:CS_GMEM + 16] = col(inp["g_mem_norm"][0], 16)
    cst[:, CS_GFFN:CS_GFFN + 16] = col(inp["g_ffn_norm"][0], 16)
    cst[:, CS_GFIN:CS_GFIN + 16] = col(inp["g_final"], 16)
    cst[:, CS_GQ:CS_GQ + 4] = col(inp["g_q_norm"][0], 4)
    cst[:, CS_GKV:CS_GKV + 4] = col(inp["g_kv_norm"][0], 4)
    cst[:, CS_GDIFF] = np.asarray(inp["g_diff_sub"][0], np.float32)
    p = np.arange(128)
    inv = (10000.0 ** (-np.arange(0, 64, 2, dtype=np.float32) / np.float32(64))).astype(np.float32)
    cst[:, CS_INV] = inv[p % 32]
    cst[:, CS_SIGN] = np.where((p % 64) < 32, -1.0, 1.0)
    cw = np.asarray(inp["conv_w"][0], np.float32)
    for k in range(3):
        cst[:, CS_CW + k * 88:CS_CW + (k + 1) * 88] = col(cw[k], 88)
    cst[:, CS_CB:CS_CB + 88] = col(inp["conv_b"][0], 88)
    cst[:, CS_EPS1] = EPS
    cst[:, CS_EPS2] = EPS / ((1.0 - LAM_INIT) ** 2)
    lamv = np.concatenate([np.asarray(inp[k][0], np.float32) for k in
                           ("lambda_q1", "lambda_k1", "lambda_q2", "lambda_k2")])[None, :].repeat(128, 0)
    cmat = np.zeros((128, 256), np.float32)
    perm = np.where((p % 64) < 32, p + 32, p - 32)
    cmat[perm, p] = 1.0
    cmat[p, 128 + p] = 1.0
    shared = {
        "cst": cst, "lamv": np.ascontiguousarray(lamv), "cmat": cmat,
        "w_in": np.ascontiguousarray(inp["w_in"][0]), "w_uq": np.ascontiguousarray(inp["w_uq"][0]),
        "w_ukv": np.ascontiguousarray(inp["w_ukv"][0]), "w_o_mla": np.ascontiguousarray(inp["w_o_mla"][0]),
        "w_o_diff": np.ascontiguousarray(inp["w_o_diff"][0]), "w_out": np.ascontiguousarray(inp["w_out"][0]),
        "w_cross_q": np.ascontiguousarray(inp["w_cross_q"][0]),
        "w_cross_kv": np.ascontiguousarray(inp["w_cross_kv"][0]),
        "w_cross_o": np.ascontiguousarray(inp["w_cross_o"][0]), "w_up": np.ascontiguousarray(inp["w_up"][0]),
        "w_down": np.ascontiguousarray(inp["w_down"][0]),
    }
    shared = {k: np.asarray(v, np.float32) for k, v in shared.items()}
    in_maps = []
    for c in range(8):
        b, half = divmod(c, 2)
        q0 = half * 1023
        m = dict(shared)
        order = np.concatenate([np.arange(q0, q0 + Q), np.arange(0, q0), np.arange(q0 + Q, 2048)])
        m["xkvT"] = np.ascontiguousarray(x[b][order].T)
        m["xqT"] = np.ascontiguousarray(x[b, q0:q0 + Q].T)
        m["memT"] = np.ascontiguousarray(mem[b].T)
        m["poskv"] = np.ascontiguousarray(pos[b][order][None, :].repeat(128, 0))
        m["posq"] = np.ascontiguousarray(pos[b, q0:q0 + Q][None, :].repeat(128, 0))
        in_maps.append(m)
    return in_maps


def kernel(**inp):
    if "nc" not in _NC_CACHE:
        _NC_CACHE["nc"] = build_nc()
    nc = _NC_CACHE["nc"]
    in_maps = _host_inputs(inp)
    res = run_bass_kernel_spmd(nc, in_maps, core_ids=list(range(8)))
    out = np.empty((4, 2048, D), np.float32)
    for c in range(8):
        b, half = divmod(c, 2)
        o = res.results[c]["outT"]
        if half == 0:
            out[b, 0:1024, :] = o[:, 0:1024].T
        else:
            out[b, 1024:2048, :] = o[:, 1:1025].T
    return out
```

```python
import math
from contextlib import ExitStack

import numpy as np
import concourse.bass as bass
import concourse.mybir as mybir
from concourse.bass_utils import run_bass_kernel_spmd

F32 = mybir.dt.float32
BF16 = mybir.dt.bfloat16
I32 = mybir.dt.int32
U8 = mybir.dt.uint8
AF = mybir.ActivationFunctionType
ALU = mybir.AluOpType
AX = mybir.AxisListType

ENG_NAMES = ["tensor", "vector", "scalar", "gpsimd", "sync"]
SAME_ENGINE_SYNC = True
_DBG_EMIT = False

D = 2048
KC = 16
T = 2048
Q = 1025
QG = [(0, 342), (342, 342), (684, 341)]
KG = [(i * 512, 512) for i in range(4)]
FFN = 5632
EPS = 1e-6
C_CQ, C_CKV, C_KPE, C_DQ, C_DK, C_DV, C_GA, C_GB = 0, 512, 1024, 1088, 2112, 3136, 4160, 6208
CS_GMIX, CS_GCROSS, CS_GMEM, CS_GFFN, CS_GFIN = 0, 16, 32, 48, 64
CS_GQ, CS_GKV, CS_GDIFF, CS_INV, CS_SIGN = 80, 84, 88, 89, 90
CS_CW, CS_CB = 91, 91 + 264
CS_EPS1, CS_EPS2 = 91 + 264 + 88, 91 + 264 + 89
NCST = 91 + 264 + 90
LAM_INIT = 0.8 - 0.6 * math.exp(-0.3 * 0)


def _bufname(k):
    return k if isinstance(k, str) else k[0]


class Op:
    __slots__ = ("id", "eng", "fn", "deps", "is_dma", "dma_sem", "dma_val", "sig", "cnt")


class Prog:
    def __init__(self):
        self.ops = []
        self.by_eng = {e: [] for e in ENG_NAMES}
        self.lastw = {}
        self.readers = {}
        self.dma_cnt = {}
        self.dma_last = {}
        self.out_dmas = []
        self.pending = {}
        self.bufkeys = {}

    def add(self, eng, fn, reads=(), writes=(), dma_sem=None, is_output=False, extra_deps=()):
        op = Op()
        op.id = len(self.ops)
        op.eng = eng
        op.fn = fn
        deps = set(extra_deps)
        if eng != "tensor":
            writes = list(writes) + [r for r in reads if isinstance(r, tuple) and r[0] == "ps" and r not in writes]
        for r in reads:
            w = self.lastw.get(r)
            if w is not None:
                deps.add(w)
            pd = self.pending.get(_bufname(r))
            if pd:
                deps |= pd
        for k in writes:
            w = self.lastw.get(k)
            if w is not None:
                deps.add(w)
            for rd in self.readers.get(k, ()):
                deps.add(rd)
            pd = self.pending.get(_bufname(k))
            if pd:
                deps |= pd
        op.is_dma = dma_sem is not None
        op.dma_sem = dma_sem
        op.dma_val = 0
        if op.is_dma:
            n = self.dma_cnt.get(dma_sem, 0) + 1
            self.dma_cnt[dma_sem] = n
            op.dma_val = 16 * n
            prev = self.dma_last.get(dma_sem)
            if prev is not None:
                deps.add(prev)
            self.dma_last[dma_sem] = op.id
            if is_output:
                self.out_dmas.append(op.id)
        deps.discard(op.id)
        op.deps = deps
        op.sig = False
        op.cnt = 0
        for k in writes:
            self.lastw[k] = op.id
            self.readers[k] = []
            self.bufkeys.setdefault(_bufname(k), set()).add(k)
        for r in reads:
            if r not in writes:
                self.readers.setdefault(r, []).append(op.id)
            self.bufkeys.setdefault(_bufname(r), set()).add(r)
        self.ops.append(op)
        self.by_eng[eng].append(op)
        return op

    def touch_ops(self, name):
        s = set()
        for k in self.bufkeys.get(name, ()):
            w = self.lastw.pop(k, None)
            if w is not None:
                s.add(w)
            for r in self.readers.pop(k, ()):
                s.add(r)
        s |= self.pending.pop(name, set())
        self.bufkeys.pop(name, None)
        return s

    def finish(self, eng="sync"):
        self.add(eng, None, extra_deps=list(self.out_dmas))

    def emit(self, nc):
        needed = set()
        for op in self.ops:
            agg = {}
            for d in op.deps:
                p = self.ops[d]
                if p.is_dma:
                    key = ("d", p.dma_sem)
                else:
                    if p.fn is None:
                        continue
                    if p.eng == op.eng and p.eng == "tensor" and not op.is_dma:
                        continue
                    if p.eng == op.eng and not op.is_dma and not SAME_ENGINE_SYNC:
                        continue
                    key = ("e", p.eng)
                if key not in agg or agg[key] < d:
                    agg[key] = d
            op.deps = set(agg.values())
            for d in op.deps:
                if not self.ops[d].is_dma:
                    needed.add(d)
        for e in ENG_NAMES:
            c = 0
            for op in self.by_eng[e]:
                if op.is_dma or op.fn is None:
                    continue
                if op.id in needed:
                    c += 1
                    op.cnt = c
                    op.sig = True
        with ExitStack() as st:
            esem = {e: st.enter_context(nc.semaphore("es_" + e)) for e in ENG_NAMES}
            dsem = {}
            for i, k in enumerate(self.dma_cnt):
                dsem[k] = st.enter_context(nc.semaphore("ds_%d" % i))
            block = st.enter_context(nc.Block())
            prog = self

            def run(engh, e):
                waited = {}
                for op in prog.by_eng[e]:
                    for d in sorted(op.deps):
                        p = prog.ops[d]
                        if p.is_dma:
                            key, val, sem = ("d", p.dma_sem), p.dma_val, dsem[p.dma_sem]
                        else:
                            key, val, sem = ("e", p.eng), p.cnt, esem[p.eng]
                        if waited.get(key, 0) >= val:
                            continue
                        engh.wait_ge(sem, val)
                        waited[key] = val
                        if _DBG_EMIT:
                            print("  [%s] op%d wait %s >= %d" % (e, op.id, key, val))
                    if _DBG_EMIT:
                        print("[%s] op%d %s sig=%s cnt=%d dma=%s" % (e, op.id, "none" if op.fn is None else "", op.sig, op.cnt, op.dma_sem if op.is_dma else ""))
                    if op.fn is None:
                        continue
                    ins = op.fn(engh)
                    if op.is_dma:
                        ins.then_inc(dsem[op.dma_sem], 16)
                    elif op.sig:
                        ins.then_inc(esem[e], 1)

            @block.tensor
            def _(eng):
                run(eng, "tensor")

            @block.vector
            def _(eng):
                run(eng, "vector")

            @block.scalar
            def _(eng):
                run(eng, "scalar")

            @block.gpsimd
            def _(eng):
                run(eng, "gpsimd")

            @block.sync
            def _(eng):
                run(eng, "sync")


_DT_SIZE = {F32: 4, BF16: 2, I32: 4, U8: 1}


class Arena:
    def __init__(self, nc, P, base, size):
        self.nc, self.P = nc, P
        self.free = [(base, size)]
        self.live = {}
        self.retired = []
        self.n = 0

    def alloc(self, name, shape, dtype, top=False):
        nbytes = _DT_SIZE[dtype]
        for s in shape[1:]:
            nbytes *= s
        nbytes = (nbytes + 63) // 64 * 64
        order = range(len(self.free) - 1, -1, -1) if top else range(len(self.free))
        for i in order:
            (o, s) = self.free[i]
            if s >= nbytes:
                if s == nbytes:
                    self.free.pop(i)
                elif top:
                    self.free[i] = (o, s - nbytes)
                    o = o + s - nbytes
                else:
                    self.free[i] = (o + nbytes, s - nbytes)
                break
        else:
            raise RuntimeError("SBUF arena full allocating %s (%d B); live=%s free=%s" % (
                name, nbytes, {k: v[1] for k, v in self.live.items()}, self.free))
        self.live[name] = (o, nbytes)
        deps = set()
        keep = []
        for (ro, rs, ops) in self.retired:
            if ro < o + nbytes and o < ro + rs:
                deps |= ops
            keep.append((ro, rs, ops))
        self.retired = keep
        if deps:
            self.P.pending[name] = deps
        self.n += 1
        return self.nc.alloc_sbuf_tensor_at("%s_%d" % (name, self.n), list(shape), dtype, offset=o)

    def release(self, name):
        o, s = self.live.pop(name)
        ops = self.P.touch_ops(name)
        self.retired.append((o, s, ops))
        fl = sorted(self.free + [(o, s)])
        merged = []
        for (a, b) in fl:
            if merged and merged[-1][0] + merged[-1][1] == a:
                merged[-1] = (merged[-1][0], merged[-1][1] + b)
            else:
                merged.append((a, b))
        self.free = merged


class _Stop(Exception):
    pass


def build_nc(stop_after=None):
    nc = bass.Bass("TRN2", target_bir_lowering=False)
    P = Prog()
    try:
        _build_body(nc, P, stop_after)
    except _Stop:
        pass
    return nc


def _build_body(nc, P, stop_after):
    def checkpoint(k, items):
        if stop_after != k:
            return
        last = [ops[-1].id for e, ops in P.by_eng.items() if ops]
        for (name, t) in items:
            d = nc.dram_tensor("dbg_" + name, list(t.shape), t.dtype, kind="ExternalOutput").ap()
            idx = tuple(slice(None) for _ in t.shape)
            P.add("sync", lambda e, d=d, t=t, idx=idx: e.dma_start(out=d[idx], in_=t[idx]),
                  dma_sem=("dbg", name), is_output=True, extra_deps=last)
        P.finish("sync")
        P.emit(nc)
        raise _Stop()


    def din(name, shape, dt=F32):
        return nc.dram_tensor(name, list(shape), dt, kind="ExternalInput").ap()

    xkvT = din("xkvT", [D, T])
    xqT = din("xqT", [D, Q])
    memT = din("memT", [D, 256])
    poskv_d = din("poskv", [128, T], I32)
    posq_d = din("posq", [128, Q], I32)
    cst_d = din("cst", [128, NCST])
    lamv_d = din("lamv", [128, 256])
    cmat_d = din("cmat", [128, 256])
    w_in = din("w_in", [D, 8256])
    w_uq = din("w_uq", [512, 1536])
    w_ukv = din("w_ukv", [512, 2048])
    w_o_mla = din("w_o_mla", [1024, D])
    w_o_diff = din("w_o_diff", [1024, D])
    w_out = din("w_out", [D, D])
    w_cq = din("w_cross_q", [D, 512])
    w_ckv = din("w_cross_kv", [D, 1024])
    w_co = din("w_cross_o", [512, D])
    w_up = din("w_up", [D, 2 * FFN])
    w_down = din("w_down", [FFN, D])
    outT = nc.dram_tensor("outT", [D, Q], F32, kind="ExternalOutput").ap()
    dbg_list = []

    base = (nc.sbuf_base + 63) // 64 * 64
    asize = (nc.sbuf_top - base) // 64 * 64
    nc.alloc_sbuf_tensor("arena", [128, asize], U8)
    A = Arena(nc, P, base, asize)
    ps = nc.alloc_psum_tensor("ps", [128, 8, 512], F32)

    def PSK(b):
        return ("ps", b)

    def act(out, in_, func, reads, writes, bias=None, scale=None):
        kw = {}
        if bias is not None:
            kw["bias"] = bias
        if scale is not None:
            kw["scale"] = scale
        P.add("scalar", lambda e: e.activation(out=out, in_=in_, func=func, **kw), reads=reads, writes=writes)

    def tt(out, in0, in1, op, reads, writes):
        P.add("vector", lambda e: e.tensor_tensor(out=out, in0=in0, in1=in1, op=op), reads=reads, writes=writes)

    def ts(out, in0, s1, s2, op0, op1, reads, writes):
        if op1 is None:
            P.add("vector", lambda e: e.tensor_scalar(out=out, in0=in0, scalar1=s1, scalar2=None, op0=op0),
                  reads=reads, writes=writes)
        else:
            P.add("vector", lambda e: e.tensor_scalar(out=out, in0=in0, scalar1=s1, scalar2=s2, op0=op0, op1=op1),
                  reads=reads, writes=writes)

    def stt(out, in0, scalar, in1, op0, op1, reads, writes):
        P.add("vector", lambda e: e.scalar_tensor_tensor(out=out, in0=in0, scalar=scalar, in1=in1, op0=op0, op1=op1),
              reads=reads, writes=writes)

    def vcopy(out, in_, reads, writes):
        P.add("vector", lambda e: e.tensor_copy(out=out, in_=in_), reads=reads, writes=writes)

    def recip(out, in_, reads, writes):
        P.add("vector", lambda e: e.reciprocal(out=out, in_=in_), reads=reads, writes=writes)

    def mm(out, lhsT, rhs, start, stop, reads, bank):
        P.add("tensor", lambda e: e.matmul(out, lhsT=lhsT, rhs=rhs, start=start, stop=stop),
              reads=reads, writes=[PSK(bank)])

    def dma(eng, out, in_, reads, writes, sem, is_output=False):
        P.add(eng, lambda e: e.dma_start(out=out, in_=in_), reads=reads, writes=writes, dma_sem=sem,
              is_output=is_output)

    cst = A.alloc("cst", [128, NCST], F32)
    cmat = A.alloc("cmat", [128, 256], BF16)
    ones = A.alloc("ones", [128, 128], BF16)
    lamc = A.alloc("lamc", [128, 4], F32)
    dma("sync", cst[:], cst_d[:, :], [], ["cst"], "cst")
    dma("gpsimd", cmat[:], cmat_d[:, :], [], ["cmat"], "cmat")
    P.add("vector", lambda e: e.memset(ones[:], 1.0), writes=["ones"])
    Rm = cmat[:, 0:128]

    def ccol(c0, n=1):
        return cst[:, c0:c0 + n]

    NSLOT = 3
    SLOT_ELEMS = 4096
    wstate = {"i": 0, "wsl": None, "ns": NSLOT}

    def wload(src, nk, ncols):
        assert nk * ncols <= SLOT_ELEMS, (nk, ncols)
        s = wstate["i"] % wstate["ns"]
        wstate["i"] += 1
        wsl = wstate["wsl"]
        view = wsl[:, s, 0:nk * ncols].rearrange("p (k n) -> p k n", k=nk)
        dma("gpsimd", view, src.rearrange("(k p) n -> p k n", p=128), [], [("wsl", s)], ("wsl", s))
        return ("wsl", s), view

    TWO_PI = 2.0 * math.pi
    HI = 6.28125
    LO = TWO_PI - HI
    PI_S = 3.141592

    def rope_tables(pos_d, N, cosn, sinn):
        cos_t = A.alloc(cosn, [128, N], F32)
        sin_t = A.alloc(sinn, [128, N], F32)
        pi_ = A.alloc("rt_pi", [128, N], I32)
        r = A.alloc("rt_r", [128, N], F32)
        m = A.alloc("rt_m", [128, N], F32)
        dma("sync", pi_[:], pos_d[:, :], [], ["rt_pi"], "rt_pi")
        vcopy(r[:], pi_[:], ["rt_pi"], ["rt_r"])
        ts(r[:], r[:], ccol(CS_INV), None, ALU.mult, None, ["rt_r", "cst"], ["rt_r"])
        ts(pi_[:], r[:], 1.0 / TWO_PI, None, ALU.mult, None, ["rt_r"], ["rt_pi"])
        vcopy(m[:], pi_[:], ["rt_pi"], ["rt_m"])
        stt(r[:], m[:], -HI, r[:], ALU.mult, ALU.add, ["rt_m", "rt_r"], ["rt_r"])
        stt(r[:], m[:], -LO, r[:], ALU.mult, ALU.add, ["rt_m", "rt_r"], ["rt_r"])

        def wrap(v, vn):
            ts(m[:], v[:], math.pi, -TWO_PI, ALU.is_gt, ALU.mult, [vn], ["rt_m"])
            tt(v[:], v[:], m[:], ALU.add, [vn, "rt_m"], [vn])
            ts(m[:], v[:], -math.pi, TWO_PI, ALU.is_lt, ALU.mult, [vn], ["rt_m"])
            tt(v[:], v[:], m[:], ALU.add, [vn, "rt_m"], [vn])
            ts(v[:], v[:], -PI_S, PI_S, ALU.max, ALU.min, [vn], [vn])

        wrap(r, "rt_r")
        act(sin_t[:], r[:], AF.Sin, ["rt_r", "cst"], [sinn], scale=ccol(CS_SIGN))
        ts(r[:], r[:], math.pi / 2, None, ALU.add, None, ["rt_r"], ["rt_r"])
        wrap(r, "rt_r")
        act(cos_t[:], r[:], AF.Sin, ["rt_r"], [cosn])
        for n_ in ("rt_pi", "rt_r", "rt_m"):
            A.release(n_)
        return cos_t, sin_t

    def norm_T(src, srck, dst, dstk, nk, groups, gcol0, dn, sqn, bank_list, extra=1.0, per_group_keys=True):
        N = groups[-1][0] + groups[-1][1]
        rstd = A.alloc(sqn + "_rstd", [128, N], F32)
        sq = A.alloc(sqn + "_sq", [128, nk, 512], BF16)
        for gi, (c0, n) in enumerate(groups):
            bank = bank_list[gi % len(bank_list)]
            for kc in range(nk):
                act(sq[:, kc, 0:n], src(kc, c0, n), AF.Square, [srck(kc, gi)], [(sqn + "_sq", kc)])
            for kc in range(nk):
                mm(ps[:, bank, 0:n], ones[:, :], sq[:, kc, 0:n], kc == 0, kc == nk - 1,
                   ["ones", (sqn + "_sq", kc)], bank)
            act(rstd[:, c0:c0 + n], ps[:, bank, 0:n], AF.Sqrt, [PSK(bank), "cst"], [(sqn + "_rstd", gi)],
                bias=ccol(CS_EPS1 if extra == 1.0 else CS_EPS2), scale=float(1.0 / (dn * extra * extra)))
            recip(rstd[:, c0:c0 + n], rstd[:, c0:c0 + n], [(sqn + "_rstd", gi)], [(sqn + "_rstd", gi)])
            for kc in range(nk):
                stt(dst(kc, c0, n), src(kc, c0, n), ccol(gcol0 + kc), rstd[:, c0:c0 + n], ALU.mult, ALU.mult,
                    [srck(kc, gi), "cst", (sqn + "_rstd", gi)], [dstk(kc, gi)])
        A.release(sqn + "_rstd")
        A.release(sqn + "_sq")

    rstate = {"i": 0, "t": None}

    rope_pend = []

    def rope_flush():
        while rope_pend:
            rope_pend.pop(0)()

    def rope(bank, bank2, np_, n, cos_ap, sin_ap, tabkeys, out, outk):
        b = rstate["i"] % 2
        rstate["i"] += 1
        ropet = rstate["t"]
        qa = ropet[:, b, 512:768].bitcast(BF16)[0:np_, 0:n]
        t_ = ropet[0:np_, b, 0:n]
        u_ = ropet[0:np_, b, 768:768 + n]
        act(qa, ps[0:np_, bank, 0:n], AF.Copy, [PSK(bank)], [("ropet", b, 0)])
        rope_flush()

        def part_b():
            mm(ps[0:np_, bank2, 0:n], cmat[0:np_, 0:np_], qa, True, True, ["cmat", ("ropet", b, 0)], bank2)
            tt(t_, ps[0:np_, bank, 0:n], cos_ap, ALU.mult, [PSK(bank), ("ropet", b, 0)] + tabkeys, [("ropet", b, 1)])
            tt(u_, ps[0:np_, bank2, 0:n], sin_ap, ALU.mult, [PSK(bank2)] + tabkeys, [("ropet", b, 2)])
            tt(out, u_, t_, ALU.add, [("ropet", b, 1), ("ropet", b, 2)], [outk])

        rope_pend.append(part_b)

    checkpoint(-1, [("cst", cst), ("cmat", cmat), ("ones", ones)])
    bankc = {"i": 0}

    def nbank(lst):
        b = lst[bankc["i"] % len(lst)]
        bankc["i"] += 1
        return b

    hkv = A.alloc("hkv", [128, KC, T], BF16)
    xblk = A.alloc("xblk", [128, 2, KC, 512], F32)
    ckv32 = A.alloc("ckv32", [128, 4, T], F32)
    wstate["wsl"] = A.alloc("wsl", [128, NSLOT, SLOT_ELEMS], BF16, top=True)
    ckvw = [wload(w_in[:, C_CKV + blk2 * 256:C_CKV + (blk2 + 1) * 256], KC, 256) for blk2 in range(2)]
    def norm_blk(blk):
        xb = blk % 2
        for kq in range(4):
            dma("sync", xblk[:, xb, kq * 4:(kq + 1) * 4, :],
                xkvT[kq * 512:(kq + 1) * 512, blk * 512:(blk + 1) * 512].rearrange("(k p) t -> p k t", p=128),
                [], [("xblk", xb, kq)], ("xblk", xb, kq))
        norm_T(lambda kc, c0, n, xb=xb: xblk[:, xb, kc, c0:c0 + n], lambda kc, gi, xb=xb: ("xblk", xb, kc // 4),
               lambda kc, c0, n, blk=blk: hkv[:, kc, blk * 512 + c0:blk * 512 + c0 + n],
               lambda kc, gi, blk=blk: ("hkv", blk, kc),
               KC, [(0, 512)], CS_GMIX, D, "n1", [blk % 2])

    def proj_blk(blk):
        g, (c0, n) = blk, KG[blk]
        for cc in range(4):
            wk, wv = ckvw[cc // 2]
            cl = cc % 2
            bank = nbank([2, 3, 4, 5])
            for kc in range(KC):
                mm(ps[:, bank, 0:n], wv[:, kc, cl * 128:(cl + 1) * 128], hkv[:, kc, c0:c0 + n],
                   kc == 0, kc == KC - 1, [wk, ("hkv", g, kc)], bank)
            act(ckv32[:, cc, c0:c0 + n], ps[:, bank, 0:n], AF.Copy, [PSK(bank)], [("ckv32", cc, g)])

    norm_blk(0)
    for blk in range(4):
        if blk + 1 < 4:
            norm_blk(blk + 1)
        proj_blk(blk)
    checkpoint(1, [("hkv", hkv)])
    A.release("xblk")

    ckvn = A.alloc("ckvn", [128, 4, T], BF16)
    kpe = A.alloc("kpe", [128, T], BF16)
    P.add("vector", lambda e: e.memset(kpe[64:128, :], 0.0), writes=[("kpe", "z")])
    norm_T(lambda kc, c0, n: ckv32[:, kc, c0:c0 + n], lambda kc, gi: ("ckv32", kc, gi),
           lambda kc, c0, n: ckvn[:, kc, c0:c0 + n], lambda kc, gi: ("ckvn", kc, gi),
           4, KG, CS_GKV, 512, "n2", [0, 1])
    checkpoint(21, [("ckvn", ckvn)])
    A.release("ckv32")

    dv = A.alloc("dv", [128, 16, 1024], BF16)
    for cb in range(4):
        wk, wv = wload(w_in[:, C_DV + cb * 256:C_DV + (cb + 1) * 256], KC, 256)
        for tp in range(8):
            bank = nbank([2, 3, 4, 5])
            for sub in range(2):
                tc = tp * 2 + sub
                for kc in range(KC):
                    mm(ps[:, bank, sub * 256:(sub + 1) * 256], hkv[:, kc, tc * 128:(tc + 1) * 128], wv[:, kc, :],
                       kc == 0, kc == KC - 1, [wk, ("hkv", tc // 4, kc)], bank)
            act(dv[:, tp * 2:tp * 2 + 2, cb * 256:(cb + 1) * 256],
                ps[:, bank, :].rearrange("p (a b) -> p a b", a=2), AF.Copy, [PSK(bank)], [("dv", tp, cb)])
    cos_kv, sin_kv = rope_tables(poskv_d, T, "cos_kv", "sin_kv")
    lamv = A.alloc("lamv", [128, 256], F32)
    lamt = A.alloc("lamt", [128, 128], F32)
    dma("sync", lamv[:], lamv_d[:, :], [], ["lamv"], "lamv")
    tt(lamt[:, 0:64], lamv[:, 0:64], lamv[:, 64:128], ALU.mult, ["lamv"], [("lamt", 0)])
    tt(lamt[:, 64:128], lamv[:, 128:192], lamv[:, 192:256], ALU.mult, ["lamv"], [("lamt", 1)])
    P.add("vector", lambda e: e.reduce_sum(out=lamc[:, 2:3], in_=lamt[:, 0:64], axis=AX.X),
          reads=[("lamt", 0)], writes=[("lamc", 2)])
    P.add("vector", lambda e: e.reduce_sum(out=lamc[:, 3:4], in_=lamt[:, 64:128], axis=AX.X),
          reads=[("lamt", 1)], writes=[("lamc", 3)])
    act(lamc[:, 2:4], lamc[:, 2:4], AF.Exp, [("lamc", 2), ("lamc", 3)], [("lamc", 2), ("lamc", 3)])
    tt(lamc[:, 0:1], lamc[:, 2:3], lamc[:, 3:4], ALU.subtract, [("lamc", 2), ("lamc", 3)], [("lamc", 0)])
    ts(lamc[:, 0:1], lamc[:, 0:1], float(LAM_INIT), None, ALU.add, None, [("lamc", 0)], [("lamc", 0)])
    ts(lamc[:, 1:2], lamc[:, 0:1], -1.0, None, ALU.mult, None, [("lamc", 0)], [("lamc", 1)])
    A.release("lamv")
    A.release("lamt")
    rstate["t"] = A.alloc("ropet", [128, 2, 1280], F32)
    dk = A.alloc("dk", [128, 8, T], BF16)

    wk, wv = wload(w_in[:, C_KPE:C_KPE + 64], KC, 64)
    for g in range(4):
        c0, n = KG[g]
        bank = nbank([2, 3, 4, 5])
        KCX = KC
        for kc in range(KCX):
            mm(ps[0:64, bank, 0:n], wv[:, kc, 0:64], hkv[:, kc, c0:c0 + n], kc == 0, kc == KCX - 1,
               [wk, ("hkv", g, kc)], bank)
        rope(bank, 6 + g % 2, 64, n,
             cos_kv[0:64, c0:c0 + n], sin_kv[0:64, c0:c0 + n], ["cos_kv", "sin_kv"],
             kpe[0:64, c0:c0 + n], ("kpe", g))

    checkpoint(22, [("ckvn", ckvn), ("kpe", kpe)])
    rope_flush()
    for blk2 in range(4):
        wk, wv = wload(w_in[:, C_DK + blk2 * 256:C_DK + (blk2 + 1) * 256], KC, 256)
        for cl in range(2):
            h = blk2 * 2 + cl
            for g, (c0, n) in enumerate(KG):
                bank = nbank([2, 3, 4, 5])
                for kc in range(KC):
                    mm(ps[:, bank, 0:n], wv[:, kc, cl * 128:(cl + 1) * 128], hkv[:, kc, c0:c0 + n],
                       kc == 0, kc == KC - 1, [wk, ("hkv", g, kc)], bank)
                rope(bank, 6 + g % 2, 128, n, cos_kv[:, c0:c0 + n], sin_kv[:, c0:c0 + n], ["cos_kv", "sin_kv"],
                     dk[:, h, c0:c0 + n], ("dk", h, g))

    rope_flush()
    checkpoint(2, [("ckvn", ckvn), ("kpe", kpe), ("dk", dk), ("dv", dv)])
    rope_flush()
    A.release("ropet")
    A.release("wsl")
    cos_q = A.alloc("cos_q", [128, Q], F32, top=True)
    sin_q = A.alloc("sin_q", [128, Q], F32, top=True)
    vcopy(cos_q[:], cos_kv[:, 0:Q], ["cos_kv"], ["cos_q"])
    vcopy(sin_q[:], sin_kv[:, 0:Q], ["sin_kv"], ["sin_q"])
    A.release("cos_kv")
    A.release("sin_kv")
    hq_lo = A.alloc("hq", [128, 8, Q], BF16)
    hq_hi = A.alloc("hqh", [128, 8, Q], BF16)

    def hqv(kc, c0, n):
        return (hq_lo if kc < 8 else hq_hi)[:, kc % 8, c0:c0 + n]

    def hqk(g, kc):
        return ("hq" if kc < 8 else "hqh", g, kc)

    for gi, (c0, n) in enumerate(QG):
        blks = sorted(set([c0 // 512, (c0 + n - 1) // 512]))
        for half, t_ in enumerate((hq_lo, hq_hi)):
            vcopy(t_[:, :, c0:c0 + n], hkv[:, half * 8:half * 8 + 8, c0:c0 + n],
                  [("hkv", b_, kc) for b_ in blks for kc in range(half * 8, half * 8 + 8)],
                  [hqk(gi, kc) for kc in range(half * 8, half * 8 + 8)])
    A.release("hkv")
    wstate["wsl"] = A.alloc("wsl", [128, NSLOT, SLOT_ELEMS], BF16)
    rstate["t"] = A.alloc("ropet", [128, 2, 1280], F32)

    def hq_keys(g):
        return [("hq", g, kc) for kc in range(KC)]

    O_BANK, SUM_BANK = 6, 7
    LOOK = 2
    ast = {"i": 0, "c": 0}
    deferred = []

    def attn_alloc():
        return (A.alloc("pT", [128, 3, 2, 342], BF16), A.alloc("osb", [128, 2, 342], F32),
                A.alloc("ssb", [128, 2, 342], F32))

    def attn_core(bufs, nkc, n, qk_list, v_of, scale, after):
        pT, osb, ssb = bufs
        npairs = nkc // 2
        pend = []

        def s_stage(p):
            sp = p % 3
            for j in range(2):
                kc = 2 * p + j
                sb = 2 * sp + j
                for i, (lo, rhs, ko) in enumerate(qk_list):
                    mm(ps[:, sb, 0:n], lo(kc), rhs[0], i == 0, i == len(qk_list) - 1, ko(kc) + rhs[1], sb)
            act(pT[:, sp, :, 0:n], ps[:, 2 * sp:2 * sp + 2, 0:n], AF.Exp, [PSK(2 * sp), PSK(2 * sp + 1)],
                [("pT", sp)], scale=float(scale))
            return sp

        def pv_stage(p, sp):
            for j in range(2):
                kc = 2 * p + j
                vl, vk = v_of(kc)
                mm(ps[:, O_BANK, 0:n], vl, pT[:, sp, j, 0:n], kc == 0, kc == nkc - 1, vk + [("pT", sp)], O_BANK)
                mm(ps[:, SUM_BANK, 0:n], ones[:, :], pT[:, sp, j, 0:n], kc == 0, kc == nkc - 1,
                   ["ones", ("pT", sp)], SUM_BANK)

        for p in range(npairs):
            pend.append((p, s_stage(p)))
            if p == min(4, npairs - 1):
                while deferred:
                    deferred.pop(0)()
            if len(pend) > LOOK:
                pv_stage(*pend.pop(0))
        while pend:
            pv_stage(*pend.pop(0))
        cb = ast["c"] % 2
        ast["c"] += 1
        vcopy(osb[:, cb, 0:n], ps[:, O_BANK, 0:n], [PSK(O_BANK)], [("osb", cb)])
        vcopy(ssb[:, cb, 0:n], ps[:, SUM_BANK, 0:n], [PSK(SUM_BANK)], [("ssb", cb)])
        recip(ssb[:, cb, 0:n], ssb[:, cb, 0:n], [("ssb", cb)], [("ssb", cb)])
        after(osb[:, cb, 0:n], ("osb", cb), ssb[:, cb, 0:n], ("ssb", cb))

    def attn_stream(bufs, calls):
        pT, osb, ssb = bufs
        jobs = [(ci, p) for ci, c in enumerate(calls) for p in range(c["nkc"] // 2)]

        def s_stage(ji):
            ci, p = jobs[ji]
            c = calls[ci]
            n = c["n"]
            if p == 0 and c.get("pre"):
                c["pre"]()
            sp = ji % 3
            for j in range(2):
                kc = 2 * p + j
                sb = 2 * sp + j
                for i, (lo, rhs, ko) in enumerate(c["qk"]):
                    mm(ps[:, sb, 0:n], lo(kc), rhs[0], i == 0, i == len(c["qk"]) - 1, ko(kc) + rhs[1], sb)
            act(pT[:, sp, :, 0:n], ps[:, 2 * sp:2 * sp + 2, 0:n], AF.Exp, [PSK(2 * sp), PSK(2 * sp + 1)],
                [("pT", sp)], scale=float(c["scale"]))

        def pv_stage(ji):
            ci, p = jobs[ji]
            c = calls[ci]
            n, nkc = c["n"], c["nkc"]
            sp = ji % 3
            for j in range(2):
                kc = 2 * p + j
                vl, vk = c["v_of"](kc)
                mm(ps[:, O_BANK, 0:n], vl, pT[:, sp, j, 0:n], kc == 0, kc == nkc - 1, vk + [("pT", sp)], O_BANK)
                mm(ps[:, SUM_BANK, 0:n], ones[:, :], pT[:, sp, j, 0:n], kc == 0, kc == nkc - 1,
                   ["ones", ("pT", sp)], SUM_BANK)
            if p == nkc // 2 - 1:
                cb = ast["c"] % 2
                ast["c"] += 1
                ast["job"] = ji
                vcopy(osb[:, cb, 0:n], ps[:, O_BANK, 0:n], [PSK(O_BANK)], [("osb", cb)])
                vcopy(ssb[:, cb, 0:n], ps[:, SUM_BANK, 0:n], [PSK(SUM_BANK)], [("ssb", cb)])
                recip(ssb[:, cb, 0:n], ssb[:, cb, 0:n], [("ssb", cb)], [("ssb", cb)])
                c["after"](osb[:, cb, 0:n], ("osb", cb), ssb[:, cb, 0:n], ("ssb", cb))

        pend = []
        for ji in range(len(jobs)):
            pend.append(ji)
            s_stage(ji)
            if len(pend) > LOOK:
                jx = pend.pop(0)
                pv_stage(jx)
                while deferred and ji - deferred[0][1] >= 9:
                    deferred.pop(0)[0](2 * (jx % 3))
        while pend:
            pv_stage(pend.pop(0))

    dq = A.alloc("dq", [128, 8, Q], BF16)
    for blk2 in range(4):
        wk, wv = wload(w_in[:, C_DQ + blk2 * 256:C_DQ + (blk2 + 1) * 256], KC, 256)
        for cl in range(2):
            h = blk2 * 2 + cl
            for g, (c0, n) in enumerate(QG):
                bank = nbank([5, 6])
                for kc in range(KC):
                    mm(ps[:, bank, 0:n], wv[:, kc, cl * 128:(cl + 1) * 128], hqv(kc, c0, n),
                       kc == 0, kc == KC - 1, [wk, hqk(g, kc)], bank)
                rope(bank, 7, 128, n, cos_q[:, c0:c0 + n], sin_q[:, c0:c0 + n], ["cos_q", "sin_q"],
                     dq[:, h, c0:c0 + n], ("dq", h, g))

    rope_flush()
    A.release("ropet")
    A.release("wsl")
    ob = A.alloc("ob", [128, 8, Q], BF16)
    dtmp = A.alloc("dtmp", [128, 2, 3, 342], F32)
    dsq = A.alloc("dsq", [128, 2, 342], BF16)
    abufs = attn_alloc()
    dqm = A.alloc("dqm", [128, 2, 2, Q], BF16)
    for hb_ in range(2):
        P.add("vector", lambda e, hb_=hb_: e.memset(dqm[64:128, hb_, 0, :], 0.0), writes=[("dqm", hb_, "z0")])
        P.add("vector", lambda e, hb_=hb_: e.memset(dqm[0:64, hb_, 1, :], 0.0), writes=[("dqm", hb_, "z1")])
    dpar = {"i": 0}
    dcalls = []
    for h in range(8):
        hb_ = h % 2

        def pre(h=h, hb_=hb_):
            vcopy(dqm[0:64, hb_, 0, :], dq[0:64, h, :], [("dq", h, g_) for g_ in range(3)], [("dqm", hb_, 0)])
            vcopy(dqm[64:128, hb_, 1, :], dq[64:128, h, :], [("dq", h, g_) for g_ in range(3)], [("dqm", hb_, 1)])

        for g, (c0, n) in enumerate(QG):
            pp = dpar["i"] % 2
            dpar["i"] += 1
            for c in range(2):
                def after(O, ok, rs, rsk, c=c, h=h, g=g, c0=c0, n=n, pp=pp):
                    if c == 0:
                        tt(dtmp[:, pp, 0, 0:n], O, rs, ALU.mult, [ok, rsk], [("dtmp", pp, 0)])
                    else:
                        tt(dtmp[:, pp, 1, 0:n], O, rs, ALU.mult, [ok, rsk], [("dtmp", pp, 1)])
                        stt(dtmp[:, pp, 1, 0:n], dtmp[:, pp, 1, 0:n], lamc[:, 1:2], dtmp[:, pp, 0, 0:n], ALU.mult,
                            ALU.add, [("dtmp", pp, 0), ("dtmp", pp, 1), ("lamc", 1)], [("dtmp", pp, 1)])
                        tt(dsq[:, pp, 0:n], dtmp[:, pp, 1, 0:n], dtmp[:, pp, 1, 0:n], ALU.mult, [("dtmp", pp, 1)],
                           [("dsq", pp)])

                        def fin(bank, h=h, g=g, c0=c0, n=n, pp=pp):
                            mm(ps[:, bank, 0:n], ones[:, :], dsq[:, pp, 0:n], True, True, ["ones", ("dsq", pp)], bank)
                            ex = 1.0 - LAM_INIT
                            act(dtmp[:, pp, 2, 0:n], ps[:, bank, 0:n], AF.Ln, [PSK(bank), "cst"], [("dtmp", pp, 2)],
                                bias=ccol(CS_EPS2), scale=float(1.0 / (128.0 * ex * ex)))
                            act(dtmp[:, pp, 2, 0:n], dtmp[:, pp, 2, 0:n], AF.Exp, [("dtmp", pp, 2)], [("dtmp", pp, 2)],
                                scale=-0.5)
                            stt(ob[:, h, c0:c0 + n], dtmp[:, pp, 1, 0:n], ccol(CS_GDIFF), dtmp[:, pp, 2, 0:n],
                                ALU.mult, ALU.mult, [("dtmp", pp, 1), ("dtmp", pp, 2), "cst"], [("ob", h, g)])

                        deferred.append((fin, ast["job"]))

                dcalls.append(dict(
                    nkc=16, n=n, scale=64.0 ** -0.5, after=after, pre=(pre if (g == 0 and c == 0) else None),
                    qk=[(lambda kc, h=h: dk[:, h, kc * 128:(kc + 1) * 128],
                         (dqm[:, hb_, c, c0:c0 + n], [("dqm", hb_, c), ("dqm", hb_, "z0"), ("dqm", hb_, "z1")]),
                         lambda kc, h=h: [("dk", h, kc // 4)])],
                    v_of=lambda kc, h=h: (dv[:, kc, h * 128:(h + 1) * 128], [("dv", kc // 2, h // 2)])))
    attn_stream(abufs, dcalls)
    while deferred:
        deferred.pop(0)[0](0)
    checkpoint(4, [("ob", ob)])
    for n_ in ("dq", "dqm", "dk", "dv", "dtmp", "dsq", "pT", "osb", "ssb"):
        A.release(n_)
    wstate["ns"] = 4
    wstate["wsl"] = A.alloc("wsl", [128, 4, SLOT_ELEMS], BF16)
    rstate["t"] = A.alloc("ropet", [128, 2, 1280], F32)

    cq32 = A.alloc("cq32", [128, 4, Q], F32)
    for blk2 in range(2):
        wk, wv = wload(w_in[:, C_CQ + blk2 * 256:C_CQ + (blk2 + 1) * 256], KC, 256)
        for cl in range(2):
            cc = blk2 * 2 + cl
            for g, (c0, n) in enumerate(QG):
                bank = nbank([0, 1, 2, 3, 4, 5])
                for kc in range(KC):
                    mm(ps[:, bank, 0:n], wv[:, kc, cl * 128:(cl + 1) * 128], hqv(kc, c0, n),
                       kc == 0, kc == KC - 1, [wk, hqk(g, kc)], bank)
                act(cq32[:, cc, c0:c0 + n], ps[:, bank, 0:n], AF.Copy, [PSK(bank)], [("cq32", cc, g)])
    cqn = A.alloc("cqn", [128, 4, Q], BF16)
    norm_T(lambda kc, c0, n: cq32[:, kc, c0:c0 + n], lambda kc, gi: ("cq32", kc, gi),
           lambda kc, c0, n: cqn[:, kc, c0:c0 + n], lambda kc, gi: ("cqn", kc, gi),
           4, QG, CS_GQ, 512, "n4", [5, 6])
    A.release("cq32")

    oa = A.alloc("oa", [128, 8, Q], BF16)
    qn = A.alloc("qn", [128, 2, Q], BF16)
    qp = A.alloc("qp", [128, 2, Q], BF16)
    P.add("vector", lambda e: e.memset(qp[64:128, :, :], 0.0), writes=[("qp", "z")])
    kn = A.alloc("kn", [128, 2, T], BF16)
    vm = A.alloc("vm", [128, 2, 16, 128], BF16)

    abufs = attn_alloc()

    def mla_prep(h):
        hb = h % 2
        wk, wv = wload(w_uq[:, h * 192:(h + 1) * 192], 4, 192)
        for g, (c0, n) in enumerate(QG):
            bank = nbank([0, 1, 2, 3, 4])
            for kc in range(4):
                mm(ps[:, bank, 0:n], wv[:, kc, 0:128], cqn[:, kc, c0:c0 + n], kc == 0, kc == 3,
                   [wk, ("cqn", kc, g)], bank)
            act(qn[:, hb, c0:c0 + n], ps[:, bank, 0:n], AF.Copy, [PSK(bank)], [("qn", hb, g)])
            bank = nbank([0, 1, 2, 3, 4])
            for kc in range(4):
                mm(ps[0:64, bank, 0:n], wv[:, kc, 128:192], cqn[:, kc, c0:c0 + n], kc == 0, kc == 3,
                   [wk, ("cqn", kc, g)], bank)
            rope(bank, 5, 64, n, cos_q[0:64, c0:c0 + n], sin_q[0:64, c0:c0 + n], ["cos_q", "sin_q"],
                 qp[0:64, hb, c0:c0 + n], ("qp", hb, g))
        rope_flush()
        wk, wv = wload(w_ukv[:, h * 256:(h + 1) * 256], 4, 256)
        for g, (c0, n) in enumerate(KG):
            bank = nbank([0, 1, 2, 3, 4])
            for kc in range(4):
                mm(ps[:, bank, 0:n], wv[:, kc, 0:128], ckvn[:, kc, c0:c0 + n], kc == 0, kc == 3,
                   [wk, ("ckvn", kc, g)], bank)
            act(kn[:, hb, c0:c0 + n], ps[:, bank, 0:n], AF.Copy, [PSK(bank)], [("kn", hb, g)])
        for tq in range(4):
            bank = nbank([0, 1, 2, 3, 4])
            for sub in range(4):
                tc = tq * 4 + sub
                for kc in range(4):
                    mm(ps[:, bank, sub * 128:(sub + 1) * 128], ckvn[:, kc, tc * 128:(tc + 1) * 128],
                       wv[:, kc, 128:256], kc == 0, kc == 3, [wk, ("ckvn", kc, tq)], bank)
            act(vm[:, hb, tq * 4:(tq + 1) * 4, :], ps[:, bank, :].rearrange("p (a b) -> p a b", a=4), AF.Copy,
                [PSK(bank)], [("vm", hb, tq)])

    mla_prep(0)
    for h in range(8):
        hb = h % 2
        if h + 1 < 8:
            mla_prep(h + 1)
        mcalls = []
        for g, (c0, n) in enumerate(QG):
            def after(O, ok, rs, rsk, h=h, g=g, c0=c0, n=n):
                tt(oa[:, h, c0:c0 + n], O, rs, ALU.mult, [ok, rsk], [("oa", h, g)])

            mcalls.append(dict(
                nkc=16, n=n, scale=192.0 ** -0.5, after=after,
                qk=[(lambda kc, hb=hb: kn[:, hb, kc * 128:(kc + 1) * 128],
                     (qn[:, hb, c0:c0 + n], [("qn", hb, g)]),
                     lambda kc, hb=hb: [("kn", hb, kc // 4)]),
                    (lambda kc: kpe[:, kc * 128:(kc + 1) * 128],
                     (qp[:, hb, c0:c0 + n], [("qp", hb, g), ("qp", "z")]),
                     lambda kc: [("kpe", kc // 4), ("kpe", "z")])],
                v_of=lambda kc, hb=hb: (vm[:, hb, kc, :], [("vm", hb, kc // 4)])))
        attn_stream(abufs, mcalls)
    checkpoint(5, [("oa", oa), ("cqn", cqn)])
    for n_ in ("cqn", "qn", "qp", "kn", "vm", "ckvn", "kpe", "cos_q", "sin_q", "pT", "osb", "ssb"):
        A.release(n_)

    mt = A.alloc("mt", [128, KC, Q], BF16, top=True)
    sg = A.alloc("sg", [128, 2, 342], F32)
    mtmp = A.alloc("mtmp", [128, 2, 342], F32)
    t1 = A.alloc("t1", [128, 2, Q], F32)
    ALLB = [0, 1, 2, 3, 4, 5, 6, 7]
    ust = {"i": 0}
    for jb in range(8):
        for br in range(2):
            wo_d, gcol, src_t, srcn = ((w_o_mla, C_GA, oa, "oa"), (w_o_diff, C_GB, ob, "ob"))[br]
            wko, wvo = wload(wo_d[:, jb * 256:(jb + 1) * 256], 8, 256)
            wkg, wvg = wload(w_in[:, gcol + jb * 256:gcol + (jb + 1) * 256], KC, 256)
            for jl in range(2):
                j = jb * 2 + jl
                cs = slice(jl * 128, (jl + 1) * 128)
                for g, (c0, n) in enumerate(QG):
                    sb_ = ust["i"] % 2
                    ust["i"] += 1
                    bg = nbank(ALLB)
                    for kc in range(KC):
                        mm(ps[:, bg, 0:n], wvg[:, kc, cs], hqv(kc, c0, n), kc == 0, kc == KC - 1,
                           [wkg, hqk(g, kc)], bg)
                    act(sg[:, sb_, 0:n], ps[:, bg, 0:n], AF.Sigmoid, [PSK(bg)], [("sg", sb_)])
                    by = nbank(ALLB)
                    for kc in range(8):
                        mm(ps[:, by, 0:n], wvo[:, kc, cs], src_t[:, kc, c0:c0 + n], kc == 0, kc == 7,
                           [wko, (srcn, kc, g)], by)
                    if br == 0:
                        tt(t1[:, jl, c0:c0 + n], ps[:, by, 0:n], sg[:, sb_, 0:n], ALU.mult,
                           [PSK(by), ("sg", sb_)], [("t1", jl, g)])
                    else:
                        tt(mtmp[:, sb_, 0:n], ps[:, by, 0:n], sg[:, sb_, 0:n], ALU.mult,
                           [PSK(by), ("sg", sb_)], [("mtmp", sb_)])
                        tt(mt[:, j, c0:c0 + n], mtmp[:, sb_, 0:n], t1[:, jl, c0:c0 + n], ALU.add,
                           [("mtmp", sb_), ("t1", jl, g)], [("mt", j, g)])
    for n_ in ("hq", "hqh", "oa", "ob", "sg", "mtmp", "t1", "ropet"):
        A.release(n_)

    A.release("wsl")
    xres = A.alloc("xres", [128, KC, Q], F32)
    wstate["wsl"] = A.alloc("wsl", [128, 4, SLOT_ELEMS], BF16)
    for kc_ in range(KC):
        dma("sync", xres[:, kc_, :], xqT[kc_ * 128:(kc_ + 1) * 128, :], [],
            [("xres", kc_, g) for g in range(3)], ("xres", kc_ % 8))

    def proj_add(wsrc_of_block, nk, act_t, actkeys, nblocks=8):
        for jb in range(nblocks):
            wk, wv = wsrc_of_block(jb)
            for jl in range(2):
                j = jb * 2 + jl
                for g, (c0, n) in enumerate(QG):
                    bank = nbank([0, 1, 2, 3, 4, 5, 6, 7])
                    for kc in range(nk):
                        mm(ps[:, bank, 0:n], wv[:, kc, jl * 128:(jl + 1) * 128], act_t[:, kc, c0:c0 + n],
                           kc == 0, kc == nk - 1, [wk] + actkeys(kc, g), bank)
                    tt(xres[:, j, c0:c0 + n], ps[:, bank, 0:n], xres[:, j, c0:c0 + n], ALU.add,
                       [PSK(bank), ("xres", j, g)], [("xres", j, g)])

    checkpoint(6, [("mt", mt)])
    proj_add(lambda jb: wload(w_out[:, jb * 256:(jb + 1) * 256], KC, 256), KC, mt,
             lambda kc, g: [("mt", kc, g)])
    checkpoint(7, [("xres", xres)])
    A.release("mt")

    h2 = A.alloc("h2", [128, KC, Q], BF16)
    norm_T(lambda kc, c0, n: xres[:, kc, c0:c0 + n], lambda kc, gi: ("xres", kc, gi),
           lambda kc, c0, n: h2[:, kc, c0:c0 + n], lambda kc, gi: ("h2", kc, gi),
           KC, QG, CS_GCROSS, D, "n5", [0, 1])
    mem32 = A.alloc("mem32", [128, KC, 256], F32)
    memn = A.alloc("memn", [128, KC, 256], BF16)
    dma("sync", mem32[:], memT[:, :].rearrange("(k p) t -> p k t", p=128), [], ["mem32"], "mem32")
    norm_T(lambda kc, c0, n: mem32[:, kc, c0:c0 + n], lambda kc, gi: "mem32",
           lambda kc, c0, n: memn[:, kc, c0:c0 + n], lambda kc, gi: ("memn", kc),
           KC, [(0, 256)], CS_GMEM, D, "n6", [2])
    A.release("mem32")
    qx = A.alloc("qx", [128, 4, Q], BF16)
    kx = A.alloc("kx", [128, 4, 256], BF16)
    vx = A.alloc("vx", [128, 2, 512], BF16)
    oc = A.alloc("oc", [128, 4, Q], BF16)
    abufs = attn_alloc()
    for hb2 in range(2):
        wk, wv = wload(w_cq[:, hb2 * 256:(hb2 + 1) * 256], KC, 256)
        for hl in range(2):
            h = hb2 * 2 + hl
            for g, (c0, n) in enumerate(QG):
                bank = nbank([0, 1, 2, 3, 4, 5])
                for kc in range(KC):
                    mm(ps[:, bank, 0:n], wv[:, kc, hl * 128:(hl + 1) * 128], h2[:, kc, c0:c0 + n],
                       kc == 0, kc == KC - 1, [wk, ("h2", kc, g)], bank)
                act(qx[:, h, c0:c0 + n], ps[:, bank, 0:n], AF.Copy, [PSK(bank)], [("qx", h, g)])
    for h in range(4):
        wk, wv = wload(w_ckv[:, h * 256:(h + 1) * 256], KC, 256)
        bank = nbank([0, 1, 2, 3, 4, 5])
        for kc in range(KC):
            mm(ps[:, bank, 0:256], wv[:, kc, 0:128], memn[:, kc, :], kc == 0, kc == KC - 1,
               [wk, ("memn", kc)], bank)
        act(kx[:, h, :], ps[:, bank, 0:256], AF.Copy, [PSK(bank)], [("kx", h)])
        bank = nbank([0, 1, 2, 3, 4, 5])
        for tc in range(2):
            for kc in range(KC):
                mm(ps[:, bank, tc * 128:(tc + 1) * 128], memn[:, kc, tc * 128:(tc + 1) * 128], wv[:, kc, 128:256],
                   kc == 0, kc == KC - 1, [wk, ("memn", kc)], bank)
        act(vx[:, :, h * 128:(h + 1) * 128], ps[:, bank, 0:256].rearrange("p (a b) -> p a b", a=2), AF.Copy,
            [PSK(bank)], [("vx", h)])
    pTx, osbx, ssbx = abufs
    xcalls = [(h, g, c0, n) for h in range(4) for g, (c0, n) in enumerate(QG)]
    xscale = 128.0 ** -0.5

    def xs_stage(i):
        h, g, c0, n = xcalls[i]
        sp = i % 3
        for j in range(2):
            mm(ps[:, 2 * sp + j, 0:n], kx[:, h, j * 128:(j + 1) * 128], qx[:, h, c0:c0 + n], True, True,
               [("kx", h), ("qx", h, g)], 2 * sp + j)
        act(pTx[:, sp, :, 0:n], ps[:, 2 * sp:2 * sp + 2, 0:n], AF.Exp, [PSK(2 * sp), PSK(2 * sp + 1)],
            [("pT", sp)], scale=float(xscale))

    def xpv_stage(i):
        h, g, c0, n = xcalls[i]
        sp = i % 3
        cb = i % 2
        for j in range(2):
            mm(ps[:, O_BANK, 0:n], vx[:, j, h * 128:(h + 1) * 128], pTx[:, sp, j, 0:n], j == 0, j == 1,
               [("vx", h), ("pT", sp)], O_BANK)
            mm(ps[:, SUM_BANK, 0:n], ones[:, :], pTx[:, sp, j, 0:n], j == 0, j == 1, ["ones", ("pT", sp)], SUM_BANK)
        vcopy(osbx[:, cb, 0:n], ps[:, O_BANK, 0:n], [PSK(O_BANK)], [("osb", cb)])
        vcopy(ssbx[:, cb, 0:n], ps[:, SUM_BANK, 0:n], [PSK(SUM_BANK)], [("ssb", cb)])
        recip(ssbx[:, cb, 0:n], ssbx[:, cb, 0:n], [("ssb", cb)], [("ssb", cb)])
        tt(oc[:, h, c0:c0 + n], osbx[:, cb, 0:n], ssbx[:, cb, 0:n], ALU.mult, [("osb", cb), ("ssb", cb)],
           [("oc", h, g)])

    xpend = []
    for i in range(len(xcalls)):
        xpend.append(i)
        xs_stage(i)
        if len(xpend) > LOOK:
            xpv_stage(xpend.pop(0))
    while xpend:
        xpv_stage(xpend.pop(0))
    proj_add(lambda jb: wload(w_co[:, jb * 256:(jb + 1) * 256], 4, 256), 4, oc,
             lambda kc, g: [("oc", kc, g)])
    checkpoint(8, [("xres", xres), ("oc", oc)])
    for n_ in ("h2", "memn", "qx", "kx", "vx", "oc", "pT", "osb", "ssb"):
        A.release(n_)

    h3 = A.alloc("h3", [128, KC, Q], BF16)
    norm_T(lambda kc, c0, n: xres[:, kc, c0:c0 + n], lambda kc, gi: ("xres", kc, gi),
           lambda kc, c0, n: h3[:, kc, c0:c0 + n], lambda kc, gi: ("h3", kc, gi),
           KC, QG, CS_GFFN, D, "n7", [0, 1])
    NQ = 4
    CPQ = 11
    aT = A.alloc("aT", [128, CPQ, Q], BF16)
    ubuf = A.alloc("ubuf", [128, 2, 2, Q + 2], F32)
    cbuf = A.alloc("cbuf", [128, 2, 2, Q], F32)
    for pb in range(2):
        for s_ in range(2):
            P.add("vector", lambda e, pb=pb, s_=s_: e.memset(ubuf[:, pb, s_, 0:1], 0.0), writes=[("ubuf", pb, s_, "l")])
            P.add("vector", lambda e, pb=pb, s_=s_: e.memset(ubuf[:, pb, s_, Q + 1:Q + 2], 0.0),
                  writes=[("ubuf", pb, s_, "r")])
    for qd in range(NQ):
        for cl in range(CPQ):
            jj = qd * CPQ + cl
            pb = jj % 2
            wkg, wvg = wload(w_up[:, jj * 128:(jj + 1) * 128], KC, 128)
            wkv_, wvv = wload(w_up[:, FFN + jj * 128:FFN + (jj + 1) * 128], KC, 128)
            for s_, (wk, wv) in enumerate(((wkg, wvg), (wkv_, wvv))):
                for g, (c0, n) in enumerate(QG):
                    bank = nbank([0, 1, 2, 3, 4, 5, 6, 7])
                    for kc in range(KC):
                        mm(ps[:, bank, 0:n], wv[:, kc, :], h3[:, kc, c0:c0 + n], kc == 0, kc == KC - 1,
                           [wk, ("h3", kc, g)], bank)
                    act(ubuf[:, pb, s_, 1 + c0:1 + c0 + n], ps[:, bank, 0:n], AF.Copy, [PSK(bank)],
                        [("ubuf", pb, s_, g)])
                ukeys = [("ubuf", pb, s_, g) for g in range(3)] + [("ubuf", pb, s_, "l"), ("ubuf", pb, s_, "r")]
                col = s_ * 44 + jj
                ts(cbuf[:, pb, s_, :], ubuf[:, pb, s_, 1:Q + 1], ccol(CS_CW + 1 * 88 + col), ccol(CS_CB + col),
                   ALU.mult, ALU.add, ukeys + ["cst"], [("cbuf", pb, s_)])
                stt(cbuf[:, pb, s_, :], ubuf[:, pb, s_, 0:Q], ccol(CS_CW + 0 * 88 + col), cbuf[:, pb, s_, :],
                    ALU.mult, ALU.add, ukeys + ["cst", ("cbuf", pb, s_)], [("cbuf", pb, s_)])
                stt(cbuf[:, pb, s_, :], ubuf[:, pb, s_, 2:Q + 2], ccol(CS_CW + 2 * 88 + col), cbuf[:, pb, s_, :],
                    ALU.mult, ALU.add, ukeys + ["cst", ("cbuf", pb, s_)], [("cbuf", pb, s_)])
            act(cbuf[:, pb, 0, :], cbuf[:, pb, 0, :], AF.Silu, [("cbuf", pb, 0)], [("cbuf", pb, 0)])
            tt(aT[:, cl, :], cbuf[:, pb, 0, :], cbuf[:, pb, 1, :], ALU.mult, [("cbuf", pb, 0), ("cbuf", pb, 1)],
               [("aT", cl)])
        proj_add(lambda jb, qd=qd: wload(w_down[qd * CPQ * 128:(qd + 1) * CPQ * 128, jb * 256:(jb + 1) * 256],
                                         CPQ, 256),
                 CPQ, aT, lambda kc, g: [("aT", kc)])
    checkpoint(9, [("xres", xres)])
    for n_ in ("h3", "aT", "ubuf", "cbuf"):
        A.release(n_)

    norm_T(lambda kc, c0, n: xres[:, kc, c0:c0 + n], lambda kc, gi: ("xres", kc, gi),
           lambda kc, c0, n: xres[:, kc, c0:c0 + n], lambda kc, gi: ("xres", kc, gi),
           KC, QG, CS_GFIN, D, "n8", [0, 1])
    for kq in range(4):
        dma("sync", outT[kq * 512:(kq + 1) * 512, :].rearrange("(k p) t -> p k t", p=128),
            xres[:, kq * 4:(kq + 1) * 4, :],
            [("xres", kc, g) for kc in range(kq * 4, kq * 4 + 4) for g in range(3)], [], ("out", kq),
            is_output=True)
    P.finish("sync")
    P.emit(nc)


_NC_CACHE = {}


def _host_inputs(inp):
    x = np.asarray(inp["x"], dtype=np.float32)
    mem = np.asarray(inp["mem"], dtype=np.float32)
    pos = np.asarray(inp["positions"], dtype=np.int32)

    def col(v, n):
        return np.asarray(v, np.float32).reshape(n, 128).T

    cst = np.zeros((128, NCST), np.float32)
    cst[:, CS_GMIX:CS_GMIX + 16] = col(inp["g_mix_norm"][0], 16)
    cst[:, CS_GCROSS:CS_GCROSS + 16] = col(inp["g_cross_norm"][0], 16)
    cst[:, CS_GMEM:CS_GMEM + 16] = col(inp["g_mem_norm"][0], 16)
    cst[:, CS_GFFN:CS_GFFN + 16] = col(inp["g_ffn_norm"][0], 16)
    cst[:, CS_GFIN:CS_GFIN + 16] = col(inp["g_final"], 16)
    cst[:, CS_GQ:CS_GQ + 4] = col(inp["g_q_norm"][0], 4)
    cst[:, CS_GKV:CS_GKV + 4] = col(inp["g_kv_norm"][0], 4)
    cst[:, CS_GDIFF] = np.asarray(inp["g_diff_sub"][0], np.float32)
    p = np.arange(128)
    inv = (10000.0 ** (-np.arange(0, 64, 2, dtype=np.float32) / np.float32(64))).astype(np.float32)
    cst[:, CS_INV] = inv[p % 32]
    cst[:, CS_SIGN] = np.where((p % 64) < 32, -1.0, 1.0)
    cw = np.asarray(inp["conv_w"][0], np.float32)
    for k in range(3):
        cst[:, CS_CW + k * 88:CS_CW + (k + 1) * 88] = col(cw[k], 88)
    cst[:, CS_CB:CS_CB + 88] = col(inp["conv_b"][0], 88)
    cst[:, CS_EPS1] = EPS
    cst[:, CS_EPS2] = EPS / ((1.0 - LAM_INIT) ** 2)
    lamv = np.concatenate([np.asarray(inp[k][0], np.float32) for k in
                           ("lambda_q1", "lambda_k1", "lambda_q2", "lambda_k2")])[None, :].repeat(128, 0)
    cmat = np.zeros((128, 256), np.float32)
    perm = np.where((p % 64) < 32, p + 32, p - 32)
    cmat[perm, p] = 1.0
    cmat[p, 128 + p] = 1.0
    shared = {
        "cst": cst, "lamv": np.ascontiguousarray(lamv), "cmat": cmat,
        "w_in": np.ascontiguousarray(inp["w_in"][0]), "w_uq": np.ascontiguousarray(inp["w_uq"][0]),
        "w_ukv": np.ascontiguousarray(inp["w_ukv"][0]), "w_o_mla": np.ascontiguousarray(inp["w_o_mla"][0]),
        "w_o_diff": np.ascontiguousarray(inp["w_o_diff"][0]), "w_out": np.ascontiguousarray(inp["w_out"][0]),
        "w_cross_q": np.ascontiguousarray(inp["w_cross_q"][0]),
        "w_cross_kv": np.ascontiguousarray(inp["w_cross_kv"][0]),
        "w_cross_o": np.ascontiguousarray(inp["w_cross_o"][0]), "w_up": np.ascontiguousarray(inp["w_up"][0]),
        "w_down": np.ascontiguousarray(inp["w_down"][0]),
    }
    shared = {k: np.asarray(v, np.float32) for k, v in shared.items()}
    in_maps = []
    for c in range(8):
        b, half = divmod(c, 2)
        q0 = half * 1023
        m = dict(shared)
        order = np.concatenate([np.arange(q0, q0 + Q), np.arange(0, q0), np.arange(q0 + Q, 2048)])
        m["xkvT"] = np.ascontiguousarray(x[b][order].T)
        m["xqT"] = np.ascontiguousarray(x[b, q0:q0 + Q].T)
        m["memT"] = np.ascontiguousarray(mem[b].T)
        m["poskv"] = np.ascontiguousarray(pos[b][order][None, :].repeat(128, 0))
        m["posq"] = np.ascontiguousarray(pos[b, q0:q0 + Q][None, :].repeat(128, 0))
        in_maps.append(m)
    return in_maps


def kernel(**inp):
    if "nc" not in _NC_CACHE:
        _NC_CACHE["nc"] = build_nc()
    nc = _NC_CACHE["nc"]
    in_maps = _host_inputs(inp)
    res = run_bass_kernel_spmd(nc, in_maps, core_ids=list(range(8)))
    out = np.empty((4, 2048, D), np.float32)
    for c in range(8):
        b, half = divmod(c, 2)
        o = res.results[c]["outT"]
        if half == 0:
            out[b, 0:1024, :] = o[:, 0:1024].T
        else:
            out[b, 1024:2048, :] = o[:, 1:1025].T
    return out
```
